# Optimizing a Trainium2 kernel written in Bass

```python
import math
import jax
import jax.numpy as jnp
from jax import lax
import numpy as np

D_MODEL = 1024
BATCH = 8
SEQ = 4096
DEPTH = 4

CTX_LEN = 256
GRID_W = 64
MIX_WIDTH = D_MODEL
ATT_QK = 64
ATT_V = 2 * ATT_QK
ATT_WIDTH = MIX_WIDTH // 2
ATT_HEADS = ATT_WIDTH // ATT_V
SSM_INNER = MIX_WIDTH // 4
SSM_HEAD_DIM = 64
SSM_HEADS = SSM_INNER // SSM_HEAD_DIM
SSM_GROUPS = 2
SSM_STATE = 128
SSM_XBC = SSM_INNER + 2 * SSM_GROUPS * SSM_STATE
CONV_W = 5
RET_WIDTH = MIX_WIDTH // 4
RET_DIM = 64
RET_HEADS = RET_WIDTH // RET_DIM
IN_SIZES = (ATT_HEADS * 2 * ATT_QK, ATT_HEADS * 2 * ATT_QK, ATT_WIDTH, SSM_INNER, SSM_XBC, SSM_HEADS, RET_WIDTH, RET_WIDTH, RET_WIDTH, RET_WIDTH)
IN_COLS = sum(IN_SIZES)
D_FF = 2816
CHUNK = 128
ATT_BLOCK = 128
ROPE_BASE = 10000.0
LN_EPS = 1e-5
RMS_EPS = 1e-6
DEEPNORM_ALPHA = (2 * DEPTH) ** 0.25
DEEPNORM_BETA = (8 * DEPTH) ** -0.25

kernel_name = 'hybrid_diffattn_ssd_retention_dit'


def layer_norm(h, g, b):
    hf = h.astype(jnp.float32)
    mu = jnp.mean(hf, axis=-1, keepdims=True)
    var = jnp.mean(jnp.square(hf - mu), axis=-1, keepdims=True)
    return ((hf - mu) * lax.rsqrt(var + LN_EPS)).astype(h.dtype) * g + b


def group_norm(h):
    hf = h.astype(jnp.float32)
    mu = jnp.mean(hf, axis=-1, keepdims=True)
    var = jnp.mean(jnp.square(hf - mu), axis=-1, keepdims=True)
    return ((hf - mu) * lax.rsqrt(var + LN_EPS)).astype(h.dtype)


def rms_norm(h, g):
    hf = h.astype(jnp.float32)
    return (hf * lax.rsqrt(jnp.mean(jnp.square(hf), axis=-1, keepdims=True) + RMS_EPS)).astype(h.dtype) * g


def rotate(h, ang):
    cos = jnp.cos(ang).astype(h.dtype)
    sin = jnp.sin(ang).astype(h.dtype)
    h1, h2 = jnp.split(h, 2, axis=-1)
    return jnp.concatenate([h1 * cos - h2 * sin, h2 * cos + h1 * sin], axis=-1)


def axial_rotary(h, ang_row, ang_col):
    hr, hc = jnp.split(h, 2, axis=-1)
    return jnp.concatenate([rotate(hr, ang_row), rotate(hc, ang_col)], axis=-1)


def swiglu(h, wg, wu, wd):
    return (jax.nn.silu(h @ wg) * (h @ wu)) @ wd


def sublayer_in(h, m, i):
    return h * (1.0 + m[..., i, 1, :]) + m[..., i, 0, :]


def residual(h, sub, m, i, g, b):
    return layer_norm(DEEPNORM_ALPHA * h + (1.0 + m[..., i, 2, :]) * sub, g, b)


def split_cols(p):
    idx = []
    acc = 0
    for s in IN_SIZES[:-1]:
        acc += s
        idx.append(acc)
    return jnp.split(p, idx, axis=-1)


def dwconv(u, w, bias):
    k = w.shape[0]
    out = lax.conv_general_dilated(u, w[:, None, :], window_strides=(1,), padding=[(k // 2, k // 2)],
                                   dimension_numbers=('NWC', 'WIO', 'NWC'), feature_group_count=u.shape[-1])
    return out + bias


def diff_attend(q, k, v, lam):
    s = jnp.einsum('bqhmd,bkhmd->bhmqk', q, k).astype(jnp.float32) * (ATT_QK ** -0.5)
    p = jax.nn.softmax(s, axis=-1)
    a = (p[:, :, 0] - lam * p[:, :, 1]).astype(v.dtype)
    return jnp.einsum('bhqk,bkhe->bqhe', a, v)


def chunked_scan(q, k, v, log_a, h0):
    b, L, h, n = q.shape
    p = v.shape[-1]
    nc = L // CHUNK
    qc = q.reshape(b, nc, CHUNK, h, n)
    kc = k.reshape(b, nc, CHUNK, h, n)
    vc = v.reshape(b, nc, CHUNK, h, p)
    a_cum = jnp.cumsum(log_a.astype(jnp.float32).reshape(b, nc, CHUNK, h), axis=2)
    seg = a_cum[:, :, :, None, :] - a_cum[:, :, None, :, :]
    lower = jnp.tril(jnp.ones((CHUNK, CHUNK), dtype=bool))[:, :, None]
    decay = jnp.exp(jnp.where(lower, seg, -jnp.inf)).astype(v.dtype)
    scores = jnp.einsum('bclhn,bcshn->bclsh', qc, kc) * decay
    y_diag = jnp.einsum('bclsh,bcshp->bclhp', scores, vc)
    to_end = jnp.exp(a_cum[:, :, -1:, :] - a_cum).astype(v.dtype)
    chunk_states = jnp.einsum('bcshn,bcshp->bchpn', kc * to_end[..., None], vc)
    chunk_decay = jnp.exp(a_cum[:, :, -1, :]).astype(v.dtype)

    def step(s, inp):
        st, dec = inp
        return s * dec[:, :, None, None] + st, s

    h_final, h_enter = lax.scan(step, h0, (jnp.moveaxis(chunk_states, 1, 0), jnp.moveaxis(chunk_decay, 1, 0)))
    h_enter = jnp.moveaxis(h_enter, 0, 1)
    from_start = jnp.exp(a_cum).astype(v.dtype)
    y_off = jnp.einsum('bclhn,bchpn->bclhp', qc * from_start[..., None], h_enter)
    return (y_diag + y_off).reshape(b, L, h, p), h_final


def final_state(k, v, log_a):
    a_cum = jnp.cumsum(log_a.astype(jnp.float32), axis=1)
    w = jnp.exp(a_cum[:, -1:, :] - a_cum).astype(v.dtype)
    return jnp.einsum('blhn,blhp->bhpn', k * w[..., None], v)


def directional_scan(q_l, k_l, v_l, la_l, q_c, k_c, v_c, la_c, reverse, need_ctx):
    if reverse:
        q_l, k_l, v_l, la_l, q_c, k_c, v_c, la_c = [jnp.flip(t, axis=1) for t in (q_l, k_l, v_l, la_l, q_c, k_c, v_c, la_c)]
    if need_ctx:
        b, _, h, n = k_c.shape
        h0 = jnp.zeros((b, h, v_c.shape[-1], n), v_c.dtype)
        y_c, h_ctx = chunked_scan(q_c, k_c, v_c, la_c, h0)
    else:
        y_c, h_ctx = None, final_state(k_c, v_c, la_c)
    y_l, _ = chunked_scan(q_l, k_l, v_l, la_l, h_ctx)
    if reverse:
        y_l = jnp.flip(y_l, axis=1)
        if need_ctx:
            y_c = jnp.flip(y_c, axis=1)
    return y_l, y_c


def ssm_streams(xbc, conv_w, conv_b):
    b, L, _ = xbc.shape
    u = jax.nn.silu(dwconv(xbc, conv_w, conv_b))
    xs, bm, cm = jnp.split(u, [SSM_INNER, SSM_INNER + SSM_GROUPS * SSM_STATE], axis=-1)
    rep = SSM_HEADS // SSM_GROUPS
    xs = xs.reshape(b, L, SSM_HEADS, SSM_HEAD_DIM)
    bm = jnp.repeat(bm.reshape(b, L, SSM_GROUPS, SSM_STATE), rep, axis=2)
    cm = jnp.repeat(cm.reshape(b, L, SSM_GROUPS, SSM_STATE), rep, axis=2)
    return xs, bm, cm


def ssm_step_inputs(xs, dt_raw, a_log, dt_bias):
    dt = jax.nn.softplus(dt_raw.astype(jnp.float32) + dt_bias.astype(jnp.float32))
    la = -dt * jnp.exp(a_log.astype(jnp.float32))
    return xs * dt[..., None].astype(xs.dtype), la


def hybrid_mixer(u_l, u_c, w_in, conv_w, conv_b, att_lambda, att_subln_g, lam_init, ssm_a_log, ssm_dt_bias, ssm_d,
                 ssm_norm_g, ret_log_gamma, w_out, ang_row, ang_col, ret_ang, need_ctx):
    b, S, _ = u_l.shape
    lc = u_c.shape[1]
    aq_l, ak_l, av_l, z_l, xbc_l, dt_l, rq_l, rk_l, rv_l, rg_l = split_cols(u_l @ w_in)
    aq_c, ak_c, av_c, z_c, xbc_c, dt_c, rq_c, rk_c, rv_c, rg_c = split_cols(u_c @ w_in)

    q_l = axial_rotary(aq_l.reshape(b, S, ATT_HEADS, 2, ATT_QK), ang_row, ang_col)
    k_l = axial_rotary(ak_l.reshape(b, S, ATT_HEADS, 2, ATT_QK), ang_row, ang_col)
    v_l = av_l.reshape(b, S, ATT_HEADS, ATT_V)
    k_c = ak_c.reshape(b, lc, ATT_HEADS, 2, ATT_QK)
    v_c = av_c.reshape(b, lc, ATT_HEADS, ATT_V)
    lv = att_lambda.astype(jnp.float32)
    lam = jnp.exp(jnp.sum(lv[0] * lv[1])) - jnp.exp(jnp.sum(lv[2] * lv[3])) + lam_init
    k_all = jnp.concatenate([k_l, k_c], axis=1)
    v_all = jnp.concatenate([v_l, v_c], axis=1)
    q_blocks = jnp.moveaxis(q_l.reshape(b, S // ATT_BLOCK, ATT_BLOCK, ATT_HEADS, 2, ATT_QK), 1, 0)
    o_l = lax.map(lambda qb: diff_attend(qb, k_all, v_all, lam), q_blocks)
    o_l = jnp.moveaxis(o_l, 0, 1).reshape(b, S, ATT_HEADS, ATT_V)

    xs_l, bm_l, cm_l = ssm_streams(xbc_l, conv_w, conv_b)
    xs_c, bm_c, cm_c = ssm_streams(xbc_c, conv_w, conv_b)
    ys_l, ys_c = [], []
    for d in range(2):
        vd_l, la_l = ssm_step_inputs(xs_l, dt_l, ssm_a_log[d], ssm_dt_bias[d])
        vd_c, la_c = ssm_step_inputs(xs_c, dt_c, ssm_a_log[d], ssm_dt_bias[d])
        yl, yc = directional_scan(cm_l, bm_l, vd_l, la_l, cm_c, bm_c, vd_c, la_c, d == 1, need_ctx)
        skip = ssm_d[d][:, None]
        ys_l.append(yl + skip * xs_l)
        if need_ctx:
            ys_c.append(yc + skip * xs_c)

    rq_l = rotate(rq_l.reshape(b, S, RET_HEADS, RET_DIM), ret_ang)
    rk_l = rotate(rk_l.reshape(b, S, RET_HEADS, RET_DIM), ret_ang) * (RET_DIM ** -0.5)
    rv_l = rv_l.reshape(b, S, RET_HEADS, RET_DIM)
    rq_c = rq_c.reshape(b, lc, RET_HEADS, RET_DIM)
    rk_c = rk_c.reshape(b, lc, RET_HEADS, RET_DIM) * (RET_DIM ** -0.5)
    rv_c = rv_c.reshape(b, lc, RET_HEADS, RET_DIM)
    yr_l, yr_c = [], []
    for d in range(2):
        la_l = jnp.broadcast_to(ret_log_gamma[d], (b, S, RET_HEADS))
        la_c = jnp.broadcast_to(ret_log_gamma[d], (b, lc, RET_HEADS))
        yl, yc = directional_scan(rq_l, rk_l, rv_l, la_l, rq_c, rk_c, rv_c, la_c, d == 1, need_ctx)
        yr_l.append(yl)
        yr_c.append(yc)

    def merge(o_att, y_ssm, z, y_ret, g, L):
        att = (rms_norm(o_att, att_subln_g) * (1.0 - lam_init)).reshape(b, L, ATT_WIDTH)
        ssm = rms_norm(y_ssm.reshape(b, L, SSM_INNER) * jax.nn.silu(z), ssm_norm_g)
        ret = group_norm(y_ret).reshape(b, L, RET_WIDTH) * jax.nn.silu(g)
        return jnp.concatenate([att, ssm, ret], axis=-1) @ w_out

    out_l = merge(o_l, ys_l[0] + ys_l[1], z_l, yr_l[0] + yr_l[1], rg_l, S)
    out_c = None
    if need_ctx:
        q_c = aq_c.reshape(b, lc, ATT_HEADS, 2, ATT_QK)
        o_c = diff_attend(q_c, k_c, v_c, lam)
        out_c = merge(o_c, ys_c[0] + ys_c[1], z_c, yr_c[0] + yr_c[1], rg_c, lc)
    return out_l, out_c


def setup_inputs(seed: int = 0) -> dict:
    key = jax.random.key(seed)
    ks = jax.random.split(key, 24)
    f32 = jnp.float32

    def nrm(k, shape, scale):
        return jax.random.normal(k, shape, f32) * scale

    x = nrm(ks[0], (BATCH, SEQ, D_MODEL), 1.0)
    c = nrm(ks[1], (BATCH, D_MODEL), 1.0)
    ctx = nrm(ks[2], (BATCH, CTX_LEN, D_MODEL), 1.0)
    c_ctx = nrm(ks[3], (D_MODEL,), 1.0)
    ada_w = nrm(ks[4], (DEPTH, D_MODEL, 9 * D_MODEL), 0.5 * D_MODEL ** -0.5)
    ada_b = nrm(ks[5], (DEPTH, 9 * D_MODEL), 0.02)
    norm_g = 1.0 + nrm(ks[6], (DEPTH, 3, D_MODEL), 0.05)
    norm_b = nrm(ks[7], (DEPTH, 3, D_MODEL), 0.02)
    ffn_w_gate = nrm(ks[8], (DEPTH, 2, D_MODEL, D_FF), D_MODEL ** -0.5)
    ffn_w_up = nrm(ks[9], (DEPTH, 2, D_MODEL, D_FF), D_MODEL ** -0.5)
    ffn_w_down = nrm(ks[10], (DEPTH, 2, D_FF, D_MODEL), DEEPNORM_BETA * D_FF ** -0.5)
    w_in = nrm(ks[11], (DEPTH, D_MODEL, IN_COLS), D_MODEL ** -0.5)
    conv_w = nrm(ks[12], (DEPTH, CONV_W, SSM_XBC), CONV_W ** -0.5)
    conv_b = nrm(ks[13], (DEPTH, SSM_XBC), 0.02)
    att_lambda = nrm(ks[14], (DEPTH, 4, ATT_QK), 0.1)
    att_subln_g = 1.0 + nrm(ks[15], (DEPTH, ATT_V), 0.05)
    ssm_a_log = jnp.log(jax.random.uniform(ks[16], (DEPTH, 2, SSM_HEADS), f32, 1.0, 16.0))
    dt0 = jnp.exp(jax.random.uniform(ks[17], (DEPTH, 2, SSM_HEADS), f32, math.log(1e-3), math.log(1e-1)))
    ssm_dt_bias = dt0 + jnp.log(-jnp.expm1(-dt0))
    ssm_d = 1.0 + nrm(ks[18], (DEPTH, 2, SSM_HEADS), 0.1)
    ssm_norm_g = 1.0 + nrm(ks[19], (DEPTH, SSM_INNER), 0.05)
    jitter = jax.random.uniform(ks[20], (DEPTH, 2, RET_HEADS), f32)
    ret_log_gamma = jnp.log1p(-jnp.exp2(-(5.0 + jnp.arange(RET_HEADS, dtype=f32) + 0.5 * jitter)))
    w_out = nrm(ks[21], (DEPTH, MIX_WIDTH, D_MODEL), DEEPNORM_BETA * MIX_WIDTH ** -0.5)
    return {'x': x, 'c': c, 'ctx': ctx, 'c_ctx': c_ctx, 'ada_w': ada_w, 'ada_b': ada_b, 'norm_g': norm_g,
            'norm_b': norm_b, 'ffn_w_gate': ffn_w_gate, 'ffn_w_up': ffn_w_up, 'ffn_w_down': ffn_w_down,
            'w_in': w_in, 'conv_w': conv_w, 'conv_b': conv_b, 'att_lambda': att_lambda,
            'att_subln_g': att_subln_g, 'ssm_a_log': ssm_a_log, 'ssm_dt_bias': ssm_dt_bias, 'ssm_d': ssm_d,
            'ssm_norm_g': ssm_norm_g, 'ret_log_gamma': ret_log_gamma, 'w_out': w_out}


def reference(x, c, ctx, c_ctx, ada_w, ada_b, norm_g, norm_b, ffn_w_gate, ffn_w_up, ffn_w_down, w_in, conv_w,
              conv_b, att_lambda, att_subln_g, ssm_a_log, ssm_dt_bias, ssm_d, ssm_norm_g, ret_log_gamma, w_out):
    b, S, D = x.shape
    rows = S // GRID_W
    f32 = jnp.float32
    row_pos = jnp.repeat(jnp.arange(rows, dtype=f32), GRID_W)
    col_pos = jnp.tile(jnp.arange(GRID_W, dtype=f32), rows)
    axis_dim = ATT_QK // 2
    axis_freq = 1.0 / (ROPE_BASE ** (jnp.arange(0, axis_dim, 2, dtype=f32) / axis_dim))
    ang_row = (row_pos[:, None] * axis_freq)[:, None, None, :]
    ang_col = (col_pos[:, None] * axis_freq)[:, None, None, :]
    ret_freq = 1.0 / (ROPE_BASE ** jnp.linspace(0.0, 1.0, RET_DIM // 2, dtype=f32))
    ret_ang = (jnp.arange(S, dtype=f32)[:, None] * ret_freq)[:, None, :]

    sc = jax.nn.silu(c)
    scc = jax.nn.silu(c_ctx)
    xc = ctx
    for l in range(DEPTH):
        need_ctx = l < DEPTH - 1
        lam_init = 0.8 - 0.6 * math.exp(-0.3 * l)
        m_l = (sc @ ada_w[l] + ada_b[l]).reshape(b, 1, 3, 3, D)
        m_c = (scc @ ada_w[l] + ada_b[l]).reshape(3, 3, D)
        x_new = residual(x, 0.5 * swiglu(sublayer_in(x, m_l, 0), ffn_w_gate[l, 0], ffn_w_up[l, 0], ffn_w_down[l, 0]),
                         m_l, 0, norm_g[l, 0], norm_b[l, 0])
        xc = residual(xc, 0.5 * swiglu(sublayer_in(xc, m_c, 0), ffn_w_gate[l, 0], ffn_w_up[l, 0], ffn_w_down[l, 0]),
                      m_c, 0, norm_g[l, 0], norm_b[l, 0])
        x = x_new
        y_l, y_c = hybrid_mixer(sublayer_in(x, m_l, 1), sublayer_in(xc, m_c, 1), w_in[l], conv_w[l], conv_b[l],
                                att_lambda[l], att_subln_g[l], lam_init, ssm_a_log[l], ssm_dt_bias[l], ssm_d[l],
                                ssm_norm_g[l], ret_log_gamma[l], w_out[l], ang_row, ang_col, ret_ang, need_ctx)
        x = residual(x, y_l, m_l, 1, norm_g[l, 1], norm_b[l, 1])
        x = residual(x, 0.5 * swiglu(sublayer_in(x, m_l, 2), ffn_w_gate[l, 1], ffn_w_up[l, 1], ffn_w_down[l, 1]),
                     m_l, 2, norm_g[l, 2], norm_b[l, 2])
        if need_ctx:
            xc = residual(xc, y_c, m_c, 1, norm_g[l, 1], norm_b[l, 1])
            xc = residual(xc, 0.5 * swiglu(sublayer_in(xc, m_c, 2), ffn_w_gate[l, 1], ffn_w_up[l, 1], ffn_w_down[l, 1]),
                          m_c, 2, norm_g[l, 2], norm_b[l, 2])
    return x
```

```python
import math
from contextlib import ExitStack

import numpy as np
import concourse.bass as bass
import concourse.mybir as mybir
from concourse.bass_utils import run_bass_kernel_spmd

F32 = mybir.dt.float32
BF16 = mybir.dt.bfloat16
AF = mybir.ActivationFunctionType
ALU = mybir.AluOpType

D = 1024
DFF = 2816
KC = D // 128
FC = DFF // 128
DEPTH = 4
LN_EPS = 1e-5
ALPHA = (2 * DEPTH) ** 0.25
INC = 3588


class Eng:
    def __init__(self, name, h, sem):
        self.name, self.h, self.sem, self.cnt = name, h, sem, 0
        self.waited = {}


class DSem:
    def __init__(self, sem):
        self.sem, self.cum = sem, 0


class Tl:
    def __init__(self, t, name):
        self.t, self.name = t, name
        self.w = {}
        self.r = {}
        self.ds = None

    def __getitem__(self, k):
        return self.t[k]


class Bld:
    def __init__(self, nc):
        self.nc = nc
        self.es = ExitStack()
        self.engs = {}
        for name, h in [("pe", nc.tensor), ("act", nc.scalar), ("dve", nc.vector), ("pool", nc.gpsimd),
                        ("sp", nc.sync)]:
            sem = self.es.enter_context(nc.semaphore("sem_" + name))
            self.engs[name] = Eng(name, h, DSem(sem))
        self.dsems = []
        self.free_dsems = []
        self.n_instr = 0

    def _uniq(self, name):
        self.n_names = getattr(self, "n_names", 0) + 1
        return "%s_%d" % (name, self.n_names)

    def sb(self, stack, name, shape, dt):
        t = stack.enter_context(self.nc.sbuf_tensor(self._uniq(name), list(shape), dt))
        return Tl(t, name)

    def ps(self, stack, name, shape, dt=F32):
        esz = 4 if dt == F32 else 2
        per_bank = 2048 // esz
        nfree = 1
        for d_ in shape[1:]:
            nfree *= d_
        nb = (nfree + per_bank - 1) // per_bank
        raw = stack.enter_context(self.nc.psum_tensor(self._uniq(name), [128, nb * per_bank], dt))
        v = raw[0:shape[0], 0:nfree]
        if len(shape) == 3:
            v = v.rearrange("p (a b) -> p a b", b=shape[2])
        elif len(shape) != 2:
            raise ValueError("ps: 2-D or 3-D shapes only")
        tl = Tl(v, name)
        tl.psum = True
        return tl

    def dram(self, name, shape, dt, kind="Internal"):
        t = self.nc.dram_tensor(name, list(shape), dt, kind=kind).ap()
        return Tl(t, name)

    def _dsem(self, tl):
        if tl.ds is None:
            if self.free_dsems:
                tl.ds = self.free_dsems.pop()
            else:
                sem = self.es.enter_context(self.nc.semaphore("dsem%d" % len(self.dsems)))
                tl.ds = DSem(sem)
                self.dsems.append(tl.ds)
        return tl.ds

    def _wait(self, E, rec):
        so, val, src = rec
        if src == "pe" and E.name == "pe":
            return
        if src == "dma":
            val = so.cum
        if E.waited.get(id(so), 0) >= val:
            return
        E.h.wait_ge(so.sem, val)
        E.waited[id(so)] = val
        self.n_instr += 1

    def _deps(self, E, r, w):
        for t in r:
            for rec in t.w.values():
                self._wait(E, rec)
            if getattr(t, "psum", False):
                for k, rec in t.r.items():
                    if k != E.name:
                        self._wait(E, rec)
        for t in w:
            for rec in t.w.values():
                self._wait(E, rec)
            for rec in t.r.values():
                self._wait(E, rec)

    def op(self, eng, fn, r=(), w=()):
        E = self.engs[eng]
        self._deps(E, r, w)
        ins = fn(E.h)
        E.cnt += 1
        E.sem.cum = E.cnt
        ins.then_inc(E.sem.sem, 1)
        rec = (E.sem, E.cnt, eng)
        for t in r:
            t.r[eng] = rec
        for t in w:
            t.w = {eng: rec}
            t.r = {}
        self.n_instr += 1
        return ins

    def dma(self, eng, out_tl, out_ap, in_tl, in_ap, sem_tl=None, **kw):
        E = self.engs[eng]
        tr_in = not (_is_dram(in_tl) and not getattr(in_tl, "tracked", False))
        tr_out = not (_is_dram(out_tl) and not getattr(out_tl, "tracked", False))
        self._deps(E, [in_tl] if tr_in else [], [out_tl] if tr_out else [])
        if sem_tl is None:
            sem_tl = out_tl if not _is_dram(out_tl) else in_tl
        ds = self._dsem(sem_tl)
        ins = E.h.dma_start(out=out_ap, in_=in_ap, **kw)
        ds.cum += 16
        ins.then_inc(ds.sem, 16)
        rec = (ds, ds.cum, "dma")
        if tr_in:
            in_tl.r["dma%d" % id(ds)] = rec
        if tr_out:
            out_tl.w = {"dma%d" % id(ds): rec}
            out_tl.r = {}
        self.n_instr += 1
        return ins

    def barrier(self):
        recs = [(E.sem, E.cnt, E.name) for E in self.engs.values() if E.cnt > 0]
        drecs = [(ds, ds.cum, "dma") for ds in self.dsems if ds.cum > 0]
        for E in self.engs.values():
            for rec in recs:
                if rec[2] != E.name:
                    so, val, src = rec
                    if E.waited.get(id(so), 0) < val:
                        E.h.wait_ge(so.sem, val)
                        E.waited[id(so)] = val
                        self.n_instr += 1
            for rec in drecs:
                self._wait(E, rec)

    def release(self, tls):
        for t in tls:
            if t.ds is not None:
                self.free_dsems.append(t.ds)
                t.ds = None


def _is_dram(tl):
    return getattr(tl, "is_dram", False)


def dram_tl(b, name, shape, dt, kind="Internal"):
    tl = b.dram(name, shape, dt, kind)
    tl.is_dram = True
    return tl


class Prog:
    def __init__(self, S=4096, SC=256, depth=DEPTH, dbg=None, stop_after=None):
        self.S, self.SC, self.depth = S, SC, depth
        self.T = S + SC
        self.dbg = dbg or {}
        self.stop_after = stop_after
        nc = bass.Bass("TRN2", target_bir_lowering=False)
        self.nc = nc
        self.b = Bld(nc)
        self.blocks = [(i * 512, 512, 0) for i in range(S // 512)] + [(S, SC, 1)]

    def declare_io(self):
        b, L = self.b, self.depth
        ein = lambda name, shape: dram_tl(b, name, shape, F32, "ExternalInput")
        self.x_in = ein("x", [self.S, D])
        self.ctx_in = ein("ctx", [self.SC, D])
        self.c2_in = ein("c2", [D, 2])
        self.ident_in = ein("ident", [128, 128])
        self.ada_w = ein("ada_w", [L, D, 9 * D])
        self.ada_b = ein("ada_b", [L, 9 * D])
        self.norm_g = ein("norm_g", [L, 3, D])
        self.norm_b = ein("norm_b", [L, 3, D])
        self.w_gate = ein("ffn_w_gate", [L, 2, D, DFF])
        self.w_up = ein("ffn_w_up", [L, 2, D, DFF])
        self.w_down = ein("ffn_w_down", [L, 2, DFF, D])
        self.out_rows = self.T if self.dbg.get("full_out") else self.S
        self.out = dram_tl(b, "out", [self.out_rows, D], F32, "ExternalOutput")
        self.X = dram_tl(b, "X_scr", [self.T, D], F32)
        self.DP = dram_tl(b, "DP_scr", [self.T, D], F32)
        self.Xb = [self._view(self.X) for _ in self.blocks]
        self.DPb = [self._view(self.DP) for _ in self.blocks]
        self.m_dram = dram_tl(b, "m_scr", [L, 2, 9 * D], F32)

    def prologue(self, st):
        b, L = self.b, self.depth
        self.ident = b.sb(st, "ident", [128, 128], F32)
        b.dma("sp", self.ident, self.ident[:], self.ident_in, self.ident_in.t)
        self.mcol = b.sb(st, "mcol", [128, L, 72, 2], F32)
        with ExitStack() as ps:
            c2 = b.sb(ps, "c2", [128, KC, 2], F32)
            sc2 = b.sb(ps, "sc2", [128, KC, 2], F32)
            b.dma("sp", c2, c2[:], self.c2_in, self.c2_in.t.rearrange("(k p) v -> p k v", p=128))
            b.op("act", lambda e: e.activation(out=sc2[:], in_=c2[:], func=AF.Silu), r=[c2], w=[sc2])
            aw = [b.sb(ps, "aw%d" % i, [128, KC, 1024], F32) for i in range(2)]
            adab = b.sb(ps, "adab", [2, 9 * D], F32)
            mrow = b.sb(ps, "mrow", [2, 9 * D], F32)
            pm = [b.ps(ps, "pm%d" % i, [2, 1024]) for i in range(2)]
            pcol = b.ps(ps, "pcol", [128, 72, 2])
            it = 0
            for l in range(L):
                for v in range(2):
                    b.dma("sp", adab, adab[v:v + 1, :], self.ada_b, self.ada_b.t[l:l + 1, :])
                for cg in range(9):
                    a = aw[it % 2]
                    p = pm[it % 2]
                    it += 1
                    b.dma("sp", a, a[:], self.ada_w,
                          self.ada_w.t[l, :, cg * 1024:(cg + 1) * 1024].rearrange("(k p) n -> p k n", p=128))
                    for h in range(2):
                        for kc in range(KC):
                            b.op("pe", lambda e, kc=kc, h=h: e.matmul(
                                p[0:2, h * 512:(h + 1) * 512], lhsT=sc2[:, kc, :],
                                rhs=a[:, kc, h * 512:(h + 1) * 512], start=(kc == 0), stop=(kc == KC - 1)),
                                r=[sc2, a], w=[p])
                    b.op("dve", lambda e: e.tensor_tensor(
                        out=mrow[0:2, cg * 1024:(cg + 1) * 1024], in0=p[0:2, :],
                        in1=adab[0:2, cg * 1024:(cg + 1) * 1024], op=ALU.add), r=[p, adab], w=[mrow])
                for j in range(72):
                    b.op("pe", lambda e, j=j: e.transpose(pcol[:, j, :], mrow[0:2, j * 128:(j + 1) * 128],
                                                        self.ident[0:2, 0:2]),
                         r=[mrow, self.ident], w=[pcol])
                b.op("dve", lambda e: e.tensor_copy(out=self.mcol[:, l, :, :], in_=pcol[:]),
                     r=[pcol], w=[self.mcol])
                b.dma("sp", self.m_dram, self.m_dram.t[l], mrow, mrow[0:2, :])
                for i in range(3):
                    j0 = (i * 3 + 1) * 8
                    b.op("dve", lambda e, j0=j0: e.tensor_scalar_add(
                        self.mcol[:, l, j0:j0 + 8, :], self.mcol[:, l, j0:j0 + 8, :], 1.0),
                        r=[self.mcol], w=[self.mcol])
            b.barrier()
            b.release(aw + [adab, mrow, c2])

    def load_bcast(self, st, l, i, gate_mul):
        b = self.b
        gate = [b.sb(st, "gate_bc%d" % v, [128, D], F32) for v in range(2)]
        g_bc = b.sb(st, "g_bc", [128, D], F32)
        b_bc = b.sb(st, "b_bc", [128, D], F32)
        off = (i * 3 + 2) * D
        for v in range(2):
            b.dma("sp", gate[v], gate[v][:], self.m_dram,
                  self.m_dram.t[l, v, off:off + D].partition_broadcast(128))
            b.op("dve", lambda e, v=v: e.tensor_scalar(gate[v][:], gate[v][:], 1.0, gate_mul, op0=ALU.add,
                                                      op1=ALU.mult), r=[gate[v]], w=[gate[v]])
        b.dma("sp", g_bc, g_bc[:], self.norm_g, self.norm_g.t[l, i, :].partition_broadcast(128))
        b.dma("sp", b_bc, b_bc[:], self.norm_b, self.norm_b.t[l, i, :].partition_broadcast(128))
        return gate, g_bc, b_bc

    def _view(self, tl):
        v = Tl(tl.t, tl.name)
        v.is_dram = True
        v.tracked = True
        return v

    def src_rows(self, first, bi, t0, n):
        if first:
            if t0 < self.S:
                return self.x_in, self.x_in.t[t0:t0 + n, :]
            return self.ctx_in, self.ctx_in.t[t0 - self.S:t0 - self.S + n, :]
        return self.Xb[bi], self.X.t[t0:t0 + n, :]

    def transpose_mod(self, xin, nsub, pT, uT, l, i, v):
        b = self.b
        ntok = nsub * 128
        for kc in range(KC):
            p = pT[kc % len(pT)]
            for s in range(nsub):
                b.op("pe", lambda e, kc=kc, s=s: e.transpose(
                    p[:, s * 128:(s + 1) * 128], xin[:, s, kc * 128:(kc + 1) * 128], self.ident[:]),
                    r=[xin, self.ident], w=[p])
            jsc = (i * 3 + 1) * 8 + kc
            jsh = (i * 3 + 0) * 8 + kc
            b.op("act", lambda e, kc=kc, jsc=jsc, jsh=jsh: e.activation(
                out=uT[:, kc, 0:ntok], in_=p[:, 0:ntok], func=AF.Identity,
                scale=self.mcol[:, l, jsc, v:v + 1], bias=self.mcol[:, l, jsh, v:v + 1]),
                r=[p, self.mcol], w=[uT])

    def layer_norm_store(self, y, xo, g_bc, b_bc, stt, mv, rstd, nmr):
        b = self.b
        for h in range(2):
            b.op("dve", lambda e, h=h: e.bn_stats(out=stt[:, h * 6:(h + 1) * 6], in_=y[:, h * 512:(h + 1) * 512]),
                 r=[y], w=[stt])
        b.op("dve", lambda e: e.bn_aggr(out=mv[:], in_=stt[:]), r=[stt], w=[mv])
        b.op("act", lambda e: e.activation(out=rstd[:], in_=mv[:, 1:2], func=AF.Sqrt, bias=self.eps_t[:], scale=1.0),
             r=[mv, self.eps_t], w=[rstd])
        b.op("dve", lambda e: e.reciprocal(out=rstd[:], in_=rstd[:]), r=[rstd], w=[rstd])
        b.op("dve", lambda e: e.tensor_scalar(nmr[:], mv[:, 0:1], -1.0, rstd[:], op0=ALU.mult, op1=ALU.mult),
             r=[mv, rstd], w=[nmr])
        b.op("act", lambda e: e.activation(out=y[:], in_=y[:], func=AF.Identity, scale=rstd[:], bias=nmr[:]),
             r=[y, rstd, nmr], w=[y])
        b.op("pool", lambda e: e.tensor_tensor(out=y[:], in0=y[:], in1=g_bc[:], op=ALU.mult), r=[y, g_bc], w=[y])
        b.op("pool", lambda e: e.tensor_tensor(out=xo, in0=y[:], in1=b_bc[:], op=ALU.add), r=[y, b_bc], w=[self._xo_tl])

    def ffn(self, l, f, first=False, skip_ctx=False, to_out=False):
        b = self.b
        i = 0 if f == 0 else 2
        HF = FC // 2
        HW = HF * 128
        with ExitStack() as st:
            gate, g_bc, b_bc = self.load_bcast(st, l, i, 0.5)
            wg = b.sb(st, "wg", [128, KC, HW], BF16)
            wu = b.sb(st, "wu", [128, KC, HW], BF16)
            wd = b.sb(st, "wd", [128, HF, D], BF16)
            stg = [b.sb(st, "stg%d" % k, [128, HW], F32) for k in range(3)]
            uT2 = [b.sb(st, "uT%d" % k, [128, KC, 512], BF16) for k in range(2)]
            hT = b.sb(st, "hT", [128, HF, 512], BF16)
            xin = [b.sb(st, "xin%d" % k, [128, 4, D], F32) for k in range(2)]
            dpt = [b.sb(st, "dpt%d" % k, [128, D], F32) for k in range(2)]
            ybuf = [b.sb(st, "ybuf%d" % k, [128, D], F32) for k in range(2)]
            sg = [b.sb(st, "sg%d" % k, [128, 512], F32) for k in range(2)]
            stt = b.sb(st, "stt", [128, 12], F32)
            mv = b.sb(st, "mv", [128, 2], F32)
            rstd = b.sb(st, "rstd", [128, 1], F32)
            nmr = b.sb(st, "nmr", [128, 1], F32)
            pT = [b.ps(st, "pT%d" % k, [128, 512]) for k in range(2)]
            pg = [b.ps(st, "pg%d" % k, [128, 512]) for k in range(2)]
            pu = [b.ps(st, "pu%d" % k, [128, 512]) for k in range(2)]
            pdh = [b.ps(st, "pd%d" % k, [128, 512]) for k in range(2)]
            cast_engs = ["act", "pool", "dve"]
            ci = 0
            for ps_ in range(2):
                c0 = ps_ * HW
                for (wt, src) in ((wg, self.w_gate), (wu, self.w_up)):
                    for kc in range(KC):
                        s_ = stg[ci % 3]
                        b.dma("sp", s_, s_[:], src, src.t[l, f, kc * 128:(kc + 1) * 128, c0:c0 + HW])
                        eng = cast_engs[ci % 3]
                        ci += 1
                        if eng == "act":
                            b.op("act", lambda e, wt=wt, kc=kc, s_=s_: e.copy(out=wt[:, kc, :], in_=s_[:]), r=[s_], w=[wt])
                        else:
                            b.op(eng, lambda e, wt=wt, kc=kc, s_=s_: e.tensor_copy(out=wt[:, kc, :], in_=s_[:]), r=[s_], w=[wt])
                for fc in range(HF):
                    s_ = stg[ci % 3]
                    r0 = (ps_ * HF + fc) * 128
                    b.dma("sp", s_, s_[:, 0:D], self.w_down, self.w_down.t[l, f, r0:r0 + 128, :])
                    eng = cast_engs[ci % 3]
                    ci += 1
                    if eng == "act":
                        b.op("act", lambda e, fc=fc, s_=s_: e.copy(out=wd[:, fc, :], in_=s_[:, 0:D]), r=[s_], w=[wd])
                    else:
                        b.op(eng, lambda e, fc=fc, s_=s_: e.tensor_copy(out=wd[:, fc, :], in_=s_[:, 0:D]), r=[s_], w=[wd])
                blks = [(bi, t0, ntok, v) for bi, (t0, ntok, v) in enumerate(self.blocks) if not (v == 1 and skip_ctx)]

                def load_blk(k):
                    bi_, t0_, ntok_, v_ = blks[k]
                    stl, sap = self.src_rows(first, bi_, t0_, ntok_)
                    b.dma("sp", xin[bi_ % 2], xin[bi_ % 2][:, 0:ntok_ // 128, :], stl, sap.rearrange("(s p) d -> p s d", p=128))

                def T_(k):
                    bi, t0, ntok, v = blks[k]
                    self.transpose_mod(xin[bi % 2], ntok // 128, pT, uT2[k % 2], l, i, v)

                def GU_(k):
                    bi, t0, ntok, v = blks[k]
                    uT = uT2[k % 2]
                    for fc in range(HF):
                        g_, u_, s_ = pg[fc % 2], pu[fc % 2], sg[fc % 2]
                        for kc in range(KC):
                            b.op("pe", lambda e, kc=kc, fc=fc: e.matmul(
                                g_[:, 0:ntok], lhsT=wg[:, kc, fc * 128:(fc + 1) * 128], rhs=uT[:, kc, 0:ntok],
                                start=(kc == 0), stop=(kc == KC - 1)), r=[wg, uT], w=[g_])
                        for kc in range(KC):
                            b.op("pe", lambda e, kc=kc, fc=fc: e.matmul(
                                u_[:, 0:ntok], lhsT=wu[:, kc, fc * 128:(fc + 1) * 128], rhs=uT[:, kc, 0:ntok],
                                start=(kc == 0), stop=(kc == KC - 1)), r=[wu, uT], w=[u_])
                        b.op("act", lambda e: e.activation(out=s_[:, 0:ntok], in_=g_[:, 0:ntok], func=AF.Silu),
                             r=[g_], w=[s_])
                        b.op("dve", lambda e, fc=fc: e.tensor_tensor(out=hT[:, fc, 0:ntok], in0=u_[:, 0:ntok],
                                                                     in1=s_[:, 0:ntok], op=ALU.mult),
                             r=[u_, s_], w=[hT])

                def DN_(k):
                    bi, t0, ntok, v = blks[k]
                    nsub = ntok // 128
                    xi = xin[bi % 2]
                    for s in range(nsub):
                        dp = dpt[s % 2]
                        y = ybuf[s % 2]
                        r0 = t0 + s * 128
                        if ps_ == 1:
                            b.dma("sp", dp, dp[:], self.DPb[bi], self.DP.t[r0:r0 + 128, :])
                        for h in range(2):
                            pd_ = pdh[h]
                            hs = slice(h * 512, (h + 1) * 512)
                            for fc in range(HF):
                                b.op("pe", lambda e, fc=fc, h=h, s=s: e.matmul(
                                    pd_[:], lhsT=hT[:, fc, s * 128:(s + 1) * 128],
                                    rhs=wd[:, fc, h * 512:(h + 1) * 512], start=(fc == 0), stop=(fc == HF - 1)),
                                    r=[hT, wd], w=[pd_])
                            if ps_ == 0:
                                b.op("act", lambda e: e.copy(out=dp[:, hs], in_=pd_[:]), r=[pd_], w=[dp])
                            else:
                                b.op("dve", lambda e: e.tensor_tensor(out=y[:, hs], in0=pd_[:], in1=dp[:, hs], op=ALU.add),
                                     r=[pd_, dp], w=[y])
                        if ps_ == 0:
                            b.dma("sp", self.DPb[bi], self.DP.t[r0:r0 + 128, :], dp, dp[:])
                        else:
                            b.op("pool", lambda e: e.tensor_tensor(out=y[:], in0=y[:], in1=gate[v][:], op=ALU.mult),
                                 r=[y, gate[v]], w=[y])
                            b.op("dve", lambda e, s=s: e.scalar_tensor_tensor(
                                out=y[:], in0=xi[:, s, :], scalar=ALPHA, in1=y[:], op0=ALU.mult, op1=ALU.add),
                                r=[xi, y], w=[y])
                            self._xo_tl = xi
                            self.layer_norm_store(y, xi[:, s, :], g_bc, b_bc, stt, mv, rstd, nmr)
                    if ps_ == 1:
                        if to_out and t0 < self.S:
                            b.dma("sp", self.out, self.out.t[t0:t0 + ntok, :].rearrange("(s p) d -> p s d", p=128),
                                  xi, xi[:, 0:nsub, :])
                        else:
                            b.dma("sp", self.Xb[bi], self.X.t[t0:t0 + ntok, :].rearrange("(s p) d -> p s d", p=128),
                                  xi, xi[:, 0:nsub, :])

                nb_ = len(blks)
                load_blk(0)
                if nb_ > 1:
                    load_blk(1)
                T_(0)
                GU_(0)
                for k_ in range(nb_):
                    if k_ + 1 < nb_:
                        T_(k_ + 1)
                    DN_(k_)
                    if k_ + 2 < nb_:
                        load_blk(k_ + 2)
                    if k_ + 1 < nb_:
                        GU_(k_ + 1)
            b.barrier()
            b.release([gate[0], gate[1], g_bc, b_bc, wg, wu, wd, hT] + uT2 + stg + xin + dpt + ybuf + sg)

    def declare_mixer_io(self):
        b, L, S, SC, T = self.b, self.depth, self.S, self.SC, self.T
        ein = lambda name, shape: dram_tl(b, name, shape, F32, "ExternalInput")
        self.w_in = ein("w_in", [L, D, INC])
        self.w_out = ein("w_out", [L, D, D])
        self.conv_w = ein("conv_w", [L, 5, 768])
        self.conv_b = ein("conv_b", [L, 768])
        self.att_lambda = ein("att_lambda", [L, 4, 64])
        self.att_subln_g = ein("att_subln_g", [L, 128])
        self.ssm_a_log = ein("ssm_a_log", [L, 8])
        self.ssm_dt_bias = ein("ssm_dt_bias", [L, 8])
        self.ssm_d = ein("ssm_d", [L, 8])
        self.ssm_norm_g = ein("ssm_norm_g", [L, 256])
        self.ret_log_gamma = ein("ret_log_gamma", [L, 8])
        nb = S // 512
        self.rot_tab = ein("rot_tab", [nb, 128, 6, 512])
        self.cmats = ein("cmats", [128, 8, 128])
        sc = lambda name, shape, dt: dram_tl(b, name, shape, dt)
        self.QT = sc("QT_scr", [4, 128, T], BF16)
        self.KT = sc("KT_scr", [4, 128, T], BF16)
        self.V1 = sc("V1_scr", [T, 512], BF16)
        self.Z = sc("Z_scr", [T, 256], F32)
        self.RG = sc("RG_scr", [T, 256], F32)
        self.DT = sc("DT_scr", [T, 4], F32)
        self.RV = sc("RV_scr", [T, 256], BF16)
        self.XBCT = sc("XBCT_scr", [6, 128, T], F32)
        self.XST = sc("XST_scr", [2, 128, T], F32)
        self.BT = sc("BT_scr", [2, 128, T], BF16)
        self.CT = sc("CT_scr", [2, 128, T], BF16)
        self.RQT = sc("RQT_scr", [4, 64, T], BF16)
        self.RKT = sc("RKT_scr", [4, 64, T], BF16)
        self.RKK = sc("RKK_scr", [T, 256], BF16)
        self.YS = sc("YS_scr", [T, 256], F32)
        self.YR = sc("YR_scr", [T, 256], F32)
        self.MGT = sc("MGT_scr", [D, T], BF16)

    def load_consts(self, st):
        b = self.b
        self.cm = b.sb(st, "cmats", [128, 8, 128], F32)
        b.dma("sp", self.cm, self.cm[:], self.cmats, self.cmats.t)
        self.identb = b.sb(st, "identb", [128, 128], BF16)
        b.op("dve", lambda e: e.tensor_copy(out=self.identb[:], in_=self.ident[:]), r=[self.ident], w=[self.identb])
        self.ones_bf = b.sb(st, "ones_bf", [128, 1], BF16)
        b.op("pool", lambda e: e.memset(self.ones_bf[:], 1.0), w=[self.ones_bf])
        self.one_t = b.sb(st, "one_t", [128, 1], F32)
        b.op("pool", lambda e: e.memset(self.one_t[:], 1.0), w=[self.one_t])
        self.eps6 = b.sb(st, "eps6", [128, 1], F32)
        b.op("pool", lambda e: e.memset(self.eps6[:], 1e-6), w=[self.eps6])
        self.mask4 = []
        for d in range(2):
            m4 = b.sb(st, "mask4_%d" % d, [128, 4, 128], F32)
            for h in range(4):
                b.op("pool", lambda e, h=h: e.tensor_copy(out=m4[:, h, :], in_=self.cm[:, 5 + d, :]), r=[self.cm], w=[m4])
            self.mask4.append(m4)

    PERM_A, PERM_R, TRI_F, NSTRICT_B, ONES = 0, 1, 2, 3, 4

    def cast_to(self, eng, out_ap, in_ap, r, w):
        b = self.b
        if eng == "act":
            b.op("act", lambda e: e.copy(out=out_ap, in_=in_ap), r=r, w=w)
        else:
            b.op(eng, lambda e: e.tensor_copy(out=out_ap, in_=in_ap), r=r, w=w)

    def mixer_inproj(self, l):
        b, S, T = self.b, self.S, self.T
        with ExitStack() as st:
            win = b.sb(st, "win", [128, KC, INC], BF16)
            wstg = [b.sb(st, "wstg%d" % k, [128, INC], F32) for k in range(2)]
            engs = ["act", "pool", "dve"]
            for kc in range(KC):
                s_ = wstg[kc % 2]
                b.dma("sp", s_, s_[:], self.w_in, self.w_in.t[l, kc * 128:(kc + 1) * 128, :])
                self.cast_to(engs[kc % 3], win[:, kc, :], s_[:], [s_], [win])
            uT = b.sb(st, "uT", [128, KC, 512], BF16)
            xin = [b.sb(st, "xin%d" % k, [128, 4, D], F32) for k in range(2)]
            tab = [b.sb(st, "tab%d" % k, [128, 6, 512], F32) for k in range(2)]
            qs = [b.sb(st, "qs%d" % k, [128, 512], F32) for k in range(2)]
            t1 = b.sb(st, "t1", [128, 512], F32)
            t2 = b.sb(st, "t2", [128, 512], F32)
            ob = [b.sb(st, "ob%d" % k, [128, 512], BF16) for k in range(3)]
            xb = [b.sb(st, "xb%d" % k, [128, 512], F32) for k in range(2)]
            rkk = b.sb(st, "rkk", [128, 4, 256], BF16)
            vst = b.sb(st, "vst", [128, 4, 512], BF16)
            zst = b.sb(st, "zst", [128, 4, 256], F32)
            rgst = b.sb(st, "rgst", [128, 4, 256], F32)
            rvst = b.sb(st, "rvst", [128, 4, 256], BF16)
            dst = b.sb(st, "dst", [128, 4, 4], F32)
            pT = [b.ps(st, "pT", [128, 512])]
            pf = [b.ps(st, "pf%d" % k, [128, 512]) for k in range(2)]
            pr = b.ps(st, "pr", [128, 512])
            ptr = b.ps(st, "ptr", [128, 512], BF16)
            pav = b.ps(st, "pav", [128, 512])
            pz = b.ps(st, "pz", [128, 512])
            prr = b.ps(st, "prr", [128, 512])
            nf = 0
            oi = 0
            def load_blk(bi_):
                t0_, ntok_, v_ = self.blocks[bi_]
                b.dma("sp", xin[bi_ % 2], xin[bi_ % 2][:, 0:ntok_ // 128, :], self.Xb[bi_],
                      self.X.t[t0_:t0_ + ntok_, :].rearrange("(s p) d -> p s d", p=128))
                if v_ == 0:
                    b.dma("sp", tab[bi_ % 2], tab[bi_ % 2][:], self.rot_tab, self.rot_tab.t[bi_])

            load_blk(0)
            for bi, (t0, ntok, v) in enumerate(self.blocks):
                if bi + 1 < len(self.blocks):
                    load_blk(bi + 1)
                nsub = ntok // 128
                xi = xin[bi % 2]
                tb = tab[bi % 2]
                self.transpose_mod(xi, nsub, pT, uT, l, 1, v)
                chunks = []
                for c in range(4):
                    chunks.append(("aq", c, c * 128))
                for c in range(4):
                    chunks.append(("ak", c, 512 + c * 128))
                for c in range(2):
                    chunks.append(("rq", c, 2564 + c * 128))
                for c in range(2):
                    chunks.append(("rk", c, 2820 + c * 128))
                for c in range(6):
                    chunks.append(("xbc", c, 1792 + c * 128))
                for (kind, c, col0) in chunks:
                    p_ = pf[nf % 2]
                    nf += 1
                    for kc in range(KC):
                        b.op("pe", lambda e, kc=kc: e.matmul(p_[:, 0:ntok], lhsT=win[:, kc, col0:col0 + 128],
                                                             rhs=uT[:, kc, 0:ntok], start=(kc == 0), stop=(kc == KC - 1)),
                             r=[win, uT], w=[p_])
                    if kind == "xbc":
                        x_ = xb[c % 2]
                        b.op("act", lambda e: e.copy(out=x_[:, 0:ntok], in_=p_[:, 0:ntok]), r=[p_], w=[x_])
                        b.dma("sp", self.XBCT, self.XBCT.t[c, :, t0:t0 + ntok], x_, x_[:, 0:ntok])
                        continue
                    o_ = ob[oi % 3]
                    oi += 1
                    if v == 1:
                        if kind == "rk":
                            b.op("act", lambda e: e.mul(o_[:, 0:ntok], p_[:, 0:ntok], 0.125), r=[p_], w=[o_])
                        else:
                            b.op("act", lambda e: e.copy(out=o_[:, 0:ntok], in_=p_[:, 0:ntok]), r=[p_], w=[o_])
                    else:
                        q_ = qs[oi % 2]
                        ti = {"aq": 0, "ak": 0, "rq": 2, "rk": 4}[kind]
                        pm = self.PERM_A if kind in ("aq", "ak") else self.PERM_R
                        b.op("act", lambda e: e.copy(out=q_[:], in_=p_[:]), r=[p_], w=[q_])
                        b.op("pe", lambda e: e.matmul(pr[:], lhsT=self.cm[:, pm, :], rhs=q_[:], start=True, stop=True),
                             r=[self.cm, q_], w=[pr])
                        b.op("pool", lambda e: e.tensor_tensor(out=t1[:], in0=q_[:], in1=tb[:, ti, :], op=ALU.mult),
                             r=[q_, tb], w=[t1])
                        b.op("dve", lambda e: e.tensor_tensor(out=t2[:], in0=pr[:], in1=tb[:, ti + 1, :], op=ALU.mult),
                             r=[pr, tb], w=[t2])
                        b.op("dve", lambda e: e.tensor_tensor(out=o_[:], in0=t1[:], in1=t2[:], op=ALU.add),
                             r=[t1, t2], w=[o_])
                    if kind == "aq":
                        b.dma("sp", self.QT, self.QT.t[c, :, t0:t0 + ntok], o_, o_[:, 0:ntok])
                    elif kind == "ak":
                        b.dma("sp", self.KT, self.KT.t[c, :, t0:t0 + ntok], o_, o_[:, 0:ntok])
                    elif kind == "rq":
                        for hh in range(2):
                            b.dma("sp", self.RQT, self.RQT.t[2 * c + hh, :, t0:t0 + ntok], o_, o_[hh * 64:(hh + 1) * 64, 0:ntok])
                    else:
                        for hh in range(2):
                            b.dma("sp", self.RKT, self.RKT.t[2 * c + hh, :, t0:t0 + ntok], o_, o_[hh * 64:(hh + 1) * 64, 0:ntok])
                        for s in range(nsub):
                            b.op("pe", lambda e, s=s: e.transpose(ptr[:, s * 128:(s + 1) * 128], o_[:, s * 128:(s + 1) * 128],
                                                                 self.identb[:]), r=[o_, self.identb], w=[ptr])
                        b.op("dve", lambda e: e.tensor_copy(
                            out=rkk[:, 0:nsub, c * 128:(c + 1) * 128],
                            in_=ptr[:, 0:ntok].rearrange("p (s c) -> p s c", c=128)), r=[ptr], w=[rkk])
                for s in range(nsub):
                    lt = lambda kc: uT[:, kc, s * 128:(s + 1) * 128]
                    for kc in range(KC):
                        b.op("pe", lambda e, kc=kc: e.matmul(pav[:], lhsT=lt(kc), rhs=win[:, kc, 1024:1536],
                                                             start=(kc == 0), stop=(kc == KC - 1)), r=[win, uT], w=[pav])
                    b.op("act", lambda e, s=s: e.copy(out=vst[:, s, :], in_=pav[:]), r=[pav], w=[vst])
                    for kc in range(KC):
                        b.op("pe", lambda e, kc=kc: e.matmul(pz[:, 0:256], lhsT=lt(kc), rhs=win[:, kc, 1536:1792],
                                                             start=(kc == 0), stop=(kc == KC - 1)), r=[win, uT], w=[pz])
                    for kc in range(KC):
                        b.op("pe", lambda e, kc=kc: e.matmul(pz[:, 256:260], lhsT=lt(kc), rhs=win[:, kc, 2560:2564],
                                                             start=(kc == 0), stop=(kc == KC - 1)), r=[win, uT], w=[pz])
                    b.op("act", lambda e, s=s: e.activation(out=zst[:, s, :], in_=pz[:, 0:256], func=AF.Silu), r=[pz], w=[zst])
                    b.op("dve", lambda e, s=s: e.tensor_copy(out=dst[:, s, :], in_=pz[:, 256:260]), r=[pz], w=[dst])
                    for kc in range(KC):
                        b.op("pe", lambda e, kc=kc: e.matmul(prr[:], lhsT=lt(kc), rhs=win[:, kc, 3076:3588],
                                                             start=(kc == 0), stop=(kc == KC - 1)), r=[win, uT], w=[prr])
                    b.op("act", lambda e, s=s: e.copy(out=rvst[:, s, :], in_=prr[:, 0:256]), r=[prr], w=[rvst])
                    b.op("act", lambda e, s=s: e.activation(out=rgst[:, s, :], in_=prr[:, 256:512], func=AF.Silu), r=[prr], w=[rgst])
                rows = lambda tl_: tl_.t[t0:t0 + ntok, :].rearrange("(s p) c -> p s c", p=128)
                b.dma("sp", self.V1, rows(self.V1), vst, vst[:, 0:nsub, :])
                b.dma("sp", self.Z, rows(self.Z), zst, zst[:, 0:nsub, :])
                b.dma("sp", self.RG, rows(self.RG), rgst, rgst[:, 0:nsub, :])
                b.dma("sp", self.RV, rows(self.RV), rvst, rvst[:, 0:nsub, :])
                b.dma("sp", self.DT, rows(self.DT), dst, dst[:, 0:nsub, :])
                b.dma("sp", self.RKK, rows(self.RKK), rkk, rkk[:, 0:nsub, :])
            b.barrier()
            b.release([win, uT, rkk, vst, zst, rgst, rvst, dst] + wstg + xin + tab + qs + ob + xb)

    def load_cols(self, st, name, src_tl, src_ap, nrow, ncol, pst):
        b = self.b
        nr = nrow + (nrow % 2)
        rowt = b.sb(st, name + "_row", [nr, ncol * 128], F32)
        colt = b.sb(st, name + "_col", [128, ncol, nr], F32)
        b.op("pool", lambda e: e.memset(rowt[:], 0.0), w=[rowt])
        b.dma("sp", rowt, rowt[0:nrow, :], src_tl, src_ap)
        for c in range(ncol):
            b.op("pe", lambda e, c=c: e.transpose(pst[:, c * nr:(c + 1) * nr], rowt[0:nr, c * 128:(c + 1) * 128],
                                                 self.ident[0:nr, 0:nr]), r=[rowt, self.ident], w=[pst])
        b.op("dve", lambda e: e.tensor_copy(out=colt[:], in_=pst[:, 0:ncol * nr].rearrange("p (c k) -> p c k", k=nr)),
             r=[pst], w=[colt])
        return colt, rowt

    def mixer_conv(self, l):
        b, S, SC = self.b, self.S, self.SC
        with ExitStack() as st:
            pst = b.ps(st, "pst", [128, 512])
            cw, r1 = self.load_cols(st, "cw", self.conv_w, self.conv_w.t[l], 5, 6, pst)
            cb, r2 = self.load_cols(st, "cb", self.conv_b, self.conv_b.t[l:l + 1, :], 1, 6, pst)
            for (t0, n) in ((0, S), (S, SC)):
                with ExitStack() as st2:
                    pre = b.sb(st2, "pre", [128, 6, n + 4], F32)
                    acc = [b.sb(st2, "acc%d" % k, [128, n], F32) for k in range(2)]
                    of = [b.sb(st2, "of%d" % k, [128, n], F32) for k in range(2)]
                    obf = [b.sb(st2, "obf%d" % k, [128, n], BF16) for k in range(2)]
                    b.op("pool", lambda e: e.memset(pre[:, :, 0:2], 0.0), w=[pre])
                    b.op("pool", lambda e: e.memset(pre[:, :, n + 2:n + 4], 0.0), w=[pre])
                    for c in range(6):
                        b.dma("sp", pre, pre[:, c, 2:n + 2], self.XBCT, self.XBCT.t[c, :, t0:t0 + n])
                    for c in range(6):
                        a_ = acc[c % 2]
                        b.op("dve", lambda e: e.tensor_scalar(a_[:], pre[:, c, 0:n], cw[:, c, 0:1], None, op0=ALU.mult),
                             r=[pre, cw], w=[a_])
                        for k in range(1, 5):
                            b.op("dve", lambda e, k=k: e.scalar_tensor_tensor(
                                out=a_[:], in0=pre[:, c, k:k + n], scalar=cw[:, c, k:k + 1], in1=a_[:], op0=ALU.mult,
                                op1=ALU.add), r=[pre, cw, a_], w=[a_])
                        if c < 2:
                            o_ = of[c % 2]
                            b.op("act", lambda e: e.activation(out=o_[:], in_=a_[:], func=AF.Silu, bias=cb[:, c, 0:1]),
                                 r=[a_, cb], w=[o_])
                            b.dma("sp", self.XST, self.XST.t[c, :, t0:t0 + n], o_, o_[:])
                        else:
                            o_ = obf[c % 2]
                            b.op("act", lambda e: e.activation(out=o_[:], in_=a_[:], func=AF.Silu, bias=cb[:, c, 0:1]),
                                 r=[a_, cb], w=[o_])
                            dstt = self.BT if c < 4 else self.CT
                            b.dma("sp", dstt, dstt.t[c % 2, :, t0:t0 + n], o_, o_[:])
                    b.barrier()
                    b.release([pre] + acc + of + obf)
            b.barrier()
            b.release([r1, r2])

    def mixer_attention(self, l):
        b, S, SC, T = self.b, self.S, self.SC, self.T
        need_ctx = l < self.depth - 1
        lam_init = 0.8 - 0.6 * math.exp(-0.3 * l)
        NKT = T // 128
        with ExitStack() as st:
            kt_sb = b.sb(st, "kt_sb", [128, 4, 2, T], BF16)
            v_sb = b.sb(st, "v_sb", [128, NKT, 512], BF16)
            for h in range(4):
                for m in range(2):
                    b.op("dve" if (h + m) % 2 == 0 else "pool", lambda e, h=h, m=m: e.memset(kt_sb[:, h, m, :], 0.0), w=[kt_sb])
            for h in range(4):
                for m in range(2):
                    b.dma("sp", kt_sb, kt_sb[m * 64:(m + 1) * 64, h, m, :], self.KT, self.KT.t[h, m * 64:(m + 1) * 64, :])
            b.dma("sp", v_sb, v_sb[:], self.V1, self.V1.t.rearrange("(k p) c -> p k c", p=128))
            lv = b.sb(st, "lv", [128, 4, 64], F32)
            lp = b.sb(st, "lp", [128, 2, 64], F32)
            ls = b.sb(st, "ls", [128, 2], F32)
            neglam = b.sb(st, "neglam", [128, 1], F32)
            gsub = b.sb(st, "gsub", [128, 1], F32)
            b.dma("sp", lv, lv[:], self.att_lambda, self.att_lambda.t[l].partition_broadcast(128))
            b.dma("sp", gsub, gsub[:], self.att_subln_g, self.att_subln_g.t[l].rearrange("(p o) -> p o", o=1))
            b.op("dve", lambda e: e.tensor_scalar(gsub[:], gsub[:], 1.0 - lam_init, None, op0=ALU.mult), r=[gsub], w=[gsub])
            for k in range(2):
                b.op("dve", lambda e, k=k: e.tensor_tensor(out=lp[:, k, :], in0=lv[:, 2 * k, :], in1=lv[:, 2 * k + 1, :],
                                                          op=ALU.mult), r=[lv], w=[lp])
                b.op("dve", lambda e, k=k: e.reduce_sum(out=ls[:, k:k + 1], in_=lp[:, k, :], axis=mybir.AxisListType.X),
                     r=[lp], w=[ls])
            b.op("act", lambda e: e.activation(out=ls[:], in_=ls[:], func=AF.Exp), r=[ls], w=[ls])
            b.op("dve", lambda e: e.tensor_tensor(out=neglam[:], in0=ls[:, 1:2], in1=ls[:, 0:1], op=ALU.subtract),
                 r=[ls], w=[neglam])
            b.op("dve", lambda e: e.tensor_scalar_add(neglam[:], neglam[:], -lam_init), r=[neglam], w=[neglam])
            qt = [b.sb(st, "qt%d" % k, [128, 4, 512], BF16) for k in range(2)]
            pb = [b.sb(st, "pb%d" % k, [128, 2, 512], BF16) for k in range(3)]
            ones128 = b.sb(st, "ones128", [128, 128], BF16)
            b.op("pool", lambda e: e.memset(ones128[:], 1.0), w=[ones128])
            os_ = [b.sb(st, "os%d" % k, [128, 512], F32) for k in range(2)]
            rl = [b.sb(st, "rl%d" % k, [128, 512], F32) for k in range(2)]
            tt = b.sb(st, "tt", [128, 512], F32)
            oo = b.sb(st, "oo", [128, 512], F32)
            sq = b.sb(st, "sq", [128, 512], F32)
            rs = b.sb(st, "rs", [128, 512], F32)
            mgo = [b.sb(st, "mgo%d" % k, [128, 512], BF16) for k in range(2)]
            psc = [b.ps(st, "psc%d" % k, [128, 2, 512]) for k in range(2)]
            po = [b.ps(st, "po%d" % k, [128, 512]) for k in range(2)]
            pl = [b.ps(st, "pl%d" % k, [128, 512]) for k in range(2)]
            pss = psc[0]
            ONES = self.cm[:, self.ONES, :]
            qblocks = [(i * 512, 512, list(range(NKT))) for i in range(S // 512)]
            if need_ctx:
                qblocks.append((S, SC, list(range(S // 128, NKT))))
            def load_q(qi_):
                q0_, nq_, _ = qblocks[qi_]
                b.dma("sp", qt[qi_ % 2], qt[qi_ % 2][:, :, 0:nq_], self.QT, self.QT.t[:, :, q0_:q0_ + nq_].rearrange("h p t -> p h t"))

            load_q(0)
            for qi, (q0, nq, kts) in enumerate(qblocks):
                q_ = qt[qi % 2]
                if qi + 1 < len(qblocks):
                    load_q(qi + 1)
                npair = len(kts) // 2
                items = [(h, m, pi) for h in range(4) for m in range(2) for pi in range(npair)]

                def qk(j):
                    h, m, pi = items[j]
                    sc_, p_ = psc[j % 2], pb[j % 3]
                    for a in range(2):
                        kt = kts[2 * pi + a]
                        b.op("pe", lambda e, a=a, kt=kt: e.matmul(sc_[:, a, 0:nq], lhsT=kt_sb[:, h, m, kt * 128:(kt + 1) * 128],
                                                                 rhs=q_[:, h, 0:nq], start=True, stop=True), r=[kt_sb, q_], w=[sc_])
                    b.op("act", lambda e: e.activation(out=p_[:, :, 0:nq], in_=sc_[:, :, 0:nq], func=AF.Exp, scale=0.125),
                         r=[sc_], w=[p_])

                def av(j):
                    h, m, pi = items[j]
                    p_ = pb[j % 3]
                    for a in range(2):
                        kt = kts[2 * pi + a]
                        first_, last_ = (pi == 0 and a == 0), (pi == npair - 1 and a == 1)
                        b.op("pe", lambda e, a=a, kt=kt: e.matmul(po[m][:, 0:nq], lhsT=v_sb[:, kt, h * 128:(h + 1) * 128],
                                                                 rhs=p_[:, a, 0:nq], start=first_, stop=last_),
                             r=[v_sb, p_], w=[po[m]])
                        b.op("pe", lambda e, a=a: e.matmul(pl[m][:, 0:nq], lhsT=ones128[:], rhs=p_[:, a, 0:nq],
                                                           start=first_, stop=last_), r=[ones128, p_], w=[pl[m]])

                def post(h):
                    for m in range(2):
                        b.op("act", lambda e, m=m: e.copy(out=os_[m][:, 0:nq], in_=po[m][:, 0:nq]), r=[po[m]], w=[os_[m]])
                        b.op("dve", lambda e, m=m: e.reciprocal(out=rl[m][:, 0:nq], in_=pl[m][:, 0:nq]), r=[pl[m]], w=[rl[m]])
                    b.op("dve", lambda e: e.tensor_tensor(out=tt[:, 0:nq], in0=os_[0][:, 0:nq], in1=rl[0][:, 0:nq], op=ALU.mult),
                         r=[os_[0], rl[0]], w=[tt])
                    b.op("dve", lambda e: e.scalar_tensor_tensor(out=oo[:, 0:nq], in0=os_[1][:, 0:nq], scalar=neglam[:, 0:1],
                                                                 in1=rl[1][:, 0:nq], op0=ALU.mult, op1=ALU.mult),
                         r=[os_[1], neglam, rl[1]], w=[oo])
                    b.op("pool", lambda e: e.tensor_tensor(out=oo[:, 0:nq], in0=oo[:, 0:nq], in1=tt[:, 0:nq], op=ALU.add),
                         r=[oo, tt], w=[oo])
                    b.op("act", lambda e: e.activation(out=sq[:, 0:nq], in_=oo[:, 0:nq], func=AF.Square), r=[oo], w=[sq])

                def post_b(h):
                    b.op("pe", lambda e: e.matmul(pss[:, 0, 0:nq], lhsT=ONES, rhs=sq[:, 0:nq], start=True, stop=True),
                         r=[self.cm, sq], w=[pss])
                    b.op("act", lambda e: e.activation(out=rs[:, 0:nq], in_=pss[:, 0, 0:nq], func=AF.Sqrt, scale=1.0 / 128,
                                                       bias=self.eps6[:]), r=[pss, self.eps6], w=[rs])
                    b.op("dve", lambda e: e.reciprocal(out=rs[:, 0:nq], in_=rs[:, 0:nq]), r=[rs], w=[rs])
                    g_ = mgo[h % 2]
                    b.op("dve", lambda e: e.scalar_tensor_tensor(out=g_[:, 0:nq], in0=oo[:, 0:nq], scalar=gsub[:, 0:1],
                                                                 in1=rs[:, 0:nq], op0=ALU.mult, op1=ALU.mult),
                         r=[oo, gsub, rs], w=[g_])
                    b.dma("sp", self.MGT, self.MGT.t[h * 128:(h + 1) * 128, q0:q0 + nq], g_, g_[:, 0:nq])

                n_it = len(items)
                qk(0)
                pend = None
                for j in range(n_it):
                    if j + 1 < n_it:
                        qk(j + 1)
                    av(j)
                    h, m, pi = items[j]
                    if pend is not None and j >= pend[1]:
                        post_b(pend[0])
                        pend = None
                    if m == 1 and pi == npair - 1:
                        post(h)
                        pend = (h, j + min(6, npair))
                if pend is not None:
                    post_b(pend[0])
            b.barrier()
            b.release([kt_sb, v_sb, lv, gsub] + qt + mgo)

    def mixer_scan(self, l, kind):
        b, S, SC, T = self.b, self.S, self.SC, self.T
        need_ctx = l < self.depth - 1
        ssd = kind == "ssd"
        n = 128 if ssd else 64
        NK = 2 if ssd else 4
        kq = (lambda h: h // 2) if ssd else (lambda h: h)
        QTd, KTd = (self.CT, self.BT) if ssd else (self.RQT, self.RKT)
        YP = self.YS if ssd else self.YR
        col_base = 512 if ssd else 768
        nlat = S // 128
        lat = [i * 128 for i in range(nlat)]
        ctxc = [S + i * 128 for i in range(SC // 128)]
        cm = self.cm
        with ExitStack() as st:
            prm = b.sb(st, "prm", [128, 3, 8], F32)
            if ssd:
                b.dma("sp", prm, prm[:, 0, :], self.ssm_a_log, self.ssm_a_log.t[l].partition_broadcast(128))
                b.dma("sp", prm, prm[:, 1, :], self.ssm_dt_bias, self.ssm_dt_bias.t[l].partition_broadcast(128))
                b.dma("sp", prm, prm[:, 2, :], self.ssm_d, self.ssm_d.t[l].partition_broadcast(128))
                negA = b.sb(st, "negA", [128, 8], F32)
                b.op("act", lambda e: e.activation(out=negA[:], in_=prm[:, 0, :], func=AF.Exp), r=[prm], w=[negA])
                b.op("dve", lambda e: e.tensor_scalar(negA[:], negA[:], -1.0, None, op0=ALU.mult), r=[negA], w=[negA])
                dsum = b.sb(st, "dsum", [128, 4], F32)
                b.op("dve", lambda e: e.tensor_tensor(out=dsum[:], in0=prm[:, 2, 0:4], in1=prm[:, 2, 4:8], op=ALU.add),
                     r=[prm], w=[dsum])
                dsum_bc = b.sb(st, "dsum_bc", [128, 4, 64], F32)
                b.op("pool", lambda e: e.memset(dsum_bc[:], 1.0), w=[dsum_bc])
                for h in range(4):
                    b.op("dve", lambda e, h=h: e.tensor_scalar(dsum_bc[:, h, :], dsum_bc[:, h, :], dsum[:, h:h + 1], None,
                                                              op0=ALU.mult), r=[dsum_bc, dsum], w=[dsum_bc])
                gn = b.sb(st, "gn", [128, 256], F32)
                b.dma("sp", gn, gn[:], self.ssm_norm_g, self.ssm_norm_g.t[l].partition_broadcast(128))
            else:
                b.dma("sp", prm, prm[:, 0, :], self.ret_log_gamma, self.ret_log_gamma.t[l].partition_broadcast(128))
            class WS:
                pass
            W = []
            for k in range(2):
                w = WS()
                sfx = "_%d" % k
                w.la = b.sb(st, "la" + sfx, [128, 4], F32)
                w.dtv = b.sb(st, "dtv" + sfx, [128, 4], F32)
                w.TL = b.sb(st, "TL" + sfx, [128, 4, 128], F32)
                w.dm = b.sb(st, "dm" + sfx, [128, 4, 128], F32)
                w.em = b.sb(st, "em" + sfx, [128, 4, 128], F32)
                w.ngam = b.sb(st, "ngam" + sfx, [128, 4], F32)
                w.totc = b.sb(st, "totc" + sfx, [128, 4], F32)
                w.dec = b.sb(st, "dec" + sfx, [128, 4, 128], F32)
                w.Ebc = b.sb(st, "Ebc" + sfx, [128, 4, 128], F32)
                w.wv = b.sb(st, "wv" + sfx, [128, 4], F32)
                w.etot = b.sb(st, "etot" + sfx, [128, 4], F32)
                w.coef = b.sb(st, "coef" + sfx, [128, 4], F32)
                w.xs = b.sb(st, "xs" + sfx, [128, 4, 64], F32)
                w.vd = b.sb(st, "vd" + sfx, [128, 4, 64], BF16)
                w.vw = b.sb(st, "vw" + sfx, [128, 4, 64], BF16)
                w.MT = b.sb(st, "MT" + sfx, [128, 4, 128], BF16)
                w.QpT = b.sb(st, "QpT" + sfx, [128, 4, 128], BF16)
                w.y2 = b.sb(st, "y2" + sfx, [128, 256], F32)
                w.y3 = b.sb(st, "y3" + sfx, [128, 256], F32)
                w.ss = b.sb(st, "ss" + sfx, [128, 4], F32)
                w.mvh = b.sb(st, "mvh" + sfx, [128, 4, 2], F32)
                w.sth = b.sb(st, "sth" + sfx, [128, 4, 6], F32)
                W.append(w)
            dtr = [b.sb(st, "dtr%d" % k, [128, 4], F32) for k in range(3)]
            qT = [b.sb(st, "qT%d" % k, [128, NK, 128], BF16) for k in range(3)]
            kT = [b.sb(st, "kT%d" % k, [128, NK, 128], BF16) for k in range(3)]
            ktm = [b.sb(st, "ktm%d" % k, [128, 256], BF16) for k in range(3)]
            xsT = [b.sb(st, "xsT%d" % k, [128, 2, 128], F32) for k in range(3)]
            vtm = [b.sb(st, "vtm%d" % k, [128, 4, 64], BF16) for k in range(3)]
            Sf = b.sb(st, "Sf", [128, 4, 64], F32)
            Sb = b.sb(st, "Sb", [128, 4, 64], BF16)
            ysb = [b.sb(st, "ysb%d" % k, [128, 256], F32) for k in range(3)]
            zt = [b.sb(st, "zt%d" % k, [128, 256], F32) for k in range(3)]
            mgo = [b.sb(st, "mgo%d" % k, [128, 2, 128], BF16) for k in range(2)]
            pc = b.ps(st, "pc", [128, 8])
            pG = b.ps(st, "pG", [128, 512])
            pGT = b.ps(st, "pGT", [128, 512])
            py = b.ps(st, "py", [128, 256])
            pS = b.ps(st, "pS", [128, 256])
            pX = b.ps(st, "pX", [128, 256])
            pK = b.ps(st, "pK", [128, 256], BF16)
            pM = b.ps(st, "pM", [128, 256])
            bc = lambda ap, shape: ap.broadcast_to(shape)
            fl = lambda t_: t_[:].rearrange("p h l -> p (h l)")

            def decay_quants(d, w):
                tri = self.TRI_F if d == 0 else self.NSTRICT_B
                la = w.la
                b.op("pe", lambda e: e.matmul(pc[:, 0:4], lhsT=cm[:, tri, :], rhs=la[:], start=True, stop=True),
                     r=[cm, la], w=[pc])
                b.op("pe", lambda e: e.matmul(pc[:, 4:8], lhsT=cm[:, self.ONES, :], rhs=la[:], start=True, stop=True),
                     r=[cm, la], w=[pc])
                b.op("dve", lambda e: e.tensor_tensor(out=w.TL[:], in0=bc(cm[:, tri, :].unsqueeze(1), [128, 4, 128]),
                                                      in1=bc(la[:].unsqueeze(2), [128, 4, 128]), op=ALU.mult),
                     r=[cm, la], w=[w.TL])
                b.op("pe", lambda e: e.matmul(pG[:], lhsT=cm[:, self.ONES, :], rhs=fl(w.TL), start=True, stop=True),
                     r=[cm, w.TL], w=[pG])
                b.op("dve", lambda e: e.tensor_scalar(w.ngam[:], pc[:, 0:4], -1.0, None, op0=ALU.mult), r=[pc], w=[w.ngam])
                b.op("dve", lambda e: e.tensor_copy(out=w.totc[:], in_=pc[:, 4:8]), r=[pc], w=[w.totc])
                b.op("dve", lambda e: e.tensor_tensor(out=fl(w.dm), in0=pG[:], in1=fl(self.mask4[d]), op=ALU.add),
                     r=[pG, self.mask4[d]], w=[w.dm])
                b.op("dve", lambda e: e.tensor_tensor(out=w.dm[:], in0=w.dm[:], in1=bc(w.ngam[:].unsqueeze(2), [128, 4, 128]),
                                                      op=ALU.add), r=[w.dm, w.ngam], w=[w.dm])
                b.op("act", lambda e: e.activation(out=fl(w.dec), in_=fl(w.dm), func=AF.Exp), r=[w.dm], w=[w.dec])
                if d == 0:
                    b.op("act", lambda e: e.activation(out=fl(w.Ebc), in_=pG[:], func=AF.Exp), r=[pG], w=[w.Ebc])
                    b.op("dve", lambda e: e.tensor_tensor(out=w.wv[:], in0=w.totc[:], in1=w.ngam[:], op=ALU.add),
                         r=[w.totc, w.ngam], w=[w.wv])
                    b.op("act", lambda e: e.activation(out=w.wv[:], in_=w.wv[:], func=AF.Exp), r=[w.wv], w=[w.wv])
                else:
                    b.op("dve", lambda e: e.tensor_tensor(out=w.em[:], in0=pG[:].rearrange("p (h l) -> p h l", l=128),
                                                          in1=bc(w.totc[:].unsqueeze(2), [128, 4, 128]), op=ALU.add),
                         r=[pG, w.totc], w=[w.em])
                    b.op("act", lambda e: e.activation(out=fl(w.Ebc), in_=fl(w.em), func=AF.Exp), r=[w.em], w=[w.Ebc])
                    b.op("act", lambda e: e.activation(out=w.wv[:], in_=w.ngam[:], func=AF.Exp), r=[w.ngam], w=[w.wv])
                b.op("act", lambda e: e.activation(out=w.etot[:], in_=w.totc[:], func=AF.Exp), r=[w.totc], w=[w.etot])

            for d in range(2):
                if d == 1:
                    b.barrier()
                order = (ctxc + lat) if d == 0 else (ctxc[::-1] + lat[::-1])
                b.op("pool", lambda e: e.memset(Sf[:], 0.0), w=[Sf])
                b.op("pool", lambda e: e.memset(Sb[:], 0.0), w=[Sb])
                if not ssd:
                    wc = W[0]
                    b.op("dve", lambda e: e.tensor_copy(out=wc.la[:], in_=prm[:, 0, d * 4:(d + 1) * 4]), r=[prm], w=[wc.la])
                    decay_quants(d, wc)
                def stage_l(ci, tc):
                    is_ctx = tc >= S
                    want_y = (not is_ctx) or need_ctx
                    q_, k_, km_, v_ = qT[ci % 3], kT[ci % 3], ktm[ci % 3], vtm[ci % 3]
                    x_, dr, yl, z_ = xsT[ci % 3], dtr[ci % 3], ysb[ci % 3], zt[ci % 3]
                    if ssd:
                        b.dma("sp", q_, q_[:], QTd, QTd.t[:, :, tc:tc + 128].rearrange("g p t -> p g t"))
                        b.dma("sp", k_, k_[:], KTd, KTd.t[:, :, tc:tc + 128].rearrange("g p t -> p g t"))
                        b.dma("sp", x_, x_[:], self.XST, self.XST.t[:, :, tc:tc + 128].rearrange("g p t -> p g t"))
                        b.dma("sp", dr, dr[:], self.DT, self.DT.t[tc:tc + 128, :])
                    else:
                        b.dma("sp", q_, q_[0:64, :, :], QTd, QTd.t[:, :, tc:tc + 128].rearrange("g p t -> p g t"))
                        b.dma("sp", k_, k_[0:64, :, :], KTd, KTd.t[:, :, tc:tc + 128].rearrange("g p t -> p g t"))
                        b.dma("sp", km_, km_[:], self.RKK, self.RKK.t[tc:tc + 128, :])
                        b.dma("sp", v_, v_[:].rearrange("p h e -> p (h e)"), self.RV, self.RV.t[tc:tc + 128, :])
                    if d == 1 and want_y:
                        b.dma("sp", yl, yl[:], YP, YP.t[tc:tc + 128, :])
                        zsrc = self.Z if ssd else self.RG
                        b.dma("sp", z_, z_[:], zsrc, zsrc.t[tc:tc + 128, :])

                def stage_a(ci, tc):
                    is_ctx = tc >= S
                    want_y = (not is_ctx) or need_ctx
                    w = W[ci % 2]
                    wq = w if ssd else W[0]
                    q_, k_, km_, v_ = qT[ci % 3], kT[ci % 3], ktm[ci % 3], vtm[ci % 3]
                    x_, dr, yl, z_ = xsT[ci % 3], dtr[ci % 3], ysb[ci % 3], zt[ci % 3]
                    if ssd:
                        b.op("dve", lambda e: e.tensor_tensor(out=w.dtv[:], in0=dr[:], in1=prm[:, 1, d * 4:(d + 1) * 4], op=ALU.add),
                             r=[dr, prm], w=[w.dtv])
                        b.op("act", lambda e: e.activation(out=w.dtv[:], in_=w.dtv[:], func=AF.Exp), r=[w.dtv], w=[w.dtv])
                        b.op("act", lambda e: e.activation(out=w.dtv[:], in_=w.dtv[:], func=AF.Ln, bias=self.one_t[:]),
                             r=[w.dtv, self.one_t], w=[w.dtv])
                        b.op("dve", lambda e: e.tensor_tensor(out=w.la[:], in0=w.dtv[:], in1=negA[:, d * 4:(d + 1) * 4], op=ALU.mult),
                             r=[w.dtv, negA], w=[w.la])
                        decay_quants(d, w)
                        for g in range(2):
                            b.op("pe", lambda e, g=g: e.transpose(pX[:, g * 128:(g + 1) * 128], x_[:, g, :], self.ident[:]),
                                 r=[x_, self.ident], w=[pX])
                        b.op("act", lambda e: e.copy(out=w.xs[:].rearrange("p h e -> p (h e)"), in_=pX[:]), r=[pX], w=[w.xs])
                        for g in range(2):
                            b.op("pe", lambda e, g=g: e.transpose(pK[:, g * 128:(g + 1) * 128], k_[:, g, :], self.identb[:]),
                                 r=[k_, self.identb], w=[pK])
                        b.op("act", lambda e: e.copy(out=km_[:], in_=pK[:]), r=[pK], w=[km_])
                        b.op("dve", lambda e: e.tensor_tensor(out=w.coef[:], in0=w.dtv[:], in1=w.wv[:], op=ALU.mult),
                             r=[w.dtv, w.wv], w=[w.coef])
                        b.op("dve", lambda e: e.tensor_tensor(out=w.vd[:], in0=w.xs[:], in1=bc(w.dtv[:].unsqueeze(2), [128, 4, 64]),
                                                              op=ALU.mult), r=[w.xs, w.dtv], w=[w.vd])
                        b.op("dve", lambda e: e.tensor_tensor(out=w.vw[:], in0=w.xs[:], in1=bc(w.coef[:].unsqueeze(2), [128, 4, 64]),
                                                              op=ALU.mult), r=[w.xs, w.coef], w=[w.vw])
                        vdd = w.vd
                    else:
                        b.op("dve", lambda e: e.tensor_tensor(out=w.vw[:], in0=v_[:], in1=bc(wq.wv[:].unsqueeze(2), [128, 4, 64]),
                                                              op=ALU.mult), r=[v_, wq.wv], w=[w.vw])
                        vdd = v_
                    if want_y:
                        for g in range(NK):
                            b.op("pe", lambda e, g=g: e.matmul(pGT[:, g * 128:(g + 1) * 128], lhsT=k_[0:n, g, :],
                                                               rhs=q_[0:n, g, :], start=True, stop=True), r=[k_, q_], w=[pGT])
                        if ssd:
                            for g in range(2):
                                b.op("dve", lambda e, g=g: e.tensor_tensor(
                                    out=w.MT[:, 2 * g:2 * g + 2, :],
                                    in0=bc(pGT[:, g * 128:(g + 1) * 128].unsqueeze(1), [128, 2, 128]),
                                    in1=wq.dec[:, 2 * g:2 * g + 2, :], op=ALU.mult), r=[pGT, wq.dec], w=[w.MT])
                                b.op("pool", lambda e, g=g: e.tensor_tensor(
                                    out=w.QpT[:, 2 * g:2 * g + 2, :], in0=bc(q_[:, g:g + 1, :], [128, 2, 128]),
                                    in1=wq.Ebc[:, 2 * g:2 * g + 2, :], op=ALU.mult), r=[q_, wq.Ebc], w=[w.QpT])
                        else:
                            b.op("dve", lambda e: e.tensor_tensor(out=fl(w.MT), in0=pGT[:], in1=fl(wq.dec), op=ALU.mult),
                                 r=[pGT, wq.dec], w=[w.MT])
                            b.op("dve", lambda e: e.tensor_tensor(out=w.QpT[0:64, :, :], in0=q_[0:64, :, :], in1=wq.Ebc[0:64, :, :],
                                                                   op=ALU.mult), r=[q_, wq.Ebc], w=[w.QpT])
                    return dict(w=w, wq=wq, q_=q_, k_=k_, km_=km_, v_=v_, vdd=vdd, want_y=want_y,
                                yl=(yl if (d == 1 and want_y) else None), z_=(z_ if (d == 1 and want_y) else None))

                def stage_b(ci, tc, cx):
                    w, wq, q_, k_, km_, v_, vdd, want_y, yl, z_ = (cx[k] for k in ('w', 'wq', 'q_', 'k_', 'km_', 'v_', 'vdd', 'want_y', 'yl', 'z_'))
                    if want_y:
                        for h in range(4):
                            b.op("pe", lambda e, h=h: e.matmul(py[:, h * 64:(h + 1) * 64], lhsT=w.MT[:, h, :], rhs=vdd[:, h, :],
                                                               start=True, stop=False), r=[w.MT, vdd], w=[py])
                            b.op("pe", lambda e, h=h: e.matmul(py[:, h * 64:(h + 1) * 64], lhsT=w.QpT[0:n, h, :], rhs=Sb[0:n, h, :],
                                                               start=False, stop=True), r=[w.QpT, Sb], w=[py])
                    last = ci == len(order) - 1
                    if not last:
                        for h in range(4):
                            g = kq(h)
                            b.op("pe", lambda e, h=h, g=g: e.matmul(pS[0:n, h * 64:(h + 1) * 64], lhsT=km_[:, g * n:(g + 1) * n],
                                                                   rhs=w.vw[:, h, :], start=True, stop=True), r=[km_, w.vw], w=[pS])
                        b.op("dve", lambda e: e.tensor_tensor(out=Sf[0:n, :, :], in0=Sf[0:n, :, :],
                                                              in1=bc(wq.etot[0:n, :].unsqueeze(2), [n, 4, 64]), op=ALU.mult),
                             r=[Sf, wq.etot], w=[Sf])
                        b.op("dve", lambda e: e.tensor_tensor(out=Sf[0:n, :, :].rearrange("p h e -> p (h e)"),
                                                              in0=Sf[0:n, :, :].rearrange("p h e -> p (h e)"), in1=pS[0:n, :], op=ALU.add),
                             r=[Sf, pS], w=[Sf])
                        b.op("act", lambda e: e.copy(out=Sb[0:n, :, :], in_=Sf[0:n, :, :]), r=[Sf], w=[Sb])
                    if not want_y:
                        return False
                    if d == 0:
                        yo = ysb[ci % 3]
                        b.op("act", lambda e: e.copy(out=yo[:], in_=py[:]), r=[py], w=[yo])
                        b.dma("sp", YP, YP.t[tc:tc + 128, :], yo, yo[:])
                        return False
                    y2, y3, ss, mvh, sth = w.y2, w.y3, w.ss, w.mvh, w.sth
                    b.op("dve", lambda e: e.tensor_tensor(out=y2[:], in0=py[:], in1=yl[:], op=ALU.add), r=[py, yl], w=[y2])
                    if ssd:
                        b.op("pool", lambda e: e.tensor_tensor(out=y3[:], in0=w.xs[:].rearrange("p h e -> p (h e)"),
                                                               in1=dsum_bc[:].rearrange("p h e -> p (h e)"), op=ALU.mult),
                             r=[w.xs, dsum_bc], w=[y3])
                        b.op("pool", lambda e: e.tensor_tensor(out=y2[:], in0=y2[:], in1=y3[:], op=ALU.add), r=[y2, y3], w=[y2])
                        b.op("dve", lambda e: e.tensor_tensor(out=y2[:], in0=y2[:], in1=z_[:], op=ALU.mult), r=[y2, z_], w=[y2])
                        b.op("act", lambda e: e.activation(out=y3[:], in_=y2[:], func=AF.Square, accum_out=ss[:, 0:1]),
                             r=[y2], w=[y3, ss])
                        b.op("act", lambda e: e.activation(out=ss[:, 1:2], in_=ss[:, 0:1], func=AF.Ln, scale=1.0 / 256,
                                                           bias=self.eps6[:]), r=[ss, self.eps6], w=[ss])
                        b.op("act", lambda e: e.activation(out=ss[:, 2:3], in_=ss[:, 1:2], func=AF.Exp, scale=-0.5), r=[ss], w=[ss])
                        b.op("dve", lambda e: e.scalar_tensor_tensor(out=y3[:], in0=y2[:], scalar=ss[:, 2:3], in1=gn[:],
                                                                     op0=ALU.mult, op1=ALU.mult), r=[y2, ss, gn], w=[y3])
                    else:
                        for h in range(4):
                            b.op("dve", lambda e, h=h: e.bn_stats(out=sth[:, h, :], in_=y2[:, h * 64:(h + 1) * 64]), r=[y2], w=[sth])
                            b.op("dve", lambda e, h=h: e.bn_aggr(out=mvh[:, h, :], in_=sth[:, h, :]), r=[sth], w=[mvh])
                        b.op("act", lambda e: e.activation(out=ss[:], in_=mvh[:, :, 1], func=AF.Ln, bias=self.eps_t[:]),
                             r=[mvh, self.eps_t], w=[ss])
                        b.op("act", lambda e: e.activation(out=ss[:], in_=ss[:], func=AF.Exp, scale=-0.5), r=[ss], w=[ss])
                        y2v = y2[:].rearrange("p (h e) -> p h e", e=64)
                        y3v = y3[:].rearrange("p (h e) -> p h e", e=64)
                        b.op("dve", lambda e: e.tensor_tensor(out=y3v, in0=y2v, in1=bc(mvh[:, :, 0:1], [128, 4, 64]), op=ALU.subtract),
                             r=[y2, mvh], w=[y3])
                        b.op("dve", lambda e: e.tensor_tensor(out=y3v, in0=y3v, in1=bc(ss[:].unsqueeze(2), [128, 4, 64]), op=ALU.mult),
                             r=[y3, ss], w=[y3])
                        b.op("dve", lambda e: e.tensor_tensor(out=y3[:], in0=y3[:], in1=z_[:], op=ALU.mult), r=[y3, z_], w=[y3])
                    return True

                def stage_c(ci, tc, cx):
                    y3 = cx['w'].y3
                    g_ = mgo[ci % 2]
                    for j in range(2):
                        b.op("pe", lambda e, j=j: e.transpose(pM[:, j * 128:(j + 1) * 128], y3[:, j * 128:(j + 1) * 128],
                                                             self.ident[:]), r=[y3, self.ident], w=[pM])
                    b.op("act", lambda e: e.copy(out=g_[:].rearrange("p j t -> p (j t)"), in_=pM[:]), r=[pM], w=[g_])
                    b.dma("sp", self.MGT, self.MGT.t[col_base:col_base + 256, tc:tc + 128].rearrange("(j p) t -> p j t", p=128),
                          g_, g_[:])
                stage_l(0, order[0])
                if len(order) > 1:
                    stage_l(1, order[1])
                cxs = {0: stage_a(0, order[0])}
                prev_c = None
                for ci, tc in enumerate(order):
                    if ci + 2 < len(order):
                        stage_l(ci + 2, order[ci + 2])
                    if ci + 1 < len(order):
                        cxs[ci + 1] = stage_a(ci + 1, order[ci + 1])
                    cx_ = cxs.pop(ci)
                    pend_ = stage_b(ci, tc, cx_)
                    if prev_c is not None:
                        stage_c(*prev_c)
                        prev_c = None
                    if pend_:
                        prev_c = (ci, tc, cx_)
                if prev_c is not None:
                    stage_c(*prev_c)
            b.barrier()
            rel = [prm] + dtr + qT + kT + ktm + xsT + vtm + ysb + zt + mgo
            if ssd:
                rel.append(gn)
            b.release(rel)

    def mixer_outproj(self, l):
        b, S, T = self.b, self.S, self.T
        need_ctx = l < self.depth - 1
        with ExitStack() as st:
            gate, g_bc, b_bc = self.load_bcast(st, l, 1, 1.0)
            wo = b.sb(st, "wo", [128, KC, D], BF16)
            wstg = [b.sb(st, "wostg%d" % k, [128, D], F32) for k in range(2)]
            engs = ["act", "pool", "dve"]
            for kc in range(KC):
                s_ = wstg[kc % 2]
                b.dma("sp", s_, s_[:], self.w_out, self.w_out.t[l, kc * 128:(kc + 1) * 128, :])
                self.cast_to(engs[kc % 3], wo[:, kc, :], s_[:], [s_], [wo])
            mg = [b.sb(st, "mg%d" % k, [128, KC, 512], BF16) for k in range(2)]
            xin = [b.sb(st, "xin%d" % k, [128, 4, D], F32) for k in range(2)]
            ybuf = [b.sb(st, "ybuf%d" % k, [128, D], F32) for k in range(2)]
            stt = b.sb(st, "stt", [128, 12], F32)
            mv = b.sb(st, "mv", [128, 2], F32)
            rstd = b.sb(st, "rstd", [128, 1], F32)
            nmr = b.sb(st, "nmr", [128, 1], F32)
            pd = [b.ps(st, "pd%d" % k, [128, D]) for k in range(2)]
            it = 0
            blks = [(bi, t0, ntok, v) for bi, (t0, ntok, v) in enumerate(self.blocks) if not (v == 1 and not need_ctx)]

            def load_blk(k):
                bi_, t0_, ntok_, v_ = blks[k]
                b.dma("sp", xin[bi_ % 2], xin[bi_ % 2][:, 0:ntok_ // 128, :], self.Xb[bi_],
                      self.X.t[t0_:t0_ + ntok_, :].rearrange("(s p) d -> p s d", p=128))
                b.dma("sp", mg[bi_ % 2], mg[bi_ % 2][:, :, 0:ntok_], self.MGT,
                      self.MGT.t[:, t0_:t0_ + ntok_].rearrange("(k p) t -> p k t", p=128))

            load_blk(0)
            for k_, (bi, t0, ntok, v) in enumerate(blks):
                if k_ + 1 < len(blks):
                    load_blk(k_ + 1)
                nsub = ntok // 128
                xi, m_ = xin[bi % 2], mg[bi % 2]
                for s in range(nsub):
                    p_ = pd[it % 2]
                    y = ybuf[it % 2]
                    it += 1
                    for h in range(2):
                        for kc in range(KC):
                            b.op("pe", lambda e, kc=kc, h=h: e.matmul(p_[:, h * 512:(h + 1) * 512], lhsT=m_[:, kc, s * 128:(s + 1) * 128],
                                                                     rhs=wo[:, kc, h * 512:(h + 1) * 512], start=(kc == 0),
                                                                     stop=(kc == KC - 1)), r=[m_, wo], w=[p_])
                    b.op("dve", lambda e: e.tensor_tensor(out=y[:], in0=p_[:], in1=gate[v][:], op=ALU.mult), r=[p_, gate[v]], w=[y])
                    b.op("dve", lambda e, s=s: e.scalar_tensor_tensor(out=y[:], in0=xi[:, s, :], scalar=ALPHA, in1=y[:],
                                                                     op0=ALU.mult, op1=ALU.add), r=[xi, y], w=[y])
                    self._xo_tl = xi
                    self.layer_norm_store(y, xi[:, s, :], g_bc, b_bc, stt, mv, rstd, nmr)
                b.dma("sp", self.Xb[bi], self.X.t[t0:t0 + ntok, :].rearrange("(s p) d -> p s d", p=128), xi, xi[:, 0:nsub, :])
            b.barrier()
            b.release([gate[0], gate[1], g_bc, b_bc, wo] + wstg + mg + xin)

    def build(self):
        b = self.b
        with b.es:
            self.declare_io()
            self.declare_mixer_io()
            with ExitStack() as st:
                self.eps_t = b.sb(st, "eps_t", [128, 1], F32)
                b.op("pool", lambda e: e.memset(self.eps_t[:], LN_EPS), w=[self.eps_t])
                self.prologue(st)
                self.load_consts(st)
                b.barrier()
                first = True
                stop = self.stop_after
                for l in range(self.depth):
                    need_ctx = l < self.depth - 1
                    self.ffn(l, 0, first=first)
                    first = False
                    if stop == ("ffn0", l):
                        break
                    self.mixer_inproj(l)
                    self.mixer_conv(l)
                    if stop == ("inproj", l):
                        break
                    self.mixer_attention(l)
                    if stop == ("att", l):
                        break
                    self.mixer_scan(l, "ssd")
                    if stop == ("ssd", l):
                        break
                    self.mixer_scan(l, "ret")
                    if stop == ("ret", l):
                        break
                    self.mixer_outproj(l)
                    if stop == ("mix", l):
                        break
                    direct = (l == self.depth - 1) and not self.dbg and stop is None
                    self.ffn(l, 1, skip_ctx=not need_ctx, to_out=direct)
                for bi, (t0, ntok, v) in enumerate(self.blocks):
                    if (l == self.depth - 1) and not self.dbg and stop is None:
                        break
                    if t0 < self.out_rows:
                        b.dma("sp", self.out, self.out.t[t0:t0 + ntok, :], self.Xb[bi], self.X.t[t0:t0 + ntok, :],
                              sem_tl=self.out)
                for name in self.dbg.get("dump", []):
                    src = getattr(self, name)
                    dst = dram_tl(b, "dbg_" + name, list(src.t.shape), src.t.dtype, "ExternalOutput")
                    b.dma("sp", dst, dst.t, src, src.t, sem_tl=self.out)
                E = b.engs["sp"]
                for ds in b.dsems:
                    if ds.cum > 0:
                        E.h.wait_ge(ds.sem, ds.cum)
                b.barrier()
        return self.nc


def _rot_tables(S):
    f32 = np.float32
    nb = S // 512
    t = np.arange(S, dtype=f32)
    row_pos = np.floor(t / f32(64)).astype(f32)
    col_pos = (t - row_pos * f32(64)).astype(f32)
    axis_freq = (f32(1.0) / (f32(10000.0) ** (np.arange(0, 32, 2, dtype=f32) / f32(32)))).astype(f32)
    ret_freq = (f32(1.0) / (f32(10000.0) ** np.linspace(0.0, 1.0, 32, dtype=f32))).astype(f32)
    r = np.arange(128)
    d = r % 64
    fa = axis_freq[d % 16]
    pos_a = np.where((d < 32)[:, None], row_pos[None, :], col_pos[None, :]).astype(f32)
    ang_a = (pos_a * fa[:, None]).astype(f32)
    sgn_a = np.where((d % 32) < 16, -1.0, 1.0).astype(f32)
    fr = ret_freq[d % 32]
    ang_r = (t[None, :] * fr[:, None]).astype(f32)
    sgn_r = np.where(d < 32, -1.0, 1.0).astype(f32)
    cosA, sinA = np.cos(ang_a).astype(f32), (np.sin(ang_a).astype(f32) * sgn_a[:, None])
    cosR, sinR = np.cos(ang_r).astype(f32), (np.sin(ang_r).astype(f32) * sgn_r[:, None])
    tabs = np.stack([cosA, sinA, cosR, sinR, cosR * f32(0.125), sinR * f32(0.125)], axis=1)
    return np.ascontiguousarray(tabs.reshape(128, 6, nb, 512).transpose(2, 0, 1, 3)).astype(f32)


def _cmats():
    cm = np.zeros((128, 8, 128), np.float32)
    r = np.arange(128)
    permA = (r // 32) * 32 + ((r % 32) + 16) % 32
    permR = (r // 64) * 64 + ((r % 64) + 32) % 64
    cm[permA, 0, r] = 1.0
    cm[permR, 1, r] = 1.0
    s, l = np.meshgrid(r, r, indexing="ij")
    cm[:, 2, :] = (s <= l)
    cm[:, 3, :] = -1.0 * (s < l)
    cm[:, 4, :] = 1.0
    cm[:, 5, :] = np.where(s <= l, 0.0, -30000.0)
    cm[:, 6, :] = np.where(s >= l, 0.0, -30000.0)
    return cm


def make_in_maps(inputs, S, SC, depth, n_cores):
    f = lambda a: np.ascontiguousarray(np.asarray(a, dtype=np.float32))
    L = depth
    shared = {
        "ident": np.eye(128, dtype=np.float32),
        "rot_tab": _rot_tables(S),
        "cmats": _cmats(),
    }
    for k in ("ada_w", "ada_b", "norm_g", "norm_b", "ffn_w_gate", "ffn_w_up", "ffn_w_down", "w_in", "w_out", "conv_w",
              "conv_b", "att_lambda", "att_subln_g", "ssm_norm_g"):
        shared[k] = f(inputs[k][:L])
    for k in ("ssm_a_log", "ssm_dt_bias", "ssm_d", "ret_log_gamma"):
        shared[k] = f(np.asarray(inputs[k][:L]).reshape(L, 8))
    maps = []
    for c in range(n_cores):
        m = dict(shared)
        m["x"] = f(inputs["x"][c])
        m["ctx"] = f(inputs["ctx"][c])
        m["c2"] = f(np.stack([np.asarray(inputs["c"][c]), np.asarray(inputs["c_ctx"])], axis=1))
        maps.append(m)
    return maps


def kernel(**inputs):
    S, SC = 4096, 256
    n = 8
    prog = Prog(S, SC, DEPTH)
    nc = prog.build()
    maps = make_in_maps(inputs, S, SC, DEPTH, n)
    res = run_bass_kernel_spmd(nc, maps, core_ids=list(range(n)))
    return np.stack([r["out"] for r in res.results], axis=0).astype(np.float32)
```

```python
import math
from contextlib import ExitStack

import numpy as np
import concourse.bass as bass
import concourse.mybir as mybir
from concourse.bass_utils import run_bass_kernel_spmd

F32 = mybir.dt.float32
BF16 = mybir.dt.bfloat16
AF = mybir.ActivationFunctionType
ALU = mybir.AluOpType

D = 1024
DFF = 2816
KC = D // 128
FC = DFF // 128
DEPTH = 4
LN_EPS = 1e-5
ALPHA = (2 * DEPTH) ** 0.25
INC = 3588


class Eng:
    def __init__(self, name, h, sem):
        self.name, self.h, self.sem, self.cnt = name, h, sem, 0
        self.waited = {}


class DSem:
    def __init__(self, sem):
        self.sem, self.cum = sem, 0


class Tl:
    def __init__(self, t, name):
        self.t, self.name = t, name
        self.w = {}
        self.r = {}
        self.ds = None

    def __getitem__(self, k):
        return self.t[k]


class Bld:
    def __init__(self, nc):
        self.nc = nc
        self.es = ExitStack()
        self.engs = {}
        for name, h in [("pe", nc.tensor), ("act", nc.scalar), ("dve", nc.vector), ("pool", nc.gpsimd),
                        ("sp", nc.sync)]:
            sem = self.es.enter_context(nc.semaphore("sem_" + name))
            self.engs[name] = Eng(name, h, DSem(sem))
        self.dsems = []
        self.free_dsems = []
        self.n_instr = 0

    def _uniq(self, name):
        self.n_names = getattr(self, "n_names", 0) + 1
        return "%s_%d" % (name, self.n_names)

    def sb(self, stack, name, shape, dt):
        t = stack.enter_context(self.nc.sbuf_tensor(self._uniq(name), list(shape), dt))
        return Tl(t, name)

    def ps(self, stack, name, shape, dt=F32):
        esz = 4 if dt == F32 else 2
        per_bank = 2048 // esz
        nfree = 1
        for d_ in shape[1:]:
            nfree *= d_
        nb = (nfree + per_bank - 1) // per_bank
        raw = stack.enter_context(self.nc.psum_tensor(self._uniq(name), [128, nb * per_bank], dt))
        v = raw[0:shape[0], 0:nfree]
        if len(shape) == 3:
            v = v.rearrange("p (a b) -> p a b", b=shape[2])
        elif len(shape) != 2:
            raise ValueError("ps: 2-D or 3-D shapes only")
        tl = Tl(v, name)
        tl.psum = True
        return tl

    def dram(self, name, shape, dt, kind="Internal"):
        t = self.nc.dram_tensor(name, list(shape), dt, kind=kind).ap()
        return Tl(t, name)

    def _dsem(self, tl):
        if tl.ds is None:
            if self.free_dsems:
                tl.ds = self.free_dsems.pop()
            else:
                sem = self.es.enter_context(self.nc.semaphore("dsem%d" % len(self.dsems)))
                tl.ds = DSem(sem)
                self.dsems.append(tl.ds)
        return tl.ds

    def _wait(self, E, rec):
        so, val, src = rec
        if src == "pe" and E.name == "pe":
            return
        if src == "dma":
            val = so.cum
        if E.waited.get(id(so), 0) >= val:
            return
        E.h.wait_ge(so.sem, val)
        E.waited[id(so)] = val
        self.n_instr += 1

    def _deps(self, E, r, w):
        for t in r:
            for rec in t.w.values():
                self._wait(E, rec)
            if getattr(t, "psum", False):
                for k, rec in t.r.items():
                    if k != E.name:
                        self._wait(E, rec)
        for t in w:
            for rec in t.w.values():
                self._wait(E, rec)
            for rec in t.r.values():
                self._wait(E, rec)

    def op(self, eng, fn, r=(), w=()):
        E = self.engs[eng]
        self._deps(E, r, w)
        ins = fn(E.h)
        E.cnt += 1
        E.sem.cum = E.cnt
        ins.then_inc(E.sem.sem, 1)
        rec = (E.sem, E.cnt, eng)
        for t in r:
            t.r[eng] = rec
        for t in w:
            t.w = {eng: rec}
            t.r = {}
        self.n_instr += 1
        return ins

    def dma(self, eng, out_tl, out_ap, in_tl, in_ap, sem_tl=None, **kw):
        E = self.engs[eng]
        tr_in = not (_is_dram(in_tl) and not getattr(in_tl, "tracked", False))
        tr_out = not (_is_dram(out_tl) and not getattr(out_tl, "tracked", False))
        self._deps(E, [in_tl] if tr_in else [], [out_tl] if tr_out else [])
        if sem_tl is None:
            sem_tl = out_tl if not _is_dram(out_tl) else in_tl
        ds = self._dsem(sem_tl)
        ins = E.h.dma_start(out=out_ap, in_=in_ap, **kw)
        ds.cum += 16
        ins.then_inc(ds.sem, 16)
        rec = (ds, ds.cum, "dma")
        if tr_in:
            in_tl.r["dma%d" % id(ds)] = rec
        if tr_out:
            out_tl.w = {"dma%d" % id(ds): rec}
            out_tl.r = {}
        self.n_instr += 1
        return ins

    def barrier(self):
        recs = [(E.sem, E.cnt, E.name) for E in self.engs.values() if E.cnt > 0]
        drecs = [(ds, ds.cum, "dma") for ds in self.dsems if ds.cum > 0]
        for E in self.engs.values():
            for rec in recs:
                if rec[2] != E.name:
                    so, val, src = rec
                    if E.waited.get(id(so), 0) < val:
                        E.h.wait_ge(so.sem, val)
                        E.waited[id(so)] = val
                        self.n_instr += 1
            for rec in drecs:
                self._wait(E, rec)

    def release(self, tls):
        for t in tls:
            if t.ds is not None:
                self.free_dsems.append(t.ds)
                t.ds = None


def _is_dram(tl):
    return getattr(tl, "is_dram", False)


def dram_tl(b, name, shape, dt, kind="Internal"):
    tl = b.dram(name, shape, dt, kind)
    tl.is_dram = True
    return tl


class Prog:
    def __init__(self, S=4096, SC=256, depth=DEPTH, dbg=None, stop_after=None):
        self.S, self.SC, self.depth = S, SC, depth
        self.T = S + SC
        self.dbg = dbg or {}
        self.stop_after = stop_after
        nc = bass.Bass("TRN2", target_bir_lowering=False)
        self.nc = nc
        self.b = Bld(nc)
        self.blocks = [(i * 512, 512, 0) for i in range(S // 512)] + [(S, SC, 1)]

    def declare_io(self):
        b, L = self.b, self.depth
        ein = lambda name, shape: dram_tl(b, name, shape, F32, "ExternalInput")
        self.x_in = ein("x", [self.S, D])
        self.ctx_in = ein("ctx", [self.SC, D])
        self.c2_in = ein("c2", [D, 2])
        self.ident_in = ein("ident", [128, 128])
        self.ada_w = ein("ada_w", [L, D, 9 * D])
        self.ada_b = ein("ada_b", [L, 9 * D])
        self.norm_g = ein("norm_g", [L, 3, D])
        self.norm_b = ein("norm_b", [L, 3, D])
        self.w_gate = ein("ffn_w_gate", [L, 2, D, DFF])
        self.w_up = ein("ffn_w_up", [L, 2, D, DFF])
        self.w_down = ein("ffn_w_down", [L, 2, DFF, D])
        self.out_rows = self.T if self.dbg.get("full_out") else self.S
        self.out = dram_tl(b, "out", [self.out_rows, D], F32, "ExternalOutput")
        self.X = dram_tl(b, "X_scr", [self.T, D], F32)
        self.DP = dram_tl(b, "DP_scr", [self.T, D], F32)
        self.Xb = [self._view(self.X) for _ in self.blocks]
        self.DPb = [self._view(self.DP) for _ in self.blocks]
        self.m_dram = dram_tl(b, "m_scr", [L, 2, 9 * D], F32)

    def prologue(self, st):
        b, L = self.b, self.depth
        self.ident = b.sb(st, "ident", [128, 128], F32)
        b.dma("sp", self.ident, self.ident[:], self.ident_in, self.ident_in.t)
        self.mcol = b.sb(st, "mcol", [128, L, 72, 2], F32)
        with ExitStack() as ps:
            c2 = b.sb(ps, "c2", [128, KC, 2], F32)
            sc2 = b.sb(ps, "sc2", [128, KC, 2], F32)
            b.dma("sp", c2, c2[:], self.c2_in, self.c2_in.t.rearrange("(k p) v -> p k v", p=128))
            b.op("act", lambda e: e.activation(out=sc2[:], in_=c2[:], func=AF.Silu), r=[c2], w=[sc2])
            aw = [b.sb(ps, "aw%d" % i, [128, KC, 1024], F32) for i in range(2)]
            adab = b.sb(ps, "adab", [2, 9 * D], F32)
            mrow = b.sb(ps, "mrow", [2, 9 * D], F32)
            pm = [b.ps(ps, "pm%d" % i, [2, 1024]) for i in range(2)]
            pcol = b.ps(ps, "pcol", [128, 72, 2])
            it = 0
            for l in range(L):
                for v in range(2):
                    b.dma("sp", adab, adab[v:v + 1, :], self.ada_b, self.ada_b.t[l:l + 1, :])
                for cg in range(9):
                    a = aw[it % 2]
                    p = pm[it % 2]
                    it += 1
                    b.dma("sp", a, a[:], self.ada_w,
                          self.ada_w.t[l, :, cg * 1024:(cg + 1) * 1024].rearrange("(k p) n -> p k n", p=128))
                    for h in range(2):
                        for kc in range(KC):
                            b.op("pe", lambda e, kc=kc, h=h: e.matmul(
                                p[0:2, h * 512:(h + 1) * 512], lhsT=sc2[:, kc, :],
                                rhs=a[:, kc, h * 512:(h + 1) * 512], start=(kc == 0), stop=(kc == KC - 1)),
                                r=[sc2, a], w=[p])
                    b.op("dve", lambda e: e.tensor_tensor(
                        out=mrow[0:2, cg * 1024:(cg + 1) * 1024], in0=p[0:2, :],
                        in1=adab[0:2, cg * 1024:(cg + 1) * 1024], op=ALU.add), r=[p, adab], w=[mrow])
                for j in range(72):
                    b.op("pe", lambda e, j=j: e.transpose(pcol[:, j, :], mrow[0:2, j * 128:(j + 1) * 128],
                                                        self.ident[0:2, 0:2]),
                         r=[mrow, self.ident], w=[pcol])
                b.op("dve", lambda e: e.tensor_copy(out=self.mcol[:, l, :, :], in_=pcol[:]),
                     r=[pcol], w=[self.mcol])
                b.dma("sp", self.m_dram, self.m_dram.t[l], mrow, mrow[0:2, :])
                for i in range(3):
                    j0 = (i * 3 + 1) * 8
                    b.op("dve", lambda e, j0=j0: e.tensor_scalar_add(
                        self.mcol[:, l, j0:j0 + 8, :], self.mcol[:, l, j0:j0 + 8, :], 1.0),
                        r=[self.mcol], w=[self.mcol])
            b.barrier()
            b.release(aw + [adab, mrow, c2])

    def load_bcast(self, st, l, i, gate_mul):
        b = self.b
        gate = [b.sb(st, "gate_bc%d" % v, [128, D], F32) for v in range(2)]
        g_bc = b.sb(st, "g_bc", [128, D], F32)
        b_bc = b.sb(st, "b_bc", [128, D], F32)
        off = (i * 3 + 2) * D
        for v in range(2):
            b.dma("sp", gate[v], gate[v][:], self.m_dram,
                  self.m_dram.t[l, v, off:off + D].partition_broadcast(128))
            b.op("dve", lambda e, v=v: e.tensor_scalar(gate[v][:], gate[v][:], 1.0, gate_mul, op0=ALU.add,
                                                      op1=ALU.mult), r=[gate[v]], w=[gate[v]])
        b.dma("sp", g_bc, g_bc[:], self.norm_g, self.norm_g.t[l, i, :].partition_broadcast(128))
        b.dma("sp", b_bc, b_bc[:], self.norm_b, self.norm_b.t[l, i, :].partition_broadcast(128))
        return gate, g_bc, b_bc

    def _view(self, tl):
        v = Tl(tl.t, tl.name)
        v.is_dram = True
        v.tracked = True
        return v

    def src_rows(self, first, bi, t0, n):
        if first:
            if t0 < self.S:
                return self.x_in, self.x_in.t[t0:t0 + n, :]
            return self.ctx_in, self.ctx_in.t[t0 - self.S:t0 - self.S + n, :]
        return self.Xb[bi], self.X.t[t0:t0 + n, :]

    def transpose_mod(self, xin, nsub, pT, uT, l, i, v, kcs=None):
        b = self.b
        ntok = nsub * 128
        for kc in (range(KC) if kcs is None else kcs):
            p = pT[kc % len(pT)]
            for s in range(nsub):
                b.op("pe", lambda e, kc=kc, s=s: e.transpose(
                    p[:, s * 128:(s + 1) * 128], xin[:, s, kc * 128:(kc + 1) * 128], self.ident[:]),
                    r=[xin, self.ident], w=[p])
            jsc = (i * 3 + 1) * 8 + kc
            jsh = (i * 3 + 0) * 8 + kc
            b.op("act", lambda e, kc=kc, jsc=jsc, jsh=jsh: e.activation(
                out=uT[:, kc, 0:ntok], in_=p[:, 0:ntok], func=AF.Identity,
                scale=self.mcol[:, l, jsc, v:v + 1], bias=self.mcol[:, l, jsh, v:v + 1]),
                r=[p, self.mcol], w=[uT])

    def layer_norm_store(self, y, xo, g_bc, b_bc, stt, mv, rstd, nmr):
        b = self.b
        for h in range(2):
            b.op("dve", lambda e, h=h: e.bn_stats(out=stt[:, h * 6:(h + 1) * 6], in_=y[:, h * 512:(h + 1) * 512]),
                 r=[y], w=[stt])
        b.op("dve", lambda e: e.bn_aggr(out=mv[:], in_=stt[:]), r=[stt], w=[mv])
        b.op("act", lambda e: e.activation(out=rstd[:], in_=mv[:, 1:2], func=AF.Sqrt, bias=self.eps_t[:], scale=1.0),
             r=[mv, self.eps_t], w=[rstd])
        b.op("dve", lambda e: e.reciprocal(out=rstd[:], in_=rstd[:]), r=[rstd], w=[rstd])
        b.op("dve", lambda e: e.tensor_scalar(nmr[:], mv[:, 0:1], -1.0, rstd[:], op0=ALU.mult, op1=ALU.mult),
             r=[mv, rstd], w=[nmr])
        b.op("act", lambda e: e.activation(out=y[:], in_=y[:], func=AF.Identity, scale=rstd[:], bias=nmr[:]),
             r=[y, rstd, nmr], w=[y])
        b.op("pool", lambda e: e.tensor_tensor(out=y[:], in0=y[:], in1=g_bc[:], op=ALU.mult), r=[y, g_bc], w=[y])
        b.op("pool", lambda e: e.tensor_tensor(out=xo, in0=y[:], in1=b_bc[:], op=ALU.add), r=[y, b_bc], w=[self._xo_tl])

    def ffn(self, l, f, first=False, skip_ctx=False):
        b = self.b
        i = 0 if f == 0 else 2
        HF = FC // 2
        HW = HF * 128
        with ExitStack() as st:
            gate, g_bc, b_bc = self.load_bcast(st, l, i, 0.5)
            wg = b.sb(st, "wg", [128, KC, HW], BF16)
            wu = b.sb(st, "wu", [128, KC, HW], BF16)
            wd = b.sb(st, "wd", [128, HF, D], BF16)
            stg = [b.sb(st, "stg%d" % k, [128, HW], F32) for k in range(3)]
            uT2 = [b.sb(st, "uT%d" % k, [128, KC, 512], BF16) for k in range(2)]
            hT = b.sb(st, "hT", [128, HF, 512], BF16)
            xin = [b.sb(st, "xin%d" % k, [128, 4, D], F32) for k in range(2)]
            dpt = [b.sb(st, "dpt%d" % k, [128, D], F32) for k in range(2)]
            ybuf = [b.sb(st, "ybuf%d" % k, [128, D], F32) for k in range(2)]
            sg = [b.sb(st, "sg%d" % k, [128, 512], F32) for k in range(2)]
            stt = b.sb(st, "stt", [128, 12], F32)
            mv = b.sb(st, "mv", [128, 2], F32)
            rstd = b.sb(st, "rstd", [128, 1], F32)
            nmr = b.sb(st, "nmr", [128, 1], F32)
            pT = [b.ps(st, "pT%d" % k, [128, 512]) for k in range(2)]
            pg = [b.ps(st, "pg%d" % k, [128, 512]) for k in range(2)]
            pu = [b.ps(st, "pu%d" % k, [128, 512]) for k in range(2)]
            pdh = [b.ps(st, "pd%d" % k, [128, 512]) for k in range(2)]
            cast_engs = ["act", "pool", "dve"]
            ci = 0
            for ps_ in range(2):
                if ps_ == 1:
                    b.barrier()
                c0 = ps_ * HW
                for (wt, src) in ((wg, self.w_gate), (wu, self.w_up)):
                    for kc in range(KC):
                        s_ = stg[ci % 3]
                        b.dma("sp", s_, s_[:], src, src.t[l, f, kc * 128:(kc + 1) * 128, c0:c0 + HW])
                        eng = cast_engs[ci % 3]
                        ci += 1
                        if eng == "act":
                            b.op("act", lambda e, wt=wt, kc=kc, s_=s_: e.copy(out=wt[:, kc, :], in_=s_[:]), r=[s_], w=[wt])
                        else:
                            b.op(eng, lambda e, wt=wt, kc=kc, s_=s_: e.tensor_copy(out=wt[:, kc, :], in_=s_[:]), r=[s_], w=[wt])
                for fc in range(HF):
                    s_ = stg[ci % 3]
                    r0 = (ps_ * HF + fc) * 128
                    b.dma("sp", s_, s_[:, 0:D], self.w_down, self.w_down.t[l, f, r0:r0 + 128, :])
                    eng = cast_engs[ci % 3]
                    ci += 1
                    if eng == "act":
                        b.op("act", lambda e, fc=fc, s_=s_: e.copy(out=wd[:, fc, :], in_=s_[:, 0:D]), r=[s_], w=[wd])
                    else:
                        b.op(eng, lambda e, fc=fc, s_=s_: e.tensor_copy(out=wd[:, fc, :], in_=s_[:, 0:D]), r=[s_], w=[wd])
                blks = [(bi, t0, ntok, v) for bi, (t0, ntok, v) in enumerate(self.blocks) if not (v == 1 and skip_ctx)]

                def load_blk(k):
                    bi_, t0_, ntok_, v_ = blks[k]
                    stl, sap = self.src_rows(first, bi_, t0_, ntok_)
                    b.dma("sp", xin[bi_ % 2], xin[bi_ % 2][:, 0:ntok_ // 128, :], stl, sap.rearrange("(s p) d -> p s d", p=128))

                def T_(k, kcs=None):
                    bi, t0, ntok, v = blks[k]
                    self.transpose_mod(xin[bi % 2], ntok // 128, pT, uT2[k % 2], l, i, v, kcs=kcs)

                def GU_(k):
                    bi, t0, ntok, v = blks[k]
                    uT = uT2[k % 2]
                    for fc in range(HF):
                        g_, u_, s_ = pg[fc % 2], pu[fc % 2], sg[fc % 2]
                        for kc in range(KC):
                            b.op("pe", lambda e, kc=kc, fc=fc: e.matmul(
                                g_[:, 0:ntok], lhsT=wg[:, kc, fc * 128:(fc + 1) * 128], rhs=uT[:, kc, 0:ntok],
                                start=(kc == 0), stop=(kc == KC - 1)), r=[wg, uT], w=[g_])
                        for kc in range(KC):
                            b.op("pe", lambda e, kc=kc, fc=fc: e.matmul(
                                u_[:, 0:ntok], lhsT=wu[:, kc, fc * 128:(fc + 1) * 128], rhs=uT[:, kc, 0:ntok],
                                start=(kc == 0), stop=(kc == KC - 1)), r=[wu, uT], w=[u_])
                        b.op("act", lambda e: e.activation(out=s_[:, 0:ntok], in_=g_[:, 0:ntok], func=AF.Silu),
                             r=[g_], w=[s_])
                        b.op("dve", lambda e, fc=fc: e.tensor_tensor(out=hT[:, fc, 0:ntok], in0=u_[:, 0:ntok],
                                                                     in1=s_[:, 0:ntok], op=ALU.mult),
                             r=[u_, s_], w=[hT])

                def DN_(k):
                    bi, t0, ntok, v = blks[k]
                    nsub = ntok // 128
                    xi = xin[bi % 2]
                    tq = list(range(KC))
                    for s in range(nsub):
                        dp = dpt[s % 2]
                        y = ybuf[s % 2]
                        r0 = t0 + s * 128
                        if ps_ == 1:
                            b.dma("sp", dp, dp[:], self.DPb[bi], self.DP.t[r0:r0 + 128, :])
                        for h in range(2):
                            pd_ = pdh[h]
                            hs = slice(h * 512, (h + 1) * 512)
                            for fc in range(HF):
                                b.op("pe", lambda e, fc=fc, h=h, s=s: e.matmul(
                                    pd_[:], lhsT=hT[:, fc, s * 128:(s + 1) * 128],
                                    rhs=wd[:, fc, h * 512:(h + 1) * 512], start=(fc == 0), stop=(fc == HF - 1)),
                                    r=[hT, wd], w=[pd_])
                            if ps_ == 0:
                                b.op("act", lambda e: e.copy(out=dp[:, hs], in_=pd_[:]), r=[pd_], w=[dp])
                            else:
                                b.op("dve", lambda e: e.tensor_tensor(out=y[:, hs], in0=pd_[:], in1=dp[:, hs], op=ALU.add),
                                     r=[pd_, dp], w=[y])
                            if k + 1 < len(blks) and tq:
                                T_(k + 1, kcs=[tq.pop(0)])
                        if ps_ == 0:
                            b.dma("sp", self.DPb[bi], self.DP.t[r0:r0 + 128, :], dp, dp[:])
                        else:
                            b.op("pool", lambda e: e.tensor_tensor(out=y[:], in0=y[:], in1=gate[v][:], op=ALU.mult),
                                 r=[y, gate[v]], w=[y])
                            b.op("dve", lambda e, s=s: e.scalar_tensor_tensor(
                                out=y[:], in0=xi[:, s, :], scalar=ALPHA, in1=y[:], op0=ALU.mult, op1=ALU.add),
                                r=[xi, y], w=[y])
                            self._xo_tl = xi
                            self.layer_norm_store(y, xi[:, s, :], g_bc, b_bc, stt, mv, rstd, nmr)
                    if k + 1 < len(blks) and tq:
                        T_(k + 1, kcs=tq)
                    if ps_ == 1:
                        b.dma("sp", self.Xb[bi], self.X.t[t0:t0 + ntok, :].rearrange("(s p) d -> p s d", p=128),
                              xi, xi[:, 0:nsub, :])

                nb_ = len(blks)
                load_blk(0)
                if nb_ > 1:
                    load_blk(1)
                T_(0)
                GU_(0)
                for k_ in range(nb_):
                    DN_(k_)
                    if k_ + 2 < nb_:
                        load_blk(k_ + 2)
                    if k_ + 1 < nb_:
                        GU_(k_ + 1)
            b.barrier()
            b.release([gate[0], gate[1], g_bc, b_bc, wg, wu, wd, hT] + uT2 + stg + xin + dpt + ybuf + sg)

    def declare_mixer_io(self):
        b, L, S, SC, T = self.b, self.depth, self.S, self.SC, self.T
        ein = lambda name, shape: dram_tl(b, name, shape, F32, "ExternalInput")
        self.w_in = ein("w_in", [L, D, INC])
        self.w_out = ein("w_out", [L, D, D])
        self.conv_w = ein("conv_w", [L, 5, 768])
        self.conv_b = ein("conv_b", [L, 768])
        self.att_lambda = ein("att_lambda", [L, 4, 64])
        self.att_subln_g = ein("att_subln_g", [L, 128])
        self.ssm_a_log = ein("ssm_a_log", [L, 8])
        self.ssm_dt_bias = ein("ssm_dt_bias", [L, 8])
        self.ssm_d = ein("ssm_d", [L, 8])
        self.ssm_norm_g = ein("ssm_norm_g", [L, 256])
        self.ret_log_gamma = ein("ret_log_gamma", [L, 8])
        nb = S // 512
        self.rot_tab = ein("rot_tab", [nb, 128, 6, 512])
        self.cmats = ein("cmats", [128, 8, 128])
        sc = lambda name, shape, dt: dram_tl(b, name, shape, dt)
        self.QT = sc("QT_scr", [4, 128, T], BF16)
        self.KT = sc("KT_scr", [4, 128, T], BF16)
        self.V1 = sc("V1_scr", [T, 512], BF16)
        self.Z = sc("Z_scr", [T, 256], F32)
        self.RG = sc("RG_scr", [T, 256], F32)
        self.DT = sc("DT_scr", [T, 4], F32)
        self.RV = sc("RV_scr", [T, 256], BF16)
        self.XBCT = sc("XBCT_scr", [6, 128, T], F32)
        self.XST = sc("XST_scr", [2, 128, T], F32)
        self.BT = sc("BT_scr", [2, 128, T], BF16)
        self.CT = sc("CT_scr", [2, 128, T], BF16)
        self.RQT = sc("RQT_scr", [4, 64, T], BF16)
        self.RKT = sc("RKT_scr", [4, 64, T], BF16)
        self.RKK = sc("RKK_scr", [T, 256], BF16)
        self.YS = sc("YS_scr", [T, 256], F32)
        self.YR = sc("YR_scr", [T, 256], F32)
        self.MGT = sc("MGT_scr", [D, T], BF16)

    def load_consts(self, st):
        b = self.b
        self.cm = b.sb(st, "cmats", [128, 8, 128], F32)
        b.dma("sp", self.cm, self.cm[:], self.cmats, self.cmats.t)
        self.identb = b.sb(st, "identb", [128, 128], BF16)
        b.op("dve", lambda e: e.tensor_copy(out=self.identb[:], in_=self.ident[:]), r=[self.ident], w=[self.identb])
        self.ones_bf = b.sb(st, "ones_bf", [128, 1], BF16)
        b.op("pool", lambda e: e.memset(self.ones_bf[:], 1.0), w=[self.ones_bf])
        self.one_t = b.sb(st, "one_t", [128, 1], F32)
        b.op("pool", lambda e: e.memset(self.one_t[:], 1.0), w=[self.one_t])
        self.eps6 = b.sb(st, "eps6", [128, 1], F32)
        b.op("pool", lambda e: e.memset(self.eps6[:], 1e-6), w=[self.eps6])
        self.mask4 = []
        for d in range(2):
            m4 = b.sb(st, "mask4_%d" % d, [128, 4, 128], F32)
            for h in range(4):
                b.op("pool", lambda e, h=h: e.tensor_copy(out=m4[:, h, :], in_=self.cm[:, 5 + d, :]), r=[self.cm], w=[m4])
            self.mask4.append(m4)

    PERM_A, PERM_R, TRI_F, NSTRICT_B, ONES = 0, 1, 2, 3, 4

    def cast_to(self, eng, out_ap, in_ap, r, w):
        b = self.b
        if eng == "act":
            b.op("act", lambda e: e.copy(out=out_ap, in_=in_ap), r=r, w=w)
        else:
            b.op(eng, lambda e: e.tensor_copy(out=out_ap, in_=in_ap), r=r, w=w)

    def mixer_inproj(self, l):
        b, S, T = self.b, self.S, self.T
        with ExitStack() as st:
            win = b.sb(st, "win", [128, KC, INC], BF16)
            wstg = [b.sb(st, "wstg%d" % k, [128, INC], F32) for k in range(2)]
            engs = ["act", "pool", "dve"]
            for kc in range(KC):
                s_ = wstg[kc % 2]
                b.dma("sp", s_, s_[:], self.w_in, self.w_in.t[l, kc * 128:(kc + 1) * 128, :])
                self.cast_to(engs[kc % 3], win[:, kc, :], s_[:], [s_], [win])
            uT = b.sb(st, "uT", [128, KC, 512], BF16)
            xin = [b.sb(st, "xin%d" % k, [128, 4, D], F32) for k in range(2)]
            tab = [b.sb(st, "tab%d" % k, [128, 6, 512], F32) for k in range(2)]
            qs = [b.sb(st, "qs%d" % k, [128, 512], F32) for k in range(2)]
            t1 = b.sb(st, "t1", [128, 512], F32)
            t2 = b.sb(st, "t2", [128, 512], F32)
            ob = [b.sb(st, "ob%d" % k, [128, 512], BF16) for k in range(3)]
            xb = [b.sb(st, "xb%d" % k, [128, 512], F32) for k in range(2)]
            rkk = b.sb(st, "rkk", [128, 4, 256], BF16)
            vst = b.sb(st, "vst", [128, 4, 512], BF16)
            zst = b.sb(st, "zst", [128, 4, 256], F32)
            rgst = b.sb(st, "rgst", [128, 4, 256], F32)
            rvst = b.sb(st, "rvst", [128, 4, 256], BF16)
            dst = b.sb(st, "dst", [128, 4, 4], F32)
            pT = [b.ps(st, "pT", [128, 512])]
            pf = [b.ps(st, "pf%d" % k, [128, 512]) for k in range(2)]
            pr = b.ps(st, "pr", [128, 512])
            ptr = b.ps(st, "ptr", [128, 512], BF16)
            pav = b.ps(st, "pav", [128, 512])
            pz = b.ps(st, "pz", [128, 512])
            prr = b.ps(st, "prr", [128, 512])
            nf = 0
            oi = 0
            def load_blk(bi_):
                t0_, ntok_, v_ = self.blocks[bi_]
                b.dma("sp", xin[bi_ % 2], xin[bi_ % 2][:, 0:ntok_ // 128, :], self.Xb[bi_],
                      self.X.t[t0_:t0_ + ntok_, :].rearrange("(s p) d -> p s d", p=128))
                if v_ == 0:
                    b.dma("sp", tab[bi_ % 2], tab[bi_ % 2][:], self.rot_tab, self.rot_tab.t[bi_])

            load_blk(0)
            for bi, (t0, ntok, v) in enumerate(self.blocks):
                if bi + 1 < len(self.blocks):
                    load_blk(bi + 1)
                nsub = ntok // 128
                xi = xin[bi % 2]
                tb = tab[bi % 2]
                self.transpose_mod(xi, nsub, pT, uT, l, 1, v)
                chunks = []
                for c in range(4):
                    chunks.append(("aq", c, c * 128))
                for c in range(4):
                    chunks.append(("ak", c, 512 + c * 128))
                for c in range(2):
                    chunks.append(("rq", c, 2564 + c * 128))
                for c in range(2):
                    chunks.append(("rk", c, 2820 + c * 128))
                for c in range(6):
                    chunks.append(("xbc", c, 1792 + c * 128))
                for (kind, c, col0) in chunks:
                    p_ = pf[nf % 2]
                    nf += 1
                    for kc in range(KC):
                        b.op("pe", lambda e, kc=kc: e.matmul(p_[:, 0:ntok], lhsT=win[:, kc, col0:col0 + 128],
                                                             rhs=uT[:, kc, 0:ntok], start=(kc == 0), stop=(kc == KC - 1)),
                             r=[win, uT], w=[p_])
                    if kind == "xbc":
                        x_ = xb[c % 2]
                        b.op("act", lambda e: e.copy(out=x_[:, 0:ntok], in_=p_[:, 0:ntok]), r=[p_], w=[x_])
                        b.dma("sp", self.XBCT, self.XBCT.t[c, :, t0:t0 + ntok], x_, x_[:, 0:ntok])
                        continue
                    o_ = ob[oi % 3]
                    oi += 1
                    if v == 1:
                        if kind == "rk":
                            b.op("act", lambda e: e.mul(o_[:, 0:ntok], p_[:, 0:ntok], 0.125), r=[p_], w=[o_])
                        else:
                            b.op("act", lambda e: e.copy(out=o_[:, 0:ntok], in_=p_[:, 0:ntok]), r=[p_], w=[o_])
                    else:
                        q_ = qs[oi % 2]
                        ti = {"aq": 0, "ak": 0, "rq": 2, "rk": 4}[kind]
                        pm = self.PERM_A if kind in ("aq", "ak") else self.PERM_R
                        b.op("act", lambda e: e.copy(out=q_[:], in_=p_[:]), r=[p_], w=[q_])
                        b.op("pe", lambda e: e.matmul(pr[:], lhsT=self.cm[:, pm, :], rhs=q_[:], start=True, stop=True),
                             r=[self.cm, q_], w=[pr])
                        b.op("pool", lambda e: e.tensor_tensor(out=t1[:], in0=q_[:], in1=tb[:, ti, :], op=ALU.mult),
                             r=[q_, tb], w=[t1])
                        b.op("dve", lambda e: e.tensor_tensor(out=t2[:], in0=pr[:], in1=tb[:, ti + 1, :], op=ALU.mult),
                             r=[pr, tb], w=[t2])
                        b.op("dve", lambda e: e.tensor_tensor(out=o_[:], in0=t1[:], in1=t2[:], op=ALU.add),
                             r=[t1, t2], w=[o_])
                    if kind == "aq":
                        b.dma("sp", self.QT, self.QT.t[c, :, t0:t0 + ntok], o_, o_[:, 0:ntok])
                    elif kind == "ak":
                        b.dma("sp", self.KT, self.KT.t[c, :, t0:t0 + ntok], o_, o_[:, 0:ntok])
                    elif kind == "rq":
                        for hh in range(2):
                            b.dma("sp", self.RQT, self.RQT.t[2 * c + hh, :, t0:t0 + ntok], o_, o_[hh * 64:(hh + 1) * 64, 0:ntok])
                    else:
                        for hh in range(2):
                            b.dma("sp", self.RKT, self.RKT.t[2 * c + hh, :, t0:t0 + ntok], o_, o_[hh * 64:(hh + 1) * 64, 0:ntok])
                        for s in range(nsub):
                            b.op("pe", lambda e, s=s: e.transpose(ptr[:, s * 128:(s + 1) * 128], o_[:, s * 128:(s + 1) * 128],
                                                                 self.identb[:]), r=[o_, self.identb], w=[ptr])
                        b.op("dve", lambda e: e.tensor_copy(
                            out=rkk[:, 0:nsub, c * 128:(c + 1) * 128],
                            in_=ptr[:, 0:ntok].rearrange("p (s c) -> p s c", c=128)), r=[ptr], w=[rkk])
                for s in range(nsub):
                    lt = lambda kc: uT[:, kc, s * 128:(s + 1) * 128]
                    for kc in range(KC):
                        b.op("pe", lambda e, kc=kc: e.matmul(pav[:], lhsT=lt(kc), rhs=win[:, kc, 1024:1536],
                                                             start=(kc == 0), stop=(kc == KC - 1)), r=[win, uT], w=[pav])
                    b.op("act", lambda e, s=s: e.copy(out=vst[:, s, :], in_=pav[:]), r=[pav], w=[vst])
                    for kc in range(KC):
                        b.op("pe", lambda e, kc=kc: e.matmul(pz[:, 0:256], lhsT=lt(kc), rhs=win[:, kc, 1536:1792],
                                                             start=(kc == 0), stop=(kc == KC - 1)), r=[win, uT], w=[pz])
                    for kc in range(KC):
                        b.op("pe", lambda e, kc=kc: e.matmul(pz[:, 256:260], lhsT=lt(kc), rhs=win[:, kc, 2560:2564],
                                                             start=(kc == 0), stop=(kc == KC - 1)), r=[win, uT], w=[pz])
                    b.op("act", lambda e, s=s: e.activation(out=zst[:, s, :], in_=pz[:, 0:256], func=AF.Silu), r=[pz], w=[zst])
                    b.op("dve", lambda e, s=s: e.tensor_copy(out=dst[:, s, :], in_=pz[:, 256:260]), r=[pz], w=[dst])
                    for kc in range(KC):
                        b.op("pe", lambda e, kc=kc: e.matmul(prr[:], lhsT=lt(kc), rhs=win[:, kc, 3076:3588],
                                                             start=(kc == 0), stop=(kc == KC - 1)), r=[win, uT], w=[prr])
                    b.op("act", lambda e, s=s: e.copy(out=rvst[:, s, :], in_=prr[:, 0:256]), r=[prr], w=[rvst])
                    b.op("act", lambda e, s=s: e.activation(out=rgst[:, s, :], in_=prr[:, 256:512], func=AF.Silu), r=[prr], w=[rgst])
                rows = lambda tl_: tl_.t[t0:t0 + ntok, :].rearrange("(s p) c -> p s c", p=128)
                b.dma("sp", self.V1, rows(self.V1), vst, vst[:, 0:nsub, :])
                b.dma("sp", self.Z, rows(self.Z), zst, zst[:, 0:nsub, :])
                b.dma("sp", self.RG, rows(self.RG), rgst, rgst[:, 0:nsub, :])
                b.dma("sp", self.RV, rows(self.RV), rvst, rvst[:, 0:nsub, :])
                b.dma("sp", self.DT, rows(self.DT), dst, dst[:, 0:nsub, :])
                b.dma("sp", self.RKK, rows(self.RKK), rkk, rkk[:, 0:nsub, :])
            b.barrier()
            b.release([win, uT, rkk, vst, zst, rgst, rvst, dst] + wstg + xin + tab + qs + ob + xb)

    def load_cols(self, st, name, src_tl, src_ap, nrow, ncol, pst):
        b = self.b
        nr = nrow + (nrow % 2)
        rowt = b.sb(st, name + "_row", [nr, ncol * 128], F32)
        colt = b.sb(st, name + "_col", [128, ncol, nr], F32)
        b.op("pool", lambda e: e.memset(rowt[:], 0.0), w=[rowt])
        b.dma("sp", rowt, rowt[0:nrow, :], src_tl, src_ap)
        for c in range(ncol):
            b.op("pe", lambda e, c=c: e.transpose(pst[:, c * nr:(c + 1) * nr], rowt[0:nr, c * 128:(c + 1) * 128],
                                                 self.ident[0:nr, 0:nr]), r=[rowt, self.ident], w=[pst])
        b.op("dve", lambda e: e.tensor_copy(out=colt[:], in_=pst[:, 0:ncol * nr].rearrange("p (c k) -> p c k", k=nr)),
             r=[pst], w=[colt])
        return colt, rowt

    def mixer_conv(self, l):
        b, S, SC = self.b, self.S, self.SC
        with ExitStack() as st:
            pst = b.ps(st, "pst", [128, 512])
            cw, r1 = self.load_cols(st, "cw", self.conv_w, self.conv_w.t[l], 5, 6, pst)
            cb, r2 = self.load_cols(st, "cb", self.conv_b, self.conv_b.t[l:l + 1, :], 1, 6, pst)
            for (t0, n) in ((0, S), (S, SC)):
                with ExitStack() as st2:
                    pre = b.sb(st2, "pre", [128, 6, n + 4], F32)
                    acc = [b.sb(st2, "acc%d" % k, [128, n], F32) for k in range(2)]
                    of = [b.sb(st2, "of%d" % k, [128, n], F32) for k in range(2)]
                    obf = [b.sb(st2, "obf%d" % k, [128, n], BF16) for k in range(2)]
                    b.op("pool", lambda e: e.memset(pre[:, :, 0:2], 0.0), w=[pre])
                    b.op("pool", lambda e: e.memset(pre[:, :, n + 2:n + 4], 0.0), w=[pre])
                    for c in range(6):
                        b.dma("sp", pre, pre[:, c, 2:n + 2], self.XBCT, self.XBCT.t[c, :, t0:t0 + n])
                    for c in range(6):
                        a_ = acc[c % 2]
                        b.op("dve", lambda e: e.tensor_scalar(a_[:], pre[:, c, 0:n], cw[:, c, 0:1], None, op0=ALU.mult),
                             r=[pre, cw], w=[a_])
                        for k in range(1, 5):
                            b.op("dve", lambda e, k=k: e.scalar_tensor_tensor(
                                out=a_[:], in0=pre[:, c, k:k + n], scalar=cw[:, c, k:k + 1], in1=a_[:], op0=ALU.mult,
                                op1=ALU.add), r=[pre, cw, a_], w=[a_])
                        if c < 2:
                            o_ = of[c % 2]
                            b.op("act", lambda e: e.activation(out=o_[:], in_=a_[:], func=AF.Silu, bias=cb[:, c, 0:1]),
                                 r=[a_, cb], w=[o_])
                            b.dma("sp", self.XST, self.XST.t[c, :, t0:t0 + n], o_, o_[:])
                        else:
                            o_ = obf[c % 2]
                            b.op("act", lambda e: e.activation(out=o_[:], in_=a_[:], func=AF.Silu, bias=cb[:, c, 0:1]),
                                 r=[a_, cb], w=[o_])
                            dstt = self.BT if c < 4 else self.CT
                            b.dma("sp", dstt, dstt.t[c % 2, :, t0:t0 + n], o_, o_[:])
                    b.barrier()
                    b.release([pre] + acc + of + obf)
            b.barrier()
            b.release([r1, r2])

    def mixer_attention(self, l):
        b, S, SC, T = self.b, self.S, self.SC, self.T
        need_ctx = l < self.depth - 1
        lam_init = 0.8 - 0.6 * math.exp(-0.3 * l)
        NKT = T // 128
        with ExitStack() as st:
            kt_sb = b.sb(st, "kt_sb", [128, 4, 2, T], BF16)
            v_sb = b.sb(st, "v_sb", [128, NKT, 512], BF16)
            for h in range(4):
                for m in range(2):
                    b.op("dve" if (h + m) % 2 == 0 else "pool", lambda e, h=h, m=m: e.memset(kt_sb[:, h, m, :], 0.0), w=[kt_sb])
            for h in range(4):
                for m in range(2):
                    b.dma("sp", kt_sb, kt_sb[m * 64:(m + 1) * 64, h, m, :], self.KT, self.KT.t[h, m * 64:(m + 1) * 64, :])
            b.dma("sp", v_sb, v_sb[:], self.V1, self.V1.t.rearrange("(k p) c -> p k c", p=128))
            lv = b.sb(st, "lv", [128, 4, 64], F32)
            lp = b.sb(st, "lp", [128, 2, 64], F32)
            ls = b.sb(st, "ls", [128, 2], F32)
            neglam = b.sb(st, "neglam", [128, 1], F32)
            gsub = b.sb(st, "gsub", [128, 1], F32)
            b.dma("sp", lv, lv[:], self.att_lambda, self.att_lambda.t[l].partition_broadcast(128))
            b.dma("sp", gsub, gsub[:], self.att_subln_g, self.att_subln_g.t[l].rearrange("(p o) -> p o", o=1))
            b.op("dve", lambda e: e.tensor_scalar(gsub[:], gsub[:], 1.0 - lam_init, None, op0=ALU.mult), r=[gsub], w=[gsub])
            for k in range(2):
                b.op("dve", lambda e, k=k: e.tensor_tensor(out=lp[:, k, :], in0=lv[:, 2 * k, :], in1=lv[:, 2 * k + 1, :],
                                                          op=ALU.mult), r=[lv], w=[lp])
                b.op("dve", lambda e, k=k: e.reduce_sum(out=ls[:, k:k + 1], in_=lp[:, k, :], axis=mybir.AxisListType.X),
                     r=[lp], w=[ls])
            b.op("act", lambda e: e.activation(out=ls[:], in_=ls[:], func=AF.Exp), r=[ls], w=[ls])
            b.op("dve", lambda e: e.tensor_tensor(out=neglam[:], in0=ls[:, 1:2], in1=ls[:, 0:1], op=ALU.subtract),
                 r=[ls], w=[neglam])
            b.op("dve", lambda e: e.tensor_scalar_add(neglam[:], neglam[:], -lam_init), r=[neglam], w=[neglam])
            qt = [b.sb(st, "qt%d" % k, [128, 4, 512], BF16) for k in range(2)]
            pb = [b.sb(st, "pb%d" % k, [128, 2, 512], BF16) for k in range(3)]
            ones128 = b.sb(st, "ones128", [128, 128], BF16)
            b.op("pool", lambda e: e.memset(ones128[:], 1.0), w=[ones128])
            os_ = [b.sb(st, "os%d" % k, [128, 512], F32) for k in range(2)]
            rl = [b.sb(st, "rl%d" % k, [128, 512], F32) for k in range(2)]
            tt = b.sb(st, "tt", [128, 512], F32)
            oo = b.sb(st, "oo", [128, 512], F32)
            sq = b.sb(st, "sq", [128, 512], F32)
            rs = b.sb(st, "rs", [128, 512], F32)
            mgo = [b.sb(st, "mgo%d" % k, [128, 512], BF16) for k in range(2)]
            psc = [b.ps(st, "psc%d" % k, [128, 2, 512]) for k in range(2)]
            po = [b.ps(st, "po%d" % k, [128, 512]) for k in range(2)]
            pl = [b.ps(st, "pl%d" % k, [128, 512]) for k in range(2)]
            pss = psc[0]
            ONES = self.cm[:, self.ONES, :]
            qblocks = [(i * 512, 512, list(range(NKT))) for i in range(S // 512)]
            if need_ctx:
                qblocks.append((S, SC, list(range(S // 128, NKT))))
            def load_q(qi_):
                q0_, nq_, _ = qblocks[qi_]
                b.dma("sp", qt[qi_ % 2], qt[qi_ % 2][:, :, 0:nq_], self.QT, self.QT.t[:, :, q0_:q0_ + nq_].rearrange("h p t -> p h t"))

            load_q(0)
            for qi, (q0, nq, kts) in enumerate(qblocks):
                q_ = qt[qi % 2]
                if qi + 1 < len(qblocks):
                    load_q(qi + 1)
                npair = len(kts) // 2
                items = [(h, m, pi) for h in range(4) for m in range(2) for pi in range(npair)]

                def qk(j):
                    h, m, pi = items[j]
                    sc_, p_ = psc[j % 2], pb[j % 3]
                    for a in range(2):
                        kt = kts[2 * pi + a]
                        b.op("pe", lambda e, a=a, kt=kt: e.matmul(sc_[:, a, 0:nq], lhsT=kt_sb[:, h, m, kt * 128:(kt + 1) * 128],
                                                                 rhs=q_[:, h, 0:nq], start=True, stop=True), r=[kt_sb, q_], w=[sc_])
                    b.op("act", lambda e: e.activation(out=p_[:, :, 0:nq], in_=sc_[:, :, 0:nq], func=AF.Exp, scale=0.125),
                         r=[sc_], w=[p_])

                def av(j):
                    h, m, pi = items[j]
                    p_ = pb[j % 3]
                    for a in range(2):
                        kt = kts[2 * pi + a]
                        first_, last_ = (pi == 0 and a == 0), (pi == npair - 1 and a == 1)
                        b.op("pe", lambda e, a=a, kt=kt: e.matmul(po[m][:, 0:nq], lhsT=v_sb[:, kt, h * 128:(h + 1) * 128],
                                                                 rhs=p_[:, a, 0:nq], start=first_, stop=last_),
                             r=[v_sb, p_], w=[po[m]])
                        b.op("pe", lambda e, a=a: e.matmul(pl[m][:, 0:nq], lhsT=ones128[:], rhs=p_[:, a, 0:nq],
                                                           start=first_, stop=last_), r=[ones128, p_], w=[pl[m]])

                def post(h):
                    for m in range(2):
                        b.op("act", lambda e, m=m: e.copy(out=os_[m][:, 0:nq], in_=po[m][:, 0:nq]), r=[po[m]], w=[os_[m]])
                        b.op("dve", lambda e, m=m: e.reciprocal(out=rl[m][:, 0:nq], in_=pl[m][:, 0:nq]), r=[pl[m]], w=[rl[m]])
                    b.op("dve", lambda e: e.tensor_tensor(out=tt[:, 0:nq], in0=os_[0][:, 0:nq], in1=rl[0][:, 0:nq], op=ALU.mult),
                         r=[os_[0], rl[0]], w=[tt])
                    b.op("dve", lambda e: e.scalar_tensor_tensor(out=oo[:, 0:nq], in0=os_[1][:, 0:nq], scalar=neglam[:, 0:1],
                                                                 in1=rl[1][:, 0:nq], op0=ALU.mult, op1=ALU.mult),
                         r=[os_[1], neglam, rl[1]], w=[oo])
                    b.op("pool", lambda e: e.tensor_tensor(out=oo[:, 0:nq], in0=oo[:, 0:nq], in1=tt[:, 0:nq], op=ALU.add),
                         r=[oo, tt], w=[oo])
                    b.op("act", lambda e: e.activation(out=sq[:, 0:nq], in_=oo[:, 0:nq], func=AF.Square), r=[oo], w=[sq])

                def post_b(h):
                    b.op("pe", lambda e: e.matmul(pss[:, 0, 0:nq], lhsT=ONES, rhs=sq[:, 0:nq], start=True, stop=True),
                         r=[self.cm, sq], w=[pss])
                    b.op("act", lambda e: e.activation(out=rs[:, 0:nq], in_=pss[:, 0, 0:nq], func=AF.Sqrt, scale=1.0 / 128,
                                                       bias=self.eps6[:]), r=[pss, self.eps6], w=[rs])
                    b.op("dve", lambda e: e.reciprocal(out=rs[:, 0:nq], in_=rs[:, 0:nq]), r=[rs], w=[rs])
                    g_ = mgo[h % 2]
                    b.op("dve", lambda e: e.scalar_tensor_tensor(out=g_[:, 0:nq], in0=oo[:, 0:nq], scalar=gsub[:, 0:1],
                                                                 in1=rs[:, 0:nq], op0=ALU.mult, op1=ALU.mult),
                         r=[oo, gsub, rs], w=[g_])
                    b.dma("sp", self.MGT, self.MGT.t[h * 128:(h + 1) * 128, q0:q0 + nq], g_, g_[:, 0:nq])

                n_it = len(items)
                qk(0)
                pend = None
                for j in range(n_it):
                    if j + 1 < n_it:
                        qk(j + 1)
                    av(j)
                    h, m, pi = items[j]
                    if pend is not None and j >= pend[1]:
                        post_b(pend[0])
                        pend = None
                    if m == 1 and pi == npair - 1:
                        post(h)
                        pend = (h, j + min(6, npair))
                if pend is not None:
                    post_b(pend[0])
            b.barrier()
            b.release([kt_sb, v_sb, lv, gsub] + qt + mgo)

    def mixer_scan(self, l, kind):
        b, S, SC, T = self.b, self.S, self.SC, self.T
        need_ctx = l < self.depth - 1
        ssd = kind == "ssd"
        n = 128 if ssd else 64
        NK = 2 if ssd else 4
        kq = (lambda h: h // 2) if ssd else (lambda h: h)
        QTd, KTd = (self.CT, self.BT) if ssd else (self.RQT, self.RKT)
        YP = self.YS if ssd else self.YR
        col_base = 512 if ssd else 768
        nlat = S // 128
        lat = [i * 128 for i in range(nlat)]
        ctxc = [S + i * 128 for i in range(SC // 128)]
        cm = self.cm
        with ExitStack() as st:
            prm = b.sb(st, "prm", [128, 3, 8], F32)
            if ssd:
                b.dma("sp", prm, prm[:, 0, :], self.ssm_a_log, self.ssm_a_log.t[l].partition_broadcast(128))
                b.dma("sp", prm, prm[:, 1, :], self.ssm_dt_bias, self.ssm_dt_bias.t[l].partition_broadcast(128))
                b.dma("sp", prm, prm[:, 2, :], self.ssm_d, self.ssm_d.t[l].partition_broadcast(128))
                negA = b.sb(st, "negA", [128, 8], F32)
                b.op("act", lambda e: e.activation(out=negA[:], in_=prm[:, 0, :], func=AF.Exp), r=[prm], w=[negA])
                b.op("dve", lambda e: e.tensor_scalar(negA[:], negA[:], -1.0, None, op0=ALU.mult), r=[negA], w=[negA])
                dsum = b.sb(st, "dsum", [128, 4], F32)
                b.op("dve", lambda e: e.tensor_tensor(out=dsum[:], in0=prm[:, 2, 0:4], in1=prm[:, 2, 4:8], op=ALU.add),
                     r=[prm], w=[dsum])
                dsum_bc = b.sb(st, "dsum_bc", [128, 4, 64], F32)
                b.op("pool", lambda e: e.memset(dsum_bc[:], 1.0), w=[dsum_bc])
                for h in range(4):
                    b.op("dve", lambda e, h=h: e.tensor_scalar(dsum_bc[:, h, :], dsum_bc[:, h, :], dsum[:, h:h + 1], None,
                                                              op0=ALU.mult), r=[dsum_bc, dsum], w=[dsum_bc])
                gn = b.sb(st, "gn", [128, 256], F32)
                b.dma("sp", gn, gn[:], self.ssm_norm_g, self.ssm_norm_g.t[l].partition_broadcast(128))
            else:
                b.dma("sp", prm, prm[:, 0, :], self.ret_log_gamma, self.ret_log_gamma.t[l].partition_broadcast(128))
            class WS:
                pass
            W = []
            for k in range(2):
                w = WS()
                sfx = "_%d" % k
                w.la = b.sb(st, "la" + sfx, [128, 4], F32)
                w.dtv = b.sb(st, "dtv" + sfx, [128, 4], F32)
                w.TL = b.sb(st, "TL" + sfx, [128, 4, 128], F32)
                w.dm = b.sb(st, "dm" + sfx, [128, 4, 128], F32)
                w.em = b.sb(st, "em" + sfx, [128, 4, 128], F32)
                w.ngam = b.sb(st, "ngam" + sfx, [128, 4], F32)
                w.totc = b.sb(st, "totc" + sfx, [128, 4], F32)
                w.dec = b.sb(st, "dec" + sfx, [128, 4, 128], F32)
                w.Ebc = b.sb(st, "Ebc" + sfx, [128, 4, 128], F32)
                w.wv = b.sb(st, "wv" + sfx, [128, 4], F32)
                w.etot = b.sb(st, "etot" + sfx, [128, 4], F32)
                w.coef = b.sb(st, "coef" + sfx, [128, 4], F32)
                w.xs = b.sb(st, "xs" + sfx, [128, 4, 64], F32)
                w.vd = b.sb(st, "vd" + sfx, [128, 4, 64], BF16)
                w.vw = b.sb(st, "vw" + sfx, [128, 4, 64], BF16)
                w.MT = b.sb(st, "MT" + sfx, [128, 4, 128], BF16)
                w.QpT = b.sb(st, "QpT" + sfx, [128, 4, 128], BF16)
                w.y2 = b.sb(st, "y2" + sfx, [128, 256], F32)
                w.y3 = b.sb(st, "y3" + sfx, [128, 256], F32)
                w.ss = b.sb(st, "ss" + sfx, [128, 4], F32)
                w.mvh = b.sb(st, "mvh" + sfx, [128, 4, 2], F32)
                w.sth = b.sb(st, "sth" + sfx, [128, 4, 6], F32)
                W.append(w)
            dtr = [b.sb(st, "dtr%d" % k, [128, 4], F32) for k in range(3)]
            qT = [b.sb(st, "qT%d" % k, [128, NK, 128], BF16) for k in range(3)]
            kT = [b.sb(st, "kT%d" % k, [128, NK, 128], BF16) for k in range(3)]
            ktm = [b.sb(st, "ktm%d" % k, [128, 256], BF16) for k in range(3)]
            xsT = [b.sb(st, "xsT%d" % k, [128, 2, 128], F32) for k in range(3)]
            vtm = [b.sb(st, "vtm%d" % k, [128, 4, 64], BF16) for k in range(3)]
            Sf = b.sb(st, "Sf", [128, 4, 64], F32)
            Sb = b.sb(st, "Sb", [128, 4, 64], BF16)
            ysb = [b.sb(st, "ysb%d" % k, [128, 256], F32) for k in range(3)]
            zt = [b.sb(st, "zt%d" % k, [128, 256], F32) for k in range(3)]
            mgo = [b.sb(st, "mgo%d" % k, [128, 2, 128], BF16) for k in range(2)]
            pc = b.ps(st, "pc", [128, 8])
            pG = b.ps(st, "pG", [128, 512])
            pGT = b.ps(st, "pGT", [128, 512])
            py = b.ps(st, "py", [128, 256])
            pS = b.ps(st, "pS", [128, 256])
            pX = b.ps(st, "pX", [128, 256])
            pK = b.ps(st, "pK", [128, 256], BF16)
            pM = b.ps(st, "pM", [128, 256])
            bc = lambda ap, shape: ap.broadcast_to(shape)
            fl = lambda t_: t_[:].rearrange("p h l -> p (h l)")

            def decay_quants(d, w):
                tri = self.TRI_F if d == 0 else self.NSTRICT_B
                la = w.la
                b.op("pe", lambda e: e.matmul(pc[:, 0:4], lhsT=cm[:, tri, :], rhs=la[:], start=True, stop=True),
                     r=[cm, la], w=[pc])
                b.op("pe", lambda e: e.matmul(pc[:, 4:8], lhsT=cm[:, self.ONES, :], rhs=la[:], start=True, stop=True),
                     r=[cm, la], w=[pc])
                b.op("dve", lambda e: e.tensor_tensor(out=w.TL[:], in0=bc(cm[:, tri, :].unsqueeze(1), [128, 4, 128]),
                                                      in1=bc(la[:].unsqueeze(2), [128, 4, 128]), op=ALU.mult),
                     r=[cm, la], w=[w.TL])
                b.op("pe", lambda e: e.matmul(pG[:], lhsT=cm[:, self.ONES, :], rhs=fl(w.TL), start=True, stop=True),
                     r=[cm, w.TL], w=[pG])
                b.op("dve", lambda e: e.tensor_scalar(w.ngam[:], pc[:, 0:4], -1.0, None, op0=ALU.mult), r=[pc], w=[w.ngam])
                b.op("dve", lambda e: e.tensor_copy(out=w.totc[:], in_=pc[:, 4:8]), r=[pc], w=[w.totc])
                b.op("dve", lambda e: e.tensor_tensor(out=fl(w.dm), in0=pG[:], in1=fl(self.mask4[d]), op=ALU.add),
                     r=[pG, self.mask4[d]], w=[w.dm])
                b.op("dve", lambda e: e.tensor_tensor(out=w.dm[:], in0=w.dm[:], in1=bc(w.ngam[:].unsqueeze(2), [128, 4, 128]),
                                                      op=ALU.add), r=[w.dm, w.ngam], w=[w.dm])
                b.op("act", lambda e: e.activation(out=fl(w.dec), in_=fl(w.dm), func=AF.Exp), r=[w.dm], w=[w.dec])
                if d == 0:
                    b.op("act", lambda e: e.activation(out=fl(w.Ebc), in_=pG[:], func=AF.Exp), r=[pG], w=[w.Ebc])
                    b.op("dve", lambda e: e.tensor_tensor(out=w.wv[:], in0=w.totc[:], in1=w.ngam[:], op=ALU.add),
                         r=[w.totc, w.ngam], w=[w.wv])
                    b.op("act", lambda e: e.activation(out=w.wv[:], in_=w.wv[:], func=AF.Exp), r=[w.wv], w=[w.wv])
                else:
                    b.op("dve", lambda e: e.tensor_tensor(out=w.em[:], in0=pG[:].rearrange("p (h l) -> p h l", l=128),
                                                          in1=bc(w.totc[:].unsqueeze(2), [128, 4, 128]), op=ALU.add),
                         r=[pG, w.totc], w=[w.em])
                    b.op("act", lambda e: e.activation(out=fl(w.Ebc), in_=fl(w.em), func=AF.Exp), r=[w.em], w=[w.Ebc])
                    b.op("act", lambda e: e.activation(out=w.wv[:], in_=w.ngam[:], func=AF.Exp), r=[w.ngam], w=[w.wv])
                b.op("act", lambda e: e.activation(out=w.etot[:], in_=w.totc[:], func=AF.Exp), r=[w.totc], w=[w.etot])

            for d in range(2):
                if d == 1:
                    b.barrier()
                order = (ctxc + lat) if d == 0 else (ctxc[::-1] + lat[::-1])
                b.op("pool", lambda e: e.memset(Sf[:], 0.0), w=[Sf])
                b.op("pool", lambda e: e.memset(Sb[:], 0.0), w=[Sb])
                if not ssd:
                    wc = W[0]
                    b.op("dve", lambda e: e.tensor_copy(out=wc.la[:], in_=prm[:, 0, d * 4:(d + 1) * 4]), r=[prm], w=[wc.la])
                    decay_quants(d, wc)
                def stage_l(ci, tc):
                    is_ctx = tc >= S
                    want_y = (not is_ctx) or need_ctx
                    q_, k_, km_, v_ = qT[ci % 3], kT[ci % 3], ktm[ci % 3], vtm[ci % 3]
                    x_, dr, yl, z_ = xsT[ci % 3], dtr[ci % 3], ysb[ci % 3], zt[ci % 3]
                    if ssd:
                        b.dma("sp", q_, q_[:], QTd, QTd.t[:, :, tc:tc + 128].rearrange("g p t -> p g t"))
                        b.dma("sp", k_, k_[:], KTd, KTd.t[:, :, tc:tc + 128].rearrange("g p t -> p g t"))
                        b.dma("sp", x_, x_[:], self.XST, self.XST.t[:, :, tc:tc + 128].rearrange("g p t -> p g t"))
                        b.dma("sp", dr, dr[:], self.DT, self.DT.t[tc:tc + 128, :])
                    else:
                        b.dma("sp", q_, q_[0:64, :, :], QTd, QTd.t[:, :, tc:tc + 128].rearrange("g p t -> p g t"))
                        b.dma("sp", k_, k_[0:64, :, :], KTd, KTd.t[:, :, tc:tc + 128].rearrange("g p t -> p g t"))
                        b.dma("sp", km_, km_[:], self.RKK, self.RKK.t[tc:tc + 128, :])
                        b.dma("sp", v_, v_[:].rearrange("p h e -> p (h e)"), self.RV, self.RV.t[tc:tc + 128, :])
                    if d == 1 and want_y:
                        b.dma("sp", yl, yl[:], YP, YP.t[tc:tc + 128, :])
                        zsrc = self.Z if ssd else self.RG
                        b.dma("sp", z_, z_[:], zsrc, zsrc.t[tc:tc + 128, :])

                def stage_a(ci, tc):
                    is_ctx = tc >= S
                    want_y = (not is_ctx) or need_ctx
                    w = W[ci % 2]
                    wq = w if ssd else W[0]
                    q_, k_, km_, v_ = qT[ci % 3], kT[ci % 3], ktm[ci % 3], vtm[ci % 3]
                    x_, dr, yl, z_ = xsT[ci % 3], dtr[ci % 3], ysb[ci % 3], zt[ci % 3]
                    if ssd:
                        b.op("dve", lambda e: e.tensor_tensor(out=w.dtv[:], in0=dr[:], in1=prm[:, 1, d * 4:(d + 1) * 4], op=ALU.add),
                             r=[dr, prm], w=[w.dtv])
                        b.op("act", lambda e: e.activation(out=w.dtv[:], in_=w.dtv[:], func=AF.Exp), r=[w.dtv], w=[w.dtv])
                        b.op("act", lambda e: e.activation(out=w.dtv[:], in_=w.dtv[:], func=AF.Ln, bias=self.one_t[:]),
                             r=[w.dtv, self.one_t], w=[w.dtv])
                        b.op("dve", lambda e: e.tensor_tensor(out=w.la[:], in0=w.dtv[:], in1=negA[:, d * 4:(d + 1) * 4], op=ALU.mult),
                             r=[w.dtv, negA], w=[w.la])
                        decay_quants(d, w)
                        for g in range(2):
                            b.op("pe", lambda e, g=g: e.transpose(pX[:, g * 128:(g + 1) * 128], x_[:, g, :], self.ident[:]),
                                 r=[x_, self.ident], w=[pX])
                        b.op("act", lambda e: e.copy(out=w.xs[:].rearrange("p h e -> p (h e)"), in_=pX[:]), r=[pX], w=[w.xs])
                        for g in range(2):
                            b.op("pe", lambda e, g=g: e.transpose(pK[:, g * 128:(g + 1) * 128], k_[:, g, :], self.identb[:]),
                                 r=[k_, self.identb], w=[pK])
                        b.op("act", lambda e: e.copy(out=km_[:], in_=pK[:]), r=[pK], w=[km_])
                        b.op("dve", lambda e: e.tensor_tensor(out=w.coef[:], in0=w.dtv[:], in1=w.wv[:], op=ALU.mult),
                             r=[w.dtv, w.wv], w=[w.coef])
                        b.op("dve", lambda e: e.tensor_tensor(out=w.vd[:], in0=w.xs[:], in1=bc(w.dtv[:].unsqueeze(2), [128, 4, 64]),
                                                              op=ALU.mult), r=[w.xs, w.dtv], w=[w.vd])
                        b.op("dve", lambda e: e.tensor_tensor(out=w.vw[:], in0=w.xs[:], in1=bc(w.coef[:].unsqueeze(2), [128, 4, 64]),
                                                              op=ALU.mult), r=[w.xs, w.coef], w=[w.vw])
                        vdd = w.vd
                    else:
                        b.op("dve", lambda e: e.tensor_tensor(out=w.vw[:], in0=v_[:], in1=bc(wq.wv[:].unsqueeze(2), [128, 4, 64]),
                                                              op=ALU.mult), r=[v_, wq.wv], w=[w.vw])
                        vdd = v_
                    if want_y:
                        for g in range(NK):
                            b.op("pe", lambda e, g=g: e.matmul(pGT[:, g * 128:(g + 1) * 128], lhsT=k_[0:n, g, :],
                                                               rhs=q_[0:n, g, :], start=True, stop=True), r=[k_, q_], w=[pGT])
                        if ssd:
                            for g in range(2):
                                b.op("dve", lambda e, g=g: e.tensor_tensor(
                                    out=w.MT[:, 2 * g:2 * g + 2, :],
                                    in0=bc(pGT[:, g * 128:(g + 1) * 128].unsqueeze(1), [128, 2, 128]),
                                    in1=wq.dec[:, 2 * g:2 * g + 2, :], op=ALU.mult), r=[pGT, wq.dec], w=[w.MT])
                                b.op("pool", lambda e, g=g: e.tensor_tensor(
                                    out=w.QpT[:, 2 * g:2 * g + 2, :], in0=bc(q_[:, g:g + 1, :], [128, 2, 128]),
                                    in1=wq.Ebc[:, 2 * g:2 * g + 2, :], op=ALU.mult), r=[q_, wq.Ebc], w=[w.QpT])
                        else:
                            b.op("dve", lambda e: e.tensor_tensor(out=fl(w.MT), in0=pGT[:], in1=fl(wq.dec), op=ALU.mult),
                                 r=[pGT, wq.dec], w=[w.MT])
                            b.op("dve", lambda e: e.tensor_tensor(out=w.QpT[0:64, :, :], in0=q_[0:64, :, :], in1=wq.Ebc[0:64, :, :],
                                                                   op=ALU.mult), r=[q_, wq.Ebc], w=[w.QpT])
                    return dict(w=w, wq=wq, q_=q_, k_=k_, km_=km_, v_=v_, vdd=vdd, want_y=want_y,
                                yl=(yl if (d == 1 and want_y) else None), z_=(z_ if (d == 1 and want_y) else None))

                def stage_b(ci, tc, cx):
                    w, wq, q_, k_, km_, v_, vdd, want_y, yl, z_ = (cx[k] for k in ('w', 'wq', 'q_', 'k_', 'km_', 'v_', 'vdd', 'want_y', 'yl', 'z_'))
                    if want_y:
                        for h in range(4):
                            b.op("pe", lambda e, h=h: e.matmul(py[:, h * 64:(h + 1) * 64], lhsT=w.MT[:, h, :], rhs=vdd[:, h, :],
                                                               start=True, stop=False), r=[w.MT, vdd], w=[py])
                            b.op("pe", lambda e, h=h: e.matmul(py[:, h * 64:(h + 1) * 64], lhsT=w.QpT[0:n, h, :], rhs=Sb[0:n, h, :],
                                                               start=False, stop=True), r=[w.QpT, Sb], w=[py])
                    last = ci == len(order) - 1
                    if not last:
                        for h in range(4):
                            g = kq(h)
                            b.op("pe", lambda e, h=h, g=g: e.matmul(pS[0:n, h * 64:(h + 1) * 64], lhsT=km_[:, g * n:(g + 1) * n],
                                                                   rhs=w.vw[:, h, :], start=True, stop=True), r=[km_, w.vw], w=[pS])
                        b.op("dve", lambda e: e.tensor_tensor(out=Sf[0:n, :, :], in0=Sf[0:n, :, :],
                                                              in1=bc(wq.etot[0:n, :].unsqueeze(2), [n, 4, 64]), op=ALU.mult),
                             r=[Sf, wq.etot], w=[Sf])
                        b.op("dve", lambda e: e.tensor_tensor(out=Sf[0:n, :, :].rearrange("p h e -> p (h e)"),
                                                              in0=Sf[0:n, :, :].rearrange("p h e -> p (h e)"), in1=pS[0:n, :], op=ALU.add),
                             r=[Sf, pS], w=[Sf])
                        b.op("act", lambda e: e.copy(out=Sb[0:n, :, :], in_=Sf[0:n, :, :]), r=[Sf], w=[Sb])
                    if not want_y:
                        return False
                    if d == 0:
                        yo = ysb[ci % 3]
                        b.op("act", lambda e: e.copy(out=yo[:], in_=py[:]), r=[py], w=[yo])
                        b.dma("sp", YP, YP.t[tc:tc + 128, :], yo, yo[:])
                        return False
                    y2, y3, ss, mvh, sth = w.y2, w.y3, w.ss, w.mvh, w.sth
                    b.op("dve", lambda e: e.tensor_tensor(out=y2[:], in0=py[:], in1=yl[:], op=ALU.add), r=[py, yl], w=[y2])
                    if ssd:
                        b.op("pool", lambda e: e.tensor_tensor(out=y3[:], in0=w.xs[:].rearrange("p h e -> p (h e)"),
                                                               in1=dsum_bc[:].rearrange("p h e -> p (h e)"), op=ALU.mult),
                             r=[w.xs, dsum_bc], w=[y3])
                        b.op("pool", lambda e: e.tensor_tensor(out=y2[:], in0=y2[:], in1=y3[:], op=ALU.add), r=[y2, y3], w=[y2])
                        b.op("dve", lambda e: e.tensor_tensor(out=y2[:], in0=y2[:], in1=z_[:], op=ALU.mult), r=[y2, z_], w=[y2])
                        b.op("act", lambda e: e.activation(out=y3[:], in_=y2[:], func=AF.Square, accum_out=ss[:, 0:1]),
                             r=[y2], w=[y3, ss])
                        b.op("act", lambda e: e.activation(out=ss[:, 1:2], in_=ss[:, 0:1], func=AF.Ln, scale=1.0 / 256,
                                                           bias=self.eps6[:]), r=[ss, self.eps6], w=[ss])
                        b.op("act", lambda e: e.activation(out=ss[:, 2:3], in_=ss[:, 1:2], func=AF.Exp, scale=-0.5), r=[ss], w=[ss])
                        b.op("dve", lambda e: e.scalar_tensor_tensor(out=y3[:], in0=y2[:], scalar=ss[:, 2:3], in1=gn[:],
                                                                     op0=ALU.mult, op1=ALU.mult), r=[y2, ss, gn], w=[y3])
                    else:
                        for h in range(4):
                            b.op("dve", lambda e, h=h: e.bn_stats(out=sth[:, h, :], in_=y2[:, h * 64:(h + 1) * 64]), r=[y2], w=[sth])
                            b.op("dve", lambda e, h=h: e.bn_aggr(out=mvh[:, h, :], in_=sth[:, h, :]), r=[sth], w=[mvh])
                        b.op("act", lambda e: e.activation(out=ss[:], in_=mvh[:, :, 1], func=AF.Ln, bias=self.eps_t[:]),
                             r=[mvh, self.eps_t], w=[ss])
                        b.op("act", lambda e: e.activation(out=ss[:], in_=ss[:], func=AF.Exp, scale=-0.5), r=[ss], w=[ss])
                        y2v = y2[:].rearrange("p (h e) -> p h e", e=64)
                        y3v = y3[:].rearrange("p (h e) -> p h e", e=64)
                        b.op("dve", lambda e: e.tensor_tensor(out=y3v, in0=y2v, in1=bc(mvh[:, :, 0:1], [128, 4, 64]), op=ALU.subtract),
                             r=[y2, mvh], w=[y3])
                        b.op("dve", lambda e: e.tensor_tensor(out=y3v, in0=y3v, in1=bc(ss[:].unsqueeze(2), [128, 4, 64]), op=ALU.mult),
                             r=[y3, ss], w=[y3])
                        b.op("dve", lambda e: e.tensor_tensor(out=y3[:], in0=y3[:], in1=z_[:], op=ALU.mult), r=[y3, z_], w=[y3])
                    return True

                def stage_c(ci, tc, cx):
                    y3 = cx['w'].y3
                    g_ = mgo[ci % 2]
                    for j in range(2):
                        b.op("pe", lambda e, j=j: e.transpose(pM[:, j * 128:(j + 1) * 128], y3[:, j * 128:(j + 1) * 128],
                                                             self.ident[:]), r=[y3, self.ident], w=[pM])
                    b.op("act", lambda e: e.copy(out=g_[:].rearrange("p j t -> p (j t)"), in_=pM[:]), r=[pM], w=[g_])
                    b.dma("sp", self.MGT, self.MGT.t[col_base:col_base + 256, tc:tc + 128].rearrange("(j p) t -> p j t", p=128),
                          g_, g_[:])
                stage_l(0, order[0])
                if len(order) > 1:
                    stage_l(1, order[1])
                cxs = {0: stage_a(0, order[0])}
                prev_c = None
                for ci, tc in enumerate(order):
                    if ci + 2 < len(order):
                        stage_l(ci + 2, order[ci + 2])
                    if ci + 1 < len(order):
                        cxs[ci + 1] = stage_a(ci + 1, order[ci + 1])
                    cx_ = cxs.pop(ci)
                    pend_ = stage_b(ci, tc, cx_)
                    if prev_c is not None:
                        stage_c(*prev_c)
                        prev_c = None
                    if pend_:
                        prev_c = (ci, tc, cx_)
                if prev_c is not None:
                    stage_c(*prev_c)
            b.barrier()
            rel = [prm] + dtr + qT + kT + ktm + xsT + vtm + ysb + zt + mgo
            if ssd:
                rel.append(gn)
            b.release(rel)

    def mixer_outproj(self, l):
        b, S, T = self.b, self.S, self.T
        need_ctx = l < self.depth - 1
        with ExitStack() as st:
            gate, g_bc, b_bc = self.load_bcast(st, l, 1, 1.0)
            wo = b.sb(st, "wo", [128, KC, D], BF16)
            wstg = [b.sb(st, "wostg%d" % k, [128, D], F32) for k in range(2)]
            engs = ["act", "pool", "dve"]
            for kc in range(KC):
                s_ = wstg[kc % 2]
                b.dma("sp", s_, s_[:], self.w_out, self.w_out.t[l, kc * 128:(kc + 1) * 128, :])
                self.cast_to(engs[kc % 3], wo[:, kc, :], s_[:], [s_], [wo])
            mg = [b.sb(st, "mg%d" % k, [128, KC, 512], BF16) for k in range(2)]
            xin = [b.sb(st, "xin%d" % k, [128, 4, D], F32) for k in range(2)]
            ybuf = [b.sb(st, "ybuf%d" % k, [128, D], F32) for k in range(2)]
            stt = b.sb(st, "stt", [128, 12], F32)
            mv = b.sb(st, "mv", [128, 2], F32)
            rstd = b.sb(st, "rstd", [128, 1], F32)
            nmr = b.sb(st, "nmr", [128, 1], F32)
            pd = [b.ps(st, "pd%d" % k, [128, D]) for k in range(2)]
            it = 0
            blks = [(bi, t0, ntok, v) for bi, (t0, ntok, v) in enumerate(self.blocks) if not (v == 1 and not need_ctx)]

            def load_blk(k):
                bi_, t0_, ntok_, v_ = blks[k]
                b.dma("sp", xin[bi_ % 2], xin[bi_ % 2][:, 0:ntok_ // 128, :], self.Xb[bi_],
                      self.X.t[t0_:t0_ + ntok_, :].rearrange("(s p) d -> p s d", p=128))
                b.dma("sp", mg[bi_ % 2], mg[bi_ % 2][:, :, 0:ntok_], self.MGT,
                      self.MGT.t[:, t0_:t0_ + ntok_].rearrange("(k p) t -> p k t", p=128))

            load_blk(0)
            for k_, (bi, t0, ntok, v) in enumerate(blks):
                if k_ + 1 < len(blks):
                    load_blk(k_ + 1)
                nsub = ntok // 128
                xi, m_ = xin[bi % 2], mg[bi % 2]
                for s in range(nsub):
                    p_ = pd[it % 2]
                    y = ybuf[it % 2]
                    it += 1
                    for h in range(2):
                        for kc in range(KC):
                            b.op("pe", lambda e, kc=kc, h=h: e.matmul(p_[:, h * 512:(h + 1) * 512], lhsT=m_[:, kc, s * 128:(s + 1) * 128],
                                                                     rhs=wo[:, kc, h * 512:(h + 1) * 512], start=(kc == 0),
                                                                     stop=(kc == KC - 1)), r=[m_, wo], w=[p_])
                    b.op("dve", lambda e: e.tensor_tensor(out=y[:], in0=p_[:], in1=gate[v][:], op=ALU.mult), r=[p_, gate[v]], w=[y])
                    b.op("dve", lambda e, s=s: e.scalar_tensor_tensor(out=y[:], in0=xi[:, s, :], scalar=ALPHA, in1=y[:],
                                                                     op0=ALU.mult, op1=ALU.add), r=[xi, y], w=[y])
                    self._xo_tl = xi
                    self.layer_norm_store(y, xi[:, s, :], g_bc, b_bc, stt, mv, rstd, nmr)
                b.dma("sp", self.Xb[bi], self.X.t[t0:t0 + ntok, :].rearrange("(s p) d -> p s d", p=128), xi, xi[:, 0:nsub, :])
            b.barrier()
            b.release([gate[0], gate[1], g_bc, b_bc, wo] + wstg + mg + xin)

    def build(self):
        b = self.b
        with b.es:
            self.declare_io()
            self.declare_mixer_io()
            with ExitStack() as st:
                self.eps_t = b.sb(st, "eps_t", [128, 1], F32)
                b.op("pool", lambda e: e.memset(self.eps_t[:], LN_EPS), w=[self.eps_t])
                self.prologue(st)
                self.load_consts(st)
                b.barrier()
                first = True
                stop = self.stop_after
                for l in range(self.depth):
                    need_ctx = l < self.depth - 1
                    self.ffn(l, 0, first=first)
                    first = False
                    if stop == ("ffn0", l):
                        break
                    self.mixer_inproj(l)
                    self.mixer_conv(l)
                    if stop == ("inproj", l):
                        break
                    self.mixer_attention(l)
                    if stop == ("att", l):
                        break
                    self.mixer_scan(l, "ssd")
                    if stop == ("ssd", l):
                        break
                    self.mixer_scan(l, "ret")
                    if stop == ("ret", l):
                        break
                    self.mixer_outproj(l)
                    if stop == ("mix", l):
                        break
                    self.ffn(l, 1, skip_ctx=not need_ctx)
                for bi, (t0, ntok, v) in enumerate(self.blocks):
                    if t0 < self.out_rows:
                        b.dma("sp", self.out, self.out.t[t0:t0 + ntok, :], self.Xb[bi], self.X.t[t0:t0 + ntok, :],
                              sem_tl=self.out)
                for name in self.dbg.get("dump", []):
                    src = getattr(self, name)
                    dst = dram_tl(b, "dbg_" + name, list(src.t.shape), src.t.dtype, "ExternalOutput")
                    b.dma("sp", dst, dst.t, src, src.t, sem_tl=self.out)
                E = b.engs["sp"]
                for ds in b.dsems:
                    if ds.cum > 0:
                        E.h.wait_ge(ds.sem, ds.cum)
                b.barrier()
        return self.nc


def _rot_tables(S):
    f32 = np.float32
    nb = S // 512
    t = np.arange(S, dtype=f32)
    row_pos = np.floor(t / f32(64)).astype(f32)
    col_pos = (t - row_pos * f32(64)).astype(f32)
    axis_freq = (f32(1.0) / (f32(10000.0) ** (np.arange(0, 32, 2, dtype=f32) / f32(32)))).astype(f32)
    ret_freq = (f32(1.0) / (f32(10000.0) ** np.linspace(0.0, 1.0, 32, dtype=f32))).astype(f32)
    r = np.arange(128)
    d = r % 64
    fa = axis_freq[d % 16]
    pos_a = np.where((d < 32)[:, None], row_pos[None, :], col_pos[None, :]).astype(f32)
    ang_a = (pos_a * fa[:, None]).astype(f32)
    sgn_a = np.where((d % 32) < 16, -1.0, 1.0).astype(f32)
    fr = ret_freq[d % 32]
    ang_r = (t[None, :] * fr[:, None]).astype(f32)
    sgn_r = np.where(d < 32, -1.0, 1.0).astype(f32)
    cosA, sinA = np.cos(ang_a).astype(f32), (np.sin(ang_a).astype(f32) * sgn_a[:, None])
    cosR, sinR = np.cos(ang_r).astype(f32), (np.sin(ang_r).astype(f32) * sgn_r[:, None])
    tabs = np.stack([cosA, sinA, cosR, sinR, cosR * f32(0.125), sinR * f32(0.125)], axis=1)
    return np.ascontiguousarray(tabs.reshape(128, 6, nb, 512).transpose(2, 0, 1, 3)).astype(f32)


def _cmats():
    cm = np.zeros((128, 8, 128), np.float32)
    r = np.arange(128)
    permA = (r // 32) * 32 + ((r % 32) + 16) % 32
    permR = (r // 64) * 64 + ((r % 64) + 32) % 64
    cm[permA, 0, r] = 1.0
    cm[permR, 1, r] = 1.0
    s, l = np.meshgrid(r, r, indexing="ij")
    cm[:, 2, :] = (s <= l)
    cm[:, 3, :] = -1.0 * (s < l)
    cm[:, 4, :] = 1.0
    cm[:, 5, :] = np.where(s <= l, 0.0, -30000.0)
    cm[:, 6, :] = np.where(s >= l, 0.0, -30000.0)
    return cm


def make_in_maps(inputs, S, SC, depth, n_cores):
    f = lambda a: np.ascontiguousarray(np.asarray(a, dtype=np.float32))
    L = depth
    shared = {
        "ident": np.eye(128, dtype=np.float32),
        "rot_tab": _rot_tables(S),
        "cmats": _cmats(),
    }
    for k in ("ada_w", "ada_b", "norm_g", "norm_b", "ffn_w_gate", "ffn_w_up", "ffn_w_down", "w_in", "w_out", "conv_w",
              "conv_b", "att_lambda", "att_subln_g", "ssm_norm_g"):
        shared[k] = f(inputs[k][:L])
    for k in ("ssm_a_log", "ssm_dt_bias", "ssm_d", "ret_log_gamma"):
        shared[k] = f(np.asarray(inputs[k][:L]).reshape(L, 8))
    maps = []
    for c in range(n_cores):
        m = dict(shared)
        m["x"] = f(inputs["x"][c])
        m["ctx"] = f(inputs["ctx"][c])
        m["c2"] = f(np.stack([np.asarray(inputs["c"][c]), np.asarray(inputs["c_ctx"])], axis=1))
        maps.append(m)
    return maps


def kernel(**inputs):
    S, SC = 4096, 256
    n = 8
    prog = Prog(S, SC, DEPTH)
    nc = prog.build()
    maps = make_in_maps(inputs, S, SC, DEPTH, n)
    res = run_bass_kernel_spmd(nc, maps, core_ids=list(range(n)))
    return np.stack([r["out"] for r in res.results], axis=0).astype(np.float32)
```

```python
import math
from contextlib import ExitStack

import numpy as np
import concourse.bass as bass
import concourse.mybir as mybir
from concourse.bass_utils import run_bass_kernel_spmd

F32 = mybir.dt.float32
BF16 = mybir.dt.bfloat16
AF = mybir.ActivationFunctionType
ALU = mybir.AluOpType

D = 1024
DFF = 2816
KC = D // 128
FC = DFF // 128
DEPTH = 4
LN_EPS = 1e-5
ALPHA = (2 * DEPTH) ** 0.25
INC = 3588


class Eng:
    def __init__(self, name, h, sem):
        self.name, self.h, self.sem, self.cnt = name, h, sem, 0
        self.waited = {}


class DSem:
    def __init__(self, sem):
        self.sem, self.cum = sem, 0


class Tl:
    def __init__(self, t, name):
        self.t, self.name = t, name
        self.w = {}
        self.r = {}
        self.ds = None

    def __getitem__(self, k):
        return self.t[k]


class Bld:
    def __init__(self, nc):
        self.nc = nc
        self.es = ExitStack()
        self.engs = {}
        for name, h in [("pe", nc.tensor), ("act", nc.scalar), ("dve", nc.vector), ("pool", nc.gpsimd),
                        ("sp", nc.sync)]:
            sem = self.es.enter_context(nc.semaphore("sem_" + name))
            self.engs[name] = Eng(name, h, DSem(sem))
        self.dsems = []
        self.free_dsems = []
        self.n_instr = 0

    def _uniq(self, name):
        self.n_names = getattr(self, "n_names", 0) + 1
        return "%s_%d" % (name, self.n_names)

    def sb(self, stack, name, shape, dt):
        t = stack.enter_context(self.nc.sbuf_tensor(self._uniq(name), list(shape), dt))
        return Tl(t, name)

    def ps(self, stack, name, shape, dt=F32):
        esz = 4 if dt == F32 else 2
        per_bank = 2048 // esz
        nfree = 1
        for d_ in shape[1:]:
            nfree *= d_
        nb = (nfree + per_bank - 1) // per_bank
        raw = stack.enter_context(self.nc.psum_tensor(self._uniq(name), [128, nb * per_bank], dt))
        v = raw[0:shape[0], 0:nfree]
        if len(shape) == 3:
            v = v.rearrange("p (a b) -> p a b", b=shape[2])
        elif len(shape) != 2:
            raise ValueError("ps: 2-D or 3-D shapes only")
        tl = Tl(v, name)
        tl.psum = True
        return tl

    def dram(self, name, shape, dt, kind="Internal"):
        t = self.nc.dram_tensor(name, list(shape), dt, kind=kind).ap()
        return Tl(t, name)

    def _dsem(self, tl):
        if tl.ds is None:
            if self.free_dsems:
                tl.ds = self.free_dsems.pop()
            else:
                sem = self.es.enter_context(self.nc.semaphore("dsem%d" % len(self.dsems)))
                tl.ds = DSem(sem)
                self.dsems.append(tl.ds)
        return tl.ds

    def _wait(self, E, rec):
        so, val, src = rec
        if src == "pe" and E.name == "pe":
            return
        if src == "dma":
            val = so.cum
        if E.waited.get(id(so), 0) >= val:
            return
        E.h.wait_ge(so.sem, val)
        E.waited[id(so)] = val
        self.n_instr += 1

    def _deps(self, E, r, w):
        for t in r:
            for rec in t.w.values():
                self._wait(E, rec)
            if getattr(t, "psum", False):
                for k, rec in t.r.items():
                    if k != E.name:
                        self._wait(E, rec)
        for t in w:
            for rec in t.w.values():
                self._wait(E, rec)
            for rec in t.r.values():
                self._wait(E, rec)

    def op(self, eng, fn, r=(), w=()):
        E = self.engs[eng]
        self._deps(E, r, w)
        ins = fn(E.h)
        E.cnt += 1
        E.sem.cum = E.cnt
        ins.then_inc(E.sem.sem, 1)
        rec = (E.sem, E.cnt, eng)
        for t in r:
            t.r[eng] = rec
        for t in w:
            t.w = {eng: rec}
            t.r = {}
        self.n_instr += 1
        return ins

    def dma(self, eng, out_tl, out_ap, in_tl, in_ap, sem_tl=None, **kw):
        E = self.engs[eng]
        tr_in = not (_is_dram(in_tl) and not getattr(in_tl, "tracked", False))
        tr_out = not (_is_dram(out_tl) and not getattr(out_tl, "tracked", False))
        self._deps(E, [in_tl] if tr_in else [], [out_tl] if tr_out else [])
        if sem_tl is None:
            sem_tl = out_tl if not _is_dram(out_tl) else in_tl
        ds = self._dsem(sem_tl)
        ins = E.h.dma_start(out=out_ap, in_=in_ap, **kw)
        ds.cum += 16
        ins.then_inc(ds.sem, 16)
        rec = (ds, ds.cum, "dma")
        if tr_in:
            in_tl.r["dma%d" % id(ds)] = rec
        if tr_out:
            out_tl.w = {"dma%d" % id(ds): rec}
            out_tl.r = {}
        self.n_instr += 1
        return ins

    def barrier(self):
        recs = [(E.sem, E.cnt, E.name) for E in self.engs.values() if E.cnt > 0]
        drecs = [(ds, ds.cum, "dma") for ds in self.dsems if ds.cum > 0]
        for E in self.engs.values():
            for rec in recs:
                if rec[2] != E.name:
                    so, val, src = rec
                    if E.waited.get(id(so), 0) < val:
                        E.h.wait_ge(so.sem, val)
                        E.waited[id(so)] = val
                        self.n_instr += 1
            for rec in drecs:
                self._wait(E, rec)

    def release(self, tls):
        for t in tls:
            if t.ds is not None:
                self.free_dsems.append(t.ds)
                t.ds = None


def _is_dram(tl):
    return getattr(tl, "is_dram", False)


def dram_tl(b, name, shape, dt, kind="Internal"):
    tl = b.dram(name, shape, dt, kind)
    tl.is_dram = True
    return tl


class Prog:
    def __init__(self, S=4096, SC=256, depth=DEPTH, dbg=None, stop_after=None):
        self.S, self.SC, self.depth = S, SC, depth
        self.T = S + SC
        self.dbg = dbg or {}
        self.stop_after = stop_after
        nc = bass.Bass("TRN2", target_bir_lowering=False)
        self.nc = nc
        self.b = Bld(nc)
        self.blocks = [(i * 512, 512, 0) for i in range(S // 512)] + [(S, SC, 1)]

    def declare_io(self):
        b, L = self.b, self.depth
        ein = lambda name, shape: dram_tl(b, name, shape, F32, "ExternalInput")
        self.x_in = ein("x", [self.S, D])
        self.ctx_in = ein("ctx", [self.SC, D])
        self.c2_in = ein("c2", [D, 2])
        self.ident_in = ein("ident", [128, 128])
        self.ada_w = ein("ada_w", [L, D, 9 * D])
        self.ada_b = ein("ada_b", [L, 9 * D])
        self.norm_g = ein("norm_g", [L, 3, D])
        self.norm_b = ein("norm_b", [L, 3, D])
        self.w_gate = ein("ffn_w_gate", [L, 2, D, DFF])
        self.w_up = ein("ffn_w_up", [L, 2, D, DFF])
        self.w_down = ein("ffn_w_down", [L, 2, DFF, D])
        self.out_rows = self.T if self.dbg.get("full_out") else self.S
        self.out = dram_tl(b, "out", [self.out_rows, D], F32, "ExternalOutput")
        self.X = dram_tl(b, "X_scr", [self.T, D], F32)
        self.DP = dram_tl(b, "DP_scr", [self.T, D], F32)
        self.Xb = [self._view(self.X) for _ in self.blocks]
        self.DPb = [self._view(self.DP) for _ in self.blocks]
        self.m_dram = dram_tl(b, "m_scr", [L, 2, 9 * D], F32)

    def prologue(self, st):
        b, L = self.b, self.depth
        self.ident = b.sb(st, "ident", [128, 128], F32)
        b.dma("sp", self.ident, self.ident[:], self.ident_in, self.ident_in.t)
        self.mcol = b.sb(st, "mcol", [128, L, 72, 2], F32)
        with ExitStack() as ps:
            c2 = b.sb(ps, "c2", [128, KC, 2], F32)
            sc2 = b.sb(ps, "sc2", [128, KC, 2], F32)
            b.dma("sp", c2, c2[:], self.c2_in, self.c2_in.t.rearrange("(k p) v -> p k v", p=128))
            b.op("act", lambda e: e.activation(out=sc2[:], in_=c2[:], func=AF.Silu), r=[c2], w=[sc2])
            aw = [b.sb(ps, "aw%d" % i, [128, KC, 1024], F32) for i in range(2)]
            adab = b.sb(ps, "adab", [2, 9 * D], F32)
            mrow = b.sb(ps, "mrow", [2, 9 * D], F32)
            pm = [b.ps(ps, "pm%d" % i, [2, 1024]) for i in range(2)]
            pcol = b.ps(ps, "pcol", [128, 72, 2])
            it = 0
            for l in range(L):
                for v in range(2):
                    b.dma("sp", adab, adab[v:v + 1, :], self.ada_b, self.ada_b.t[l:l + 1, :])
                for cg in range(9):
                    a = aw[it % 2]
                    p = pm[it % 2]
                    it += 1
                    b.dma("sp", a, a[:], self.ada_w,
                          self.ada_w.t[l, :, cg * 1024:(cg + 1) * 1024].rearrange("(k p) n -> p k n", p=128))
                    for h in range(2):
                        for kc in range(KC):
                            b.op("pe", lambda e, kc=kc, h=h: e.matmul(
                                p[0:2, h * 512:(h + 1) * 512], lhsT=sc2[:, kc, :],
                                rhs=a[:, kc, h * 512:(h + 1) * 512], start=(kc == 0), stop=(kc == KC - 1)),
                                r=[sc2, a], w=[p])
                    b.op("dve", lambda e: e.tensor_tensor(
                        out=mrow[0:2, cg * 1024:(cg + 1) * 1024], in0=p[0:2, :],
                        in1=adab[0:2, cg * 1024:(cg + 1) * 1024], op=ALU.add), r=[p, adab], w=[mrow])
                for j in range(72):
                    b.op("pe", lambda e, j=j: e.transpose(pcol[:, j, :], mrow[0:2, j * 128:(j + 1) * 128],
                                                        self.ident[0:2, 0:2]),
                         r=[mrow, self.ident], w=[pcol])
                b.op("dve", lambda e: e.tensor_copy(out=self.mcol[:, l, :, :], in_=pcol[:]),
                     r=[pcol], w=[self.mcol])
                b.dma("sp", self.m_dram, self.m_dram.t[l], mrow, mrow[0:2, :])
                for i in range(3):
                    j0 = (i * 3 + 1) * 8
                    b.op("dve", lambda e, j0=j0: e.tensor_scalar_add(
                        self.mcol[:, l, j0:j0 + 8, :], self.mcol[:, l, j0:j0 + 8, :], 1.0),
                        r=[self.mcol], w=[self.mcol])
            b.barrier()
            b.release(aw + [adab, mrow, c2])

    def load_bcast(self, st, l, i, gate_mul):
        b = self.b
        gate = [b.sb(st, "gate_bc%d" % v, [128, D], F32) for v in range(2)]
        g_bc = b.sb(st, "g_bc", [128, D], F32)
        b_bc = b.sb(st, "b_bc", [128, D], F32)
        off = (i * 3 + 2) * D
        for v in range(2):
            b.dma("sp", gate[v], gate[v][:], self.m_dram,
                  self.m_dram.t[l, v, off:off + D].partition_broadcast(128))
            b.op("dve", lambda e, v=v: e.tensor_scalar(gate[v][:], gate[v][:], 1.0, gate_mul, op0=ALU.add,
                                                      op1=ALU.mult), r=[gate[v]], w=[gate[v]])
        b.dma("sp", g_bc, g_bc[:], self.norm_g, self.norm_g.t[l, i, :].partition_broadcast(128))
        b.dma("sp", b_bc, b_bc[:], self.norm_b, self.norm_b.t[l, i, :].partition_broadcast(128))
        return gate, g_bc, b_bc

    def _view(self, tl):
        v = Tl(tl.t, tl.name)
        v.is_dram = True
        v.tracked = True
        return v

    def src_rows(self, first, bi, t0, n):
        if first:
            if t0 < self.S:
                return self.x_in, self.x_in.t[t0:t0 + n, :]
            return self.ctx_in, self.ctx_in.t[t0 - self.S:t0 - self.S + n, :]
        return self.Xb[bi], self.X.t[t0:t0 + n, :]

    def transpose_mod(self, xin, nsub, pT, uT, l, i, v, kcs=None):
        b = self.b
        ntok = nsub * 128
        for kc in (range(KC) if kcs is None else kcs):
            p = pT[kc % len(pT)]
            for s in range(nsub):
                b.op("pe", lambda e, kc=kc, s=s: e.transpose(
                    p[:, s * 128:(s + 1) * 128], xin[:, s, kc * 128:(kc + 1) * 128], self.ident[:]),
                    r=[xin, self.ident], w=[p])
            jsc = (i * 3 + 1) * 8 + kc
            jsh = (i * 3 + 0) * 8 + kc
            b.op("act", lambda e, kc=kc, jsc=jsc, jsh=jsh: e.activation(
                out=uT[:, kc, 0:ntok], in_=p[:, 0:ntok], func=AF.Identity,
                scale=self.mcol[:, l, jsc, v:v + 1], bias=self.mcol[:, l, jsh, v:v + 1]),
                r=[p, self.mcol], w=[uT])

    def layer_norm_store(self, y, xo, g_bc, b_bc, stt, mv, rstd, nmr):
        b = self.b
        for h in range(2):
            b.op("dve", lambda e, h=h: e.bn_stats(out=stt[:, h * 6:(h + 1) * 6], in_=y[:, h * 512:(h + 1) * 512]),
                 r=[y], w=[stt])
        b.op("dve", lambda e: e.bn_aggr(out=mv[:], in_=stt[:]), r=[stt], w=[mv])
        b.op("act", lambda e: e.activation(out=rstd[:], in_=mv[:, 1:2], func=AF.Sqrt, bias=self.eps_t[:], scale=1.0),
             r=[mv, self.eps_t], w=[rstd])
        b.op("dve", lambda e: e.reciprocal(out=rstd[:], in_=rstd[:]), r=[rstd], w=[rstd])
        b.op("dve", lambda e: e.tensor_scalar(nmr[:], mv[:, 0:1], -1.0, rstd[:], op0=ALU.mult, op1=ALU.mult),
             r=[mv, rstd], w=[nmr])
        b.op("act", lambda e: e.activation(out=y[:], in_=y[:], func=AF.Identity, scale=rstd[:], bias=nmr[:]),
             r=[y, rstd, nmr], w=[y])
        b.op("pool", lambda e: e.tensor_tensor(out=y[:], in0=y[:], in1=g_bc[:], op=ALU.mult), r=[y, g_bc], w=[y])
        b.op("pool", lambda e: e.tensor_tensor(out=xo, in0=y[:], in1=b_bc[:], op=ALU.add), r=[y, b_bc], w=[self._xo_tl])

    def ffn(self, l, f, first=False, skip_ctx=False):
        b = self.b
        i = 0 if f == 0 else 2
        HF = FC // 2
        HW = HF * 128
        with ExitStack() as st:
            gate, g_bc, b_bc = self.load_bcast(st, l, i, 0.5)
            wg = b.sb(st, "wg", [128, KC, HW], BF16)
            wu = b.sb(st, "wu", [128, KC, HW], BF16)
            wd = b.sb(st, "wd", [128, HF, D], BF16)
            stg = [b.sb(st, "stg%d" % k, [128, HW], F32) for k in range(5)]
            uT2 = [b.sb(st, "uT%d" % k, [128, KC, 512], BF16) for k in range(2)]
            hT = b.sb(st, "hT", [128, HF, 512], BF16)
            xin = [b.sb(st, "xin%d" % k, [128, 4, D], F32) for k in range(2)]
            dpt = [b.sb(st, "dpt%d" % k, [128, D], F32) for k in range(2)]
            ybuf = [b.sb(st, "ybuf%d" % k, [128, D], F32) for k in range(2)]
            sg = [b.sb(st, "sg%d" % k, [128, 512], F32) for k in range(2)]
            stt = b.sb(st, "stt", [128, 12], F32)
            mv = b.sb(st, "mv", [128, 2], F32)
            rstd = b.sb(st, "rstd", [128, 1], F32)
            nmr = b.sb(st, "nmr", [128, 1], F32)
            pT = [b.ps(st, "pT%d" % k, [128, 512]) for k in range(2)]
            pg = [b.ps(st, "pg%d" % k, [128, 512]) for k in range(2)]
            pu = [b.ps(st, "pu%d" % k, [128, 512]) for k in range(2)]
            pdh = [b.ps(st, "pd%d" % k, [128, 512]) for k in range(2)]
            cast_engs = ["act", "pool", "dve"]
            ci = 0
            for ps_ in range(2):
                if ps_ == 1:
                    b.barrier()
                c0 = ps_ * HW
                for (wt, src) in ((wg, self.w_gate), (wu, self.w_up)):
                    for kc in range(KC):
                        s_ = stg[ci % 5]
                        b.dma("sp", s_, s_[:], src, src.t[l, f, kc * 128:(kc + 1) * 128, c0:c0 + HW])
                        eng = cast_engs[ci % 3]
                        ci += 1
                        if eng == "act":
                            b.op("act", lambda e, wt=wt, kc=kc, s_=s_: e.copy(out=wt[:, kc, :], in_=s_[:]), r=[s_], w=[wt])
                        else:
                            b.op(eng, lambda e, wt=wt, kc=kc, s_=s_: e.tensor_copy(out=wt[:, kc, :], in_=s_[:]), r=[s_], w=[wt])
                for fc in range(HF):
                    s_ = stg[ci % 5]
                    r0 = (ps_ * HF + fc) * 128
                    b.dma("sp", s_, s_[:, 0:D], self.w_down, self.w_down.t[l, f, r0:r0 + 128, :])
                    eng = cast_engs[ci % 3]
                    ci += 1
                    if eng == "act":
                        b.op("act", lambda e, fc=fc, s_=s_: e.copy(out=wd[:, fc, :], in_=s_[:, 0:D]), r=[s_], w=[wd])
                    else:
                        b.op(eng, lambda e, fc=fc, s_=s_: e.tensor_copy(out=wd[:, fc, :], in_=s_[:, 0:D]), r=[s_], w=[wd])
                blks = [(bi, t0, ntok, v) for bi, (t0, ntok, v) in enumerate(self.blocks) if not (v == 1 and skip_ctx)]

                def load_blk(k):
                    bi_, t0_, ntok_, v_ = blks[k]
                    stl, sap = self.src_rows(first, bi_, t0_, ntok_)
                    b.dma("sp", xin[bi_ % 2], xin[bi_ % 2][:, 0:ntok_ // 128, :], stl, sap.rearrange("(s p) d -> p s d", p=128))

                def T_(k, kcs=None):
                    bi, t0, ntok, v = blks[k]
                    self.transpose_mod(xin[bi % 2], ntok // 128, pT, uT2[k % 2], l, i, v, kcs=kcs)

                def GU_(k):
                    bi, t0, ntok, v = blks[k]
                    uT = uT2[k % 2]
                    for fc in range(HF):
                        g_, u_, s_ = pg[fc % 2], pu[fc % 2], sg[fc % 2]
                        for kc in range(KC):
                            b.op("pe", lambda e, kc=kc, fc=fc: e.matmul(
                                g_[:, 0:ntok], lhsT=wg[:, kc, fc * 128:(fc + 1) * 128], rhs=uT[:, kc, 0:ntok],
                                start=(kc == 0), stop=(kc == KC - 1)), r=[wg, uT], w=[g_])
                        for kc in range(KC):
                            b.op("pe", lambda e, kc=kc, fc=fc: e.matmul(
                                u_[:, 0:ntok], lhsT=wu[:, kc, fc * 128:(fc + 1) * 128], rhs=uT[:, kc, 0:ntok],
                                start=(kc == 0), stop=(kc == KC - 1)), r=[wu, uT], w=[u_])
                        b.op("act", lambda e: e.activation(out=s_[:, 0:ntok], in_=g_[:, 0:ntok], func=AF.Silu),
                             r=[g_], w=[s_])
                        b.op("dve", lambda e, fc=fc: e.tensor_tensor(out=hT[:, fc, 0:ntok], in0=u_[:, 0:ntok],
                                                                     in1=s_[:, 0:ntok], op=ALU.mult),
                             r=[u_, s_], w=[hT])

                def DN_(k):
                    bi, t0, ntok, v = blks[k]
                    nsub = ntok // 128
                    xi = xin[bi % 2]
                    tq = list(range(KC))
                    for s in range(nsub):
                        dp = dpt[s % 2]
                        y = ybuf[s % 2]
                        r0 = t0 + s * 128
                        if ps_ == 1:
                            b.dma("sp", dp, dp[:], self.DPb[bi], self.DP.t[r0:r0 + 128, :])
                        for h in range(2):
                            pd_ = pdh[h]
                            hs = slice(h * 512, (h + 1) * 512)
                            for fc in range(HF):
                                b.op("pe", lambda e, fc=fc, h=h, s=s: e.matmul(
                                    pd_[:], lhsT=hT[:, fc, s * 128:(s + 1) * 128],
                                    rhs=wd[:, fc, h * 512:(h + 1) * 512], start=(fc == 0), stop=(fc == HF - 1)),
                                    r=[hT, wd], w=[pd_])
                            if ps_ == 0:
                                b.op("act", lambda e: e.copy(out=dp[:, hs], in_=pd_[:]), r=[pd_], w=[dp])
                            else:
                                b.op("dve", lambda e: e.tensor_tensor(out=y[:, hs], in0=pd_[:], in1=dp[:, hs], op=ALU.add),
                                     r=[pd_, dp], w=[y])
                            if k + 1 < len(blks) and tq:
                                T_(k + 1, kcs=[tq.pop(0)])
                        if ps_ == 0:
                            b.dma("sp", self.DPb[bi], self.DP.t[r0:r0 + 128, :], dp, dp[:])
                        else:
                            b.op("pool", lambda e: e.tensor_tensor(out=y[:], in0=y[:], in1=gate[v][:], op=ALU.mult),
                                 r=[y, gate[v]], w=[y])
                            b.op("dve", lambda e, s=s: e.scalar_tensor_tensor(
                                out=y[:], in0=xi[:, s, :], scalar=ALPHA, in1=y[:], op0=ALU.mult, op1=ALU.add),
                                r=[xi, y], w=[y])
                            self._xo_tl = xi
                            self.layer_norm_store(y, xi[:, s, :], g_bc, b_bc, stt, mv, rstd, nmr)
                    if k + 1 < len(blks) and tq:
                        T_(k + 1, kcs=tq)
                    if ps_ == 1:
                        b.dma("sp", self.Xb[bi], self.X.t[t0:t0 + ntok, :].rearrange("(s p) d -> p s d", p=128),
                              xi, xi[:, 0:nsub, :])

                nb_ = len(blks)
                load_blk(0)
                if nb_ > 1:
                    load_blk(1)
                T_(0)
                GU_(0)
                for k_ in range(nb_):
                    DN_(k_)
                    if k_ + 2 < nb_:
                        load_blk(k_ + 2)
                    if k_ + 1 < nb_:
                        GU_(k_ + 1)
            b.barrier()
            b.release([gate[0], gate[1], g_bc, b_bc, wg, wu, wd, hT] + uT2 + stg + xin + dpt + ybuf + sg)

    def declare_mixer_io(self):
        b, L, S, SC, T = self.b, self.depth, self.S, self.SC, self.T
        ein = lambda name, shape: dram_tl(b, name, shape, F32, "ExternalInput")
        self.w_in = ein("w_in", [L, D, INC])
        self.w_out = ein("w_out", [L, D, D])
        self.conv_w = ein("conv_w", [L, 5, 768])
        self.conv_b = ein("conv_b", [L, 768])
        self.att_lambda = ein("att_lambda", [L, 4, 64])
        self.att_subln_g = ein("att_subln_g", [L, 128])
        self.ssm_a_log = ein("ssm_a_log", [L, 8])
        self.ssm_dt_bias = ein("ssm_dt_bias", [L, 8])
        self.ssm_d = ein("ssm_d", [L, 8])
        self.ssm_norm_g = ein("ssm_norm_g", [L, 256])
        self.ret_log_gamma = ein("ret_log_gamma", [L, 8])
        nb = S // 512
        self.rot_tab = ein("rot_tab", [nb, 128, 6, 512])
        self.cmats = ein("cmats", [128, 8, 128])
        sc = lambda name, shape, dt: dram_tl(b, name, shape, dt)
        self.QT = sc("QT_scr", [4, 128, T], BF16)
        self.KT = sc("KT_scr", [4, 128, T], BF16)
        self.V1 = sc("V1_scr", [T, 512], BF16)
        self.Z = sc("Z_scr", [T, 256], F32)
        self.RG = sc("RG_scr", [T, 256], F32)
        self.DT = sc("DT_scr", [T, 4], F32)
        self.RV = sc("RV_scr", [T, 256], BF16)
        self.XBCT = sc("XBCT_scr", [6, 128, T], F32)
        self.XST = sc("XST_scr", [2, 128, T], F32)
        self.BT = sc("BT_scr", [2, 128, T], BF16)
        self.CT = sc("CT_scr", [2, 128, T], BF16)
        self.RQT = sc("RQT_scr", [4, 64, T], BF16)
        self.RKT = sc("RKT_scr", [4, 64, T], BF16)
        self.RKK = sc("RKK_scr", [T, 256], BF16)
        self.YS = sc("YS_scr", [T, 256], F32)
        self.YR = sc("YR_scr", [T, 256], F32)
        self.MGT = sc("MGT_scr", [D, T], BF16)

    def load_consts(self, st):
        b = self.b
        self.cm = b.sb(st, "cmats", [128, 8, 128], F32)
        b.dma("sp", self.cm, self.cm[:], self.cmats, self.cmats.t)
        self.identb = b.sb(st, "identb", [128, 128], BF16)
        b.op("dve", lambda e: e.tensor_copy(out=self.identb[:], in_=self.ident[:]), r=[self.ident], w=[self.identb])
        self.ones_bf = b.sb(st, "ones_bf", [128, 1], BF16)
        b.op("pool", lambda e: e.memset(self.ones_bf[:], 1.0), w=[self.ones_bf])
        self.one_t = b.sb(st, "one_t", [128, 1], F32)
        b.op("pool", lambda e: e.memset(self.one_t[:], 1.0), w=[self.one_t])
        self.eps6 = b.sb(st, "eps6", [128, 1], F32)
        b.op("pool", lambda e: e.memset(self.eps6[:], 1e-6), w=[self.eps6])
        self.mask4 = []
        for d in range(2):
            m4 = b.sb(st, "mask4_%d" % d, [128, 4, 128], F32)
            for h in range(4):
                b.op("pool", lambda e, h=h: e.tensor_copy(out=m4[:, h, :], in_=self.cm[:, 5 + d, :]), r=[self.cm], w=[m4])
            self.mask4.append(m4)

    PERM_A, PERM_R, TRI_F, NSTRICT_B, ONES = 0, 1, 2, 3, 4

    def cast_to(self, eng, out_ap, in_ap, r, w):
        b = self.b
        if eng == "act":
            b.op("act", lambda e: e.copy(out=out_ap, in_=in_ap), r=r, w=w)
        else:
            b.op(eng, lambda e: e.tensor_copy(out=out_ap, in_=in_ap), r=r, w=w)

    def mixer_inproj(self, l):
        b, S, T = self.b, self.S, self.T
        with ExitStack() as st:
            win = b.sb(st, "win", [128, KC, INC], BF16)
            wstg = [b.sb(st, "wstg%d" % k, [128, INC], F32) for k in range(2)]
            engs = ["act", "pool", "dve"]
            for kc in range(KC):
                s_ = wstg[kc % 2]
                b.dma("sp", s_, s_[:], self.w_in, self.w_in.t[l, kc * 128:(kc + 1) * 128, :])
                self.cast_to(engs[kc % 3], win[:, kc, :], s_[:], [s_], [win])
            uT = b.sb(st, "uT", [128, KC, 512], BF16)
            xin = [b.sb(st, "xin%d" % k, [128, 4, D], F32) for k in range(2)]
            tab = [b.sb(st, "tab%d" % k, [128, 6, 512], F32) for k in range(2)]
            qs = [b.sb(st, "qs%d" % k, [128, 512], F32) for k in range(2)]
            t1 = b.sb(st, "t1", [128, 512], F32)
            t2 = b.sb(st, "t2", [128, 512], F32)
            ob = [b.sb(st, "ob%d" % k, [128, 512], BF16) for k in range(3)]
            xb = [b.sb(st, "xb%d" % k, [128, 512], F32) for k in range(2)]
            rkk = b.sb(st, "rkk", [128, 4, 256], BF16)
            vst = b.sb(st, "vst", [128, 4, 512], BF16)
            zst = b.sb(st, "zst", [128, 4, 256], F32)
            rgst = b.sb(st, "rgst", [128, 4, 256], F32)
            rvst = b.sb(st, "rvst", [128, 4, 256], BF16)
            dst = b.sb(st, "dst", [128, 4, 4], F32)
            pT = [b.ps(st, "pT", [128, 512])]
            pf = [b.ps(st, "pf%d" % k, [128, 512]) for k in range(2)]
            pr = b.ps(st, "pr", [128, 512])
            ptr = b.ps(st, "ptr", [128, 512], BF16)
            pav = b.ps(st, "pav", [128, 512])
            pz = b.ps(st, "pz", [128, 512])
            prr = b.ps(st, "prr", [128, 512])
            nf = 0
            oi = 0
            def load_blk(bi_):
                t0_, ntok_, v_ = self.blocks[bi_]
                b.dma("sp", xin[bi_ % 2], xin[bi_ % 2][:, 0:ntok_ // 128, :], self.Xb[bi_],
                      self.X.t[t0_:t0_ + ntok_, :].rearrange("(s p) d -> p s d", p=128))
                if v_ == 0:
                    b.dma("sp", tab[bi_ % 2], tab[bi_ % 2][:], self.rot_tab, self.rot_tab.t[bi_])

            load_blk(0)
            for bi, (t0, ntok, v) in enumerate(self.blocks):
                if bi + 1 < len(self.blocks):
                    load_blk(bi + 1)
                nsub = ntok // 128
                xi = xin[bi % 2]
                tb = tab[bi % 2]
                self.transpose_mod(xi, nsub, pT, uT, l, 1, v)
                chunks = []
                for c in range(4):
                    chunks.append(("aq", c, c * 128))
                for c in range(4):
                    chunks.append(("ak", c, 512 + c * 128))
                for c in range(2):
                    chunks.append(("rq", c, 2564 + c * 128))
                for c in range(2):
                    chunks.append(("rk", c, 2820 + c * 128))
                for c in range(6):
                    chunks.append(("xbc", c, 1792 + c * 128))
                for (kind, c, col0) in chunks:
                    p_ = pf[nf % 2]
                    nf += 1
                    for kc in range(KC):
                        b.op("pe", lambda e, kc=kc: e.matmul(p_[:, 0:ntok], lhsT=win[:, kc, col0:col0 + 128],
                                                             rhs=uT[:, kc, 0:ntok], start=(kc == 0), stop=(kc == KC - 1)),
                             r=[win, uT], w=[p_])
                    if kind == "xbc":
                        x_ = xb[c % 2]
                        b.op("act", lambda e: e.copy(out=x_[:, 0:ntok], in_=p_[:, 0:ntok]), r=[p_], w=[x_])
                        b.dma("sp", self.XBCT, self.XBCT.t[c, :, t0:t0 + ntok], x_, x_[:, 0:ntok])
                        continue
                    o_ = ob[oi % 3]
                    oi += 1
                    if v == 1:
                        if kind == "rk":
                            b.op("act", lambda e: e.mul(o_[:, 0:ntok], p_[:, 0:ntok], 0.125), r=[p_], w=[o_])
                        else:
                            b.op("act", lambda e: e.copy(out=o_[:, 0:ntok], in_=p_[:, 0:ntok]), r=[p_], w=[o_])
                    else:
                        q_ = qs[oi % 2]
                        ti = {"aq": 0, "ak": 0, "rq": 2, "rk": 4}[kind]
                        pm = self.PERM_A if kind in ("aq", "ak") else self.PERM_R
                        b.op("act", lambda e: e.copy(out=q_[:], in_=p_[:]), r=[p_], w=[q_])
                        b.op("pe", lambda e: e.matmul(pr[:], lhsT=self.cm[:, pm, :], rhs=q_[:], start=True, stop=True),
                             r=[self.cm, q_], w=[pr])
                        b.op("pool", lambda e: e.tensor_tensor(out=t1[:], in0=q_[:], in1=tb[:, ti, :], op=ALU.mult),
                             r=[q_, tb], w=[t1])
                        b.op("dve", lambda e: e.tensor_tensor(out=t2[:], in0=pr[:], in1=tb[:, ti + 1, :], op=ALU.mult),
                             r=[pr, tb], w=[t2])
                        b.op("dve", lambda e: e.tensor_tensor(out=o_[:], in0=t1[:], in1=t2[:], op=ALU.add),
                             r=[t1, t2], w=[o_])
                    if kind == "aq":
                        b.dma("sp", self.QT, self.QT.t[c, :, t0:t0 + ntok], o_, o_[:, 0:ntok])
                    elif kind == "ak":
                        b.dma("sp", self.KT, self.KT.t[c, :, t0:t0 + ntok], o_, o_[:, 0:ntok])
                    elif kind == "rq":
                        for hh in range(2):
                            b.dma("sp", self.RQT, self.RQT.t[2 * c + hh, :, t0:t0 + ntok], o_, o_[hh * 64:(hh + 1) * 64, 0:ntok])
                    else:
                        for hh in range(2):
                            b.dma("sp", self.RKT, self.RKT.t[2 * c + hh, :, t0:t0 + ntok], o_, o_[hh * 64:(hh + 1) * 64, 0:ntok])
                        for s in range(nsub):
                            b.op("pe", lambda e, s=s: e.transpose(ptr[:, s * 128:(s + 1) * 128], o_[:, s * 128:(s + 1) * 128],
                                                                 self.identb[:]), r=[o_, self.identb], w=[ptr])
                        b.op("dve", lambda e: e.tensor_copy(
                            out=rkk[:, 0:nsub, c * 128:(c + 1) * 128],
                            in_=ptr[:, 0:ntok].rearrange("p (s c) -> p s c", c=128)), r=[ptr], w=[rkk])
                for s in range(nsub):
                    lt = lambda kc: uT[:, kc, s * 128:(s + 1) * 128]
                    for kc in range(KC):
                        b.op("pe", lambda e, kc=kc: e.matmul(pav[:], lhsT=lt(kc), rhs=win[:, kc, 1024:1536],
                                                             start=(kc == 0), stop=(kc == KC - 1)), r=[win, uT], w=[pav])
                    b.op("act", lambda e, s=s: e.copy(out=vst[:, s, :], in_=pav[:]), r=[pav], w=[vst])
                    for kc in range(KC):
                        b.op("pe", lambda e, kc=kc: e.matmul(pz[:, 0:256], lhsT=lt(kc), rhs=win[:, kc, 1536:1792],
                                                             start=(kc == 0), stop=(kc == KC - 1)), r=[win, uT], w=[pz])
                    for kc in range(KC):
                        b.op("pe", lambda e, kc=kc: e.matmul(pz[:, 256:260], lhsT=lt(kc), rhs=win[:, kc, 2560:2564],
                                                             start=(kc == 0), stop=(kc == KC - 1)), r=[win, uT], w=[pz])
                    b.op("act", lambda e, s=s: e.activation(out=zst[:, s, :], in_=pz[:, 0:256], func=AF.Silu), r=[pz], w=[zst])
                    b.op("dve", lambda e, s=s: e.tensor_copy(out=dst[:, s, :], in_=pz[:, 256:260]), r=[pz], w=[dst])
                    for kc in range(KC):
                        b.op("pe", lambda e, kc=kc: e.matmul(prr[:], lhsT=lt(kc), rhs=win[:, kc, 3076:3588],
                                                             start=(kc == 0), stop=(kc == KC - 1)), r=[win, uT], w=[prr])
                    b.op("act", lambda e, s=s: e.copy(out=rvst[:, s, :], in_=prr[:, 0:256]), r=[prr], w=[rvst])
                    b.op("act", lambda e, s=s: e.activation(out=rgst[:, s, :], in_=prr[:, 256:512], func=AF.Silu), r=[prr], w=[rgst])
                rows = lambda tl_: tl_.t[t0:t0 + ntok, :].rearrange("(s p) c -> p s c", p=128)
                b.dma("sp", self.V1, rows(self.V1), vst, vst[:, 0:nsub, :])
                b.dma("sp", self.Z, rows(self.Z), zst, zst[:, 0:nsub, :])
                b.dma("sp", self.RG, rows(self.RG), rgst, rgst[:, 0:nsub, :])
                b.dma("sp", self.RV, rows(self.RV), rvst, rvst[:, 0:nsub, :])
                b.dma("sp", self.DT, rows(self.DT), dst, dst[:, 0:nsub, :])
                b.dma("sp", self.RKK, rows(self.RKK), rkk, rkk[:, 0:nsub, :])
            b.barrier()
            b.release([win, uT, rkk, vst, zst, rgst, rvst, dst] + wstg + xin + tab + qs + ob + xb)

    def load_cols(self, st, name, src_tl, src_ap, nrow, ncol, pst):
        b = self.b
        nr = nrow + (nrow % 2)
        rowt = b.sb(st, name + "_row", [nr, ncol * 128], F32)
        colt = b.sb(st, name + "_col", [128, ncol, nr], F32)
        b.op("pool", lambda e: e.memset(rowt[:], 0.0), w=[rowt])
        b.dma("sp", rowt, rowt[0:nrow, :], src_tl, src_ap)
        for c in range(ncol):
            b.op("pe", lambda e, c=c: e.transpose(pst[:, c * nr:(c + 1) * nr], rowt[0:nr, c * 128:(c + 1) * 128],
                                                 self.ident[0:nr, 0:nr]), r=[rowt, self.ident], w=[pst])
        b.op("dve", lambda e: e.tensor_copy(out=colt[:], in_=pst[:, 0:ncol * nr].rearrange("p (c k) -> p c k", k=nr)),
             r=[pst], w=[colt])
        return colt, rowt

    def mixer_conv(self, l):
        b, S, SC = self.b, self.S, self.SC
        with ExitStack() as st:
            pst = b.ps(st, "pst", [128, 512])
            cw, r1 = self.load_cols(st, "cw", self.conv_w, self.conv_w.t[l], 5, 6, pst)
            cb, r2 = self.load_cols(st, "cb", self.conv_b, self.conv_b.t[l:l + 1, :], 1, 6, pst)
            for (t0, n) in ((0, S), (S, SC)):
                with ExitStack() as st2:
                    pre = b.sb(st2, "pre", [128, 6, n + 4], F32)
                    acc = [b.sb(st2, "acc%d" % k, [128, n], F32) for k in range(2)]
                    of = [b.sb(st2, "of%d" % k, [128, n], F32) for k in range(2)]
                    obf = [b.sb(st2, "obf%d" % k, [128, n], BF16) for k in range(2)]
                    b.op("pool", lambda e: e.memset(pre[:, :, 0:2], 0.0), w=[pre])
                    b.op("pool", lambda e: e.memset(pre[:, :, n + 2:n + 4], 0.0), w=[pre])
                    for c in range(6):
                        b.dma("sp", pre, pre[:, c, 2:n + 2], self.XBCT, self.XBCT.t[c, :, t0:t0 + n])
                    for c in range(6):
                        a_ = acc[c % 2]
                        b.op("dve", lambda e: e.tensor_scalar(a_[:], pre[:, c, 0:n], cw[:, c, 0:1], None, op0=ALU.mult),
                             r=[pre, cw], w=[a_])
                        for k in range(1, 5):
                            b.op("dve", lambda e, k=k: e.scalar_tensor_tensor(
                                out=a_[:], in0=pre[:, c, k:k + n], scalar=cw[:, c, k:k + 1], in1=a_[:], op0=ALU.mult,
                                op1=ALU.add), r=[pre, cw, a_], w=[a_])
                        if c < 2:
                            o_ = of[c % 2]
                            b.op("act", lambda e: e.activation(out=o_[:], in_=a_[:], func=AF.Silu, bias=cb[:, c, 0:1]),
                                 r=[a_, cb], w=[o_])
                            b.dma("sp", self.XST, self.XST.t[c, :, t0:t0 + n], o_, o_[:])
                        else:
                            o_ = obf[c % 2]
                            b.op("act", lambda e: e.activation(out=o_[:], in_=a_[:], func=AF.Silu, bias=cb[:, c, 0:1]),
                                 r=[a_, cb], w=[o_])
                            dstt = self.BT if c < 4 else self.CT
                            b.dma("sp", dstt, dstt.t[c % 2, :, t0:t0 + n], o_, o_[:])
                    b.barrier()
                    b.release([pre] + acc + of + obf)
            b.barrier()
            b.release([r1, r2])

    def mixer_attention(self, l):
        b, S, SC, T = self.b, self.S, self.SC, self.T
        need_ctx = l < self.depth - 1
        lam_init = 0.8 - 0.6 * math.exp(-0.3 * l)
        NKT = T // 128
        with ExitStack() as st:
            kt_sb = b.sb(st, "kt_sb", [128, 4, 2, T], BF16)
            v_sb = b.sb(st, "v_sb", [128, NKT, 512], BF16)
            for h in range(4):
                for m in range(2):
                    b.op("dve" if (h + m) % 2 == 0 else "pool", lambda e, h=h, m=m: e.memset(kt_sb[:, h, m, :], 0.0), w=[kt_sb])
            for h in range(4):
                for m in range(2):
                    b.dma("sp", kt_sb, kt_sb[m * 64:(m + 1) * 64, h, m, :], self.KT, self.KT.t[h, m * 64:(m + 1) * 64, :])
            b.dma("sp", v_sb, v_sb[:], self.V1, self.V1.t.rearrange("(k p) c -> p k c", p=128))
            lv = b.sb(st, "lv", [128, 4, 64], F32)
            lp = b.sb(st, "lp", [128, 2, 64], F32)
            ls = b.sb(st, "ls", [128, 2], F32)
            neglam = b.sb(st, "neglam", [128, 1], F32)
            gsub = b.sb(st, "gsub", [128, 1], F32)
            b.dma("sp", lv, lv[:], self.att_lambda, self.att_lambda.t[l].partition_broadcast(128))
            b.dma("sp", gsub, gsub[:], self.att_subln_g, self.att_subln_g.t[l].rearrange("(p o) -> p o", o=1))
            b.op("dve", lambda e: e.tensor_scalar(gsub[:], gsub[:], 1.0 - lam_init, None, op0=ALU.mult), r=[gsub], w=[gsub])
            for k in range(2):
                b.op("dve", lambda e, k=k: e.tensor_tensor(out=lp[:, k, :], in0=lv[:, 2 * k, :], in1=lv[:, 2 * k + 1, :],
                                                          op=ALU.mult), r=[lv], w=[lp])
                b.op("dve", lambda e, k=k: e.reduce_sum(out=ls[:, k:k + 1], in_=lp[:, k, :], axis=mybir.AxisListType.X),
                     r=[lp], w=[ls])
            b.op("act", lambda e: e.activation(out=ls[:], in_=ls[:], func=AF.Exp), r=[ls], w=[ls])
            b.op("dve", lambda e: e.tensor_tensor(out=neglam[:], in0=ls[:, 1:2], in1=ls[:, 0:1], op=ALU.subtract),
                 r=[ls], w=[neglam])
            b.op("dve", lambda e: e.tensor_scalar_add(neglam[:], neglam[:], -lam_init), r=[neglam], w=[neglam])
            qt = [b.sb(st, "qt%d" % k, [128, 4, 512], BF16) for k in range(2)]
            pb = [b.sb(st, "pb%d" % k, [128, 2, 512], BF16) for k in range(3)]
            ones128 = b.sb(st, "ones128", [128, 128], BF16)
            b.op("pool", lambda e: e.memset(ones128[:], 1.0), w=[ones128])
            os_ = [b.sb(st, "os%d" % k, [128, 512], F32) for k in range(2)]
            rl = [b.sb(st, "rl%d" % k, [128, 512], F32) for k in range(2)]
            tt = b.sb(st, "tt", [128, 512], F32)
            oo = b.sb(st, "oo", [128, 512], F32)
            sq = b.sb(st, "sq", [128, 512], F32)
            rs = b.sb(st, "rs", [128, 512], F32)
            mgo = [b.sb(st, "mgo%d" % k, [128, 512], BF16) for k in range(2)]
            psc = [b.ps(st, "psc%d" % k, [128, 2, 512]) for k in range(2)]
            po = [b.ps(st, "po%d" % k, [128, 512]) for k in range(2)]
            pl = [b.ps(st, "pl%d" % k, [128, 512]) for k in range(2)]
            pss = psc[0]
            ONES = self.cm[:, self.ONES, :]
            qblocks = [(i * 512, 512, list(range(NKT))) for i in range(S // 512)]
            if need_ctx:
                qblocks.append((S, SC, list(range(S // 128, NKT))))
            def load_q(qi_):
                q0_, nq_, _ = qblocks[qi_]
                b.dma("sp", qt[qi_ % 2], qt[qi_ % 2][:, :, 0:nq_], self.QT, self.QT.t[:, :, q0_:q0_ + nq_].rearrange("h p t -> p h t"))

            load_q(0)
            for qi, (q0, nq, kts) in enumerate(qblocks):
                q_ = qt[qi % 2]
                if qi + 1 < len(qblocks):
                    load_q(qi + 1)
                npair = len(kts) // 2
                items = [(h, m, pi) for h in range(4) for m in range(2) for pi in range(npair)]

                def qk(j):
                    h, m, pi = items[j]
                    sc_, p_ = psc[j % 2], pb[j % 3]
                    for a in range(2):
                        kt = kts[2 * pi + a]
                        b.op("pe", lambda e, a=a, kt=kt: e.matmul(sc_[:, a, 0:nq], lhsT=kt_sb[:, h, m, kt * 128:(kt + 1) * 128],
                                                                 rhs=q_[:, h, 0:nq], start=True, stop=True), r=[kt_sb, q_], w=[sc_])
                    b.op("act", lambda e: e.activation(out=p_[:, :, 0:nq], in_=sc_[:, :, 0:nq], func=AF.Exp, scale=0.125),
                         r=[sc_], w=[p_])

                def av(j):
                    h, m, pi = items[j]
                    p_ = pb[j % 3]
                    for a in range(2):
                        kt = kts[2 * pi + a]
                        first_, last_ = (pi == 0 and a == 0), (pi == npair - 1 and a == 1)
                        b.op("pe", lambda e, a=a, kt=kt: e.matmul(po[m][:, 0:nq], lhsT=v_sb[:, kt, h * 128:(h + 1) * 128],
                                                                 rhs=p_[:, a, 0:nq], start=first_, stop=last_),
                             r=[v_sb, p_], w=[po[m]])
                        b.op("pe", lambda e, a=a: e.matmul(pl[m][:, 0:nq], lhsT=ones128[:], rhs=p_[:, a, 0:nq],
                                                           start=first_, stop=last_), r=[ones128, p_], w=[pl[m]])

                def post(h):
                    for m in range(2):
                        b.op("act", lambda e, m=m: e.copy(out=os_[m][:, 0:nq], in_=po[m][:, 0:nq]), r=[po[m]], w=[os_[m]])
                        b.op("dve", lambda e, m=m: e.reciprocal(out=rl[m][:, 0:nq], in_=pl[m][:, 0:nq]), r=[pl[m]], w=[rl[m]])
                    b.op("dve", lambda e: e.tensor_tensor(out=tt[:, 0:nq], in0=os_[0][:, 0:nq], in1=rl[0][:, 0:nq], op=ALU.mult),
                         r=[os_[0], rl[0]], w=[tt])
                    b.op("dve", lambda e: e.scalar_tensor_tensor(out=oo[:, 0:nq], in0=os_[1][:, 0:nq], scalar=neglam[:, 0:1],
                                                                 in1=rl[1][:, 0:nq], op0=ALU.mult, op1=ALU.mult),
                         r=[os_[1], neglam, rl[1]], w=[oo])
                    b.op("pool", lambda e: e.tensor_tensor(out=oo[:, 0:nq], in0=oo[:, 0:nq], in1=tt[:, 0:nq], op=ALU.add),
                         r=[oo, tt], w=[oo])
                    b.op("act", lambda e: e.activation(out=sq[:, 0:nq], in_=oo[:, 0:nq], func=AF.Square), r=[oo], w=[sq])

                def post_b(h):
                    b.op("pe", lambda e: e.matmul(pss[:, 0, 0:nq], lhsT=ONES, rhs=sq[:, 0:nq], start=True, stop=True),
                         r=[self.cm, sq], w=[pss])
                    b.op("act", lambda e: e.activation(out=rs[:, 0:nq], in_=pss[:, 0, 0:nq], func=AF.Sqrt, scale=1.0 / 128,
                                                       bias=self.eps6[:]), r=[pss, self.eps6], w=[rs])
                    b.op("dve", lambda e: e.reciprocal(out=rs[:, 0:nq], in_=rs[:, 0:nq]), r=[rs], w=[rs])
                    g_ = mgo[h % 2]
                    b.op("dve", lambda e: e.scalar_tensor_tensor(out=g_[:, 0:nq], in0=oo[:, 0:nq], scalar=gsub[:, 0:1],
                                                                 in1=rs[:, 0:nq], op0=ALU.mult, op1=ALU.mult),
                         r=[oo, gsub, rs], w=[g_])
                    b.dma("sp", self.MGT, self.MGT.t[h * 128:(h + 1) * 128, q0:q0 + nq], g_, g_[:, 0:nq])

                n_it = len(items)
                qk(0)
                pend = None
                for j in range(n_it):
                    if j + 1 < n_it:
                        qk(j + 1)
                    av(j)
                    h, m, pi = items[j]
                    if pend is not None and j >= pend[1]:
                        post_b(pend[0])
                        pend = None
                    if m == 1 and pi == npair - 1:
                        post(h)
                        pend = (h, j + min(6, npair))
                if pend is not None:
                    post_b(pend[0])
            b.barrier()
            b.release([kt_sb, v_sb, lv, gsub] + qt + mgo)

    def mixer_scan(self, l, kind):
        b, S, SC, T = self.b, self.S, self.SC, self.T
        need_ctx = l < self.depth - 1
        ssd = kind == "ssd"
        n = 128 if ssd else 64
        NK = 2 if ssd else 4
        kq = (lambda h: h // 2) if ssd else (lambda h: h)
        QTd, KTd = (self.CT, self.BT) if ssd else (self.RQT, self.RKT)
        YP = self.YS if ssd else self.YR
        col_base = 512 if ssd else 768
        nlat = S // 128
        lat = [i * 128 for i in range(nlat)]
        ctxc = [S + i * 128 for i in range(SC // 128)]
        cm = self.cm
        with ExitStack() as st:
            prm = b.sb(st, "prm", [128, 3, 8], F32)
            if ssd:
                b.dma("sp", prm, prm[:, 0, :], self.ssm_a_log, self.ssm_a_log.t[l].partition_broadcast(128))
                b.dma("sp", prm, prm[:, 1, :], self.ssm_dt_bias, self.ssm_dt_bias.t[l].partition_broadcast(128))
                b.dma("sp", prm, prm[:, 2, :], self.ssm_d, self.ssm_d.t[l].partition_broadcast(128))
                negA = b.sb(st, "negA", [128, 8], F32)
                b.op("act", lambda e: e.activation(out=negA[:], in_=prm[:, 0, :], func=AF.Exp), r=[prm], w=[negA])
                b.op("dve", lambda e: e.tensor_scalar(negA[:], negA[:], -1.0, None, op0=ALU.mult), r=[negA], w=[negA])
                dsum = b.sb(st, "dsum", [128, 4], F32)
                b.op("dve", lambda e: e.tensor_tensor(out=dsum[:], in0=prm[:, 2, 0:4], in1=prm[:, 2, 4:8], op=ALU.add),
                     r=[prm], w=[dsum])
                dsum_bc = b.sb(st, "dsum_bc", [128, 4, 64], F32)
                b.op("pool", lambda e: e.memset(dsum_bc[:], 1.0), w=[dsum_bc])
                for h in range(4):
                    b.op("dve", lambda e, h=h: e.tensor_scalar(dsum_bc[:, h, :], dsum_bc[:, h, :], dsum[:, h:h + 1], None,
                                                              op0=ALU.mult), r=[dsum_bc, dsum], w=[dsum_bc])
                gn = b.sb(st, "gn", [128, 256], F32)
                b.dma("sp", gn, gn[:], self.ssm_norm_g, self.ssm_norm_g.t[l].partition_broadcast(128))
            else:
                b.dma("sp", prm, prm[:, 0, :], self.ret_log_gamma, self.ret_log_gamma.t[l].partition_broadcast(128))
            class WS:
                pass
            W = []
            for k in range(2):
                w = WS()
                sfx = "_%d" % k
                w.la = b.sb(st, "la" + sfx, [128, 4], F32)
                w.dtv = b.sb(st, "dtv" + sfx, [128, 4], F32)
                w.TL = b.sb(st, "TL" + sfx, [128, 4, 128], F32)
                w.dm = b.sb(st, "dm" + sfx, [128, 4, 128], F32)
                w.em = b.sb(st, "em" + sfx, [128, 4, 128], F32)
                w.ngam = b.sb(st, "ngam" + sfx, [128, 4], F32)
                w.totc = b.sb(st, "totc" + sfx, [128, 4], F32)
                w.dec = b.sb(st, "dec" + sfx, [128, 4, 128], F32)
                w.Ebc = b.sb(st, "Ebc" + sfx, [128, 4, 128], F32)
                w.wv = b.sb(st, "wv" + sfx, [128, 4], F32)
                w.etot = b.sb(st, "etot" + sfx, [128, 4], F32)
                w.coef = b.sb(st, "coef" + sfx, [128, 4], F32)
                w.xs = b.sb(st, "xs" + sfx, [128, 4, 64], F32)
                w.vd = b.sb(st, "vd" + sfx, [128, 4, 64], BF16)
                w.vw = b.sb(st, "vw" + sfx, [128, 4, 64], BF16)
                w.MT = b.sb(st, "MT" + sfx, [128, 4, 128], BF16)
                w.QpT = b.sb(st, "QpT" + sfx, [128, 4, 128], BF16)
                w.y2 = b.sb(st, "y2" + sfx, [128, 256], F32)
                w.y3 = b.sb(st, "y3" + sfx, [128, 256], F32)
                w.ss = b.sb(st, "ss" + sfx, [128, 4], F32)
                w.mvh = b.sb(st, "mvh" + sfx, [128, 4, 2], F32)
                w.sth = b.sb(st, "sth" + sfx, [128, 4, 6], F32)
                W.append(w)
            dtr = [b.sb(st, "dtr%d" % k, [128, 4], F32) for k in range(3)]
            qT = [b.sb(st, "qT%d" % k, [128, NK, 128], BF16) for k in range(3)]
            kT = [b.sb(st, "kT%d" % k, [128, NK, 128], BF16) for k in range(3)]
            ktm = [b.sb(st, "ktm%d" % k, [128, 256], BF16) for k in range(3)]
            xsT = [b.sb(st, "xsT%d" % k, [128, 2, 128], F32) for k in range(3)]
            vtm = [b.sb(st, "vtm%d" % k, [128, 4, 64], BF16) for k in range(3)]
            Sf = b.sb(st, "Sf", [128, 4, 64], F32)
            Sb = b.sb(st, "Sb", [128, 4, 64], BF16)
            ysb = [b.sb(st, "ysb%d" % k, [128, 256], F32) for k in range(3)]
            zt = [b.sb(st, "zt%d" % k, [128, 256], F32) for k in range(3)]
            mgo = [b.sb(st, "mgo%d" % k, [128, 2, 128], BF16) for k in range(2)]
            pc = b.ps(st, "pc", [128, 8])
            pG = b.ps(st, "pG", [128, 512])
            pGT = b.ps(st, "pGT", [128, 512])
            py = b.ps(st, "py", [128, 256])
            pS = b.ps(st, "pS", [128, 256])
            pX = b.ps(st, "pX", [128, 256])
            pK = b.ps(st, "pK", [128, 256], BF16)
            pM = b.ps(st, "pM", [128, 256])
            bc = lambda ap, shape: ap.broadcast_to(shape)
            fl = lambda t_: t_[:].rearrange("p h l -> p (h l)")

            def decay_quants(d, w):
                tri = self.TRI_F if d == 0 else self.NSTRICT_B
                la = w.la
                b.op("pe", lambda e: e.matmul(pc[:, 0:4], lhsT=cm[:, tri, :], rhs=la[:], start=True, stop=True),
                     r=[cm, la], w=[pc])
                b.op("pe", lambda e: e.matmul(pc[:, 4:8], lhsT=cm[:, self.ONES, :], rhs=la[:], start=True, stop=True),
                     r=[cm, la], w=[pc])
                b.op("dve", lambda e: e.tensor_tensor(out=w.TL[:], in0=bc(cm[:, tri, :].unsqueeze(1), [128, 4, 128]),
                                                      in1=bc(la[:].unsqueeze(2), [128, 4, 128]), op=ALU.mult),
                     r=[cm, la], w=[w.TL])
                b.op("pe", lambda e: e.matmul(pG[:], lhsT=cm[:, self.ONES, :], rhs=fl(w.TL), start=True, stop=True),
                     r=[cm, w.TL], w=[pG])
                b.op("dve", lambda e: e.tensor_scalar(w.ngam[:], pc[:, 0:4], -1.0, None, op0=ALU.mult), r=[pc], w=[w.ngam])
                b.op("dve", lambda e: e.tensor_copy(out=w.totc[:], in_=pc[:, 4:8]), r=[pc], w=[w.totc])
                b.op("dve", lambda e: e.tensor_tensor(out=fl(w.dm), in0=pG[:], in1=fl(self.mask4[d]), op=ALU.add),
                     r=[pG, self.mask4[d]], w=[w.dm])
                b.op("dve", lambda e: e.tensor_tensor(out=w.dm[:], in0=w.dm[:], in1=bc(w.ngam[:].unsqueeze(2), [128, 4, 128]),
                                                      op=ALU.add), r=[w.dm, w.ngam], w=[w.dm])
                b.op("act", lambda e: e.activation(out=fl(w.dec), in_=fl(w.dm), func=AF.Exp), r=[w.dm], w=[w.dec])
                if d == 0:
                    b.op("act", lambda e: e.activation(out=fl(w.Ebc), in_=pG[:], func=AF.Exp), r=[pG], w=[w.Ebc])
                    b.op("dve", lambda e: e.tensor_tensor(out=w.wv[:], in0=w.totc[:], in1=w.ngam[:], op=ALU.add),
                         r=[w.totc, w.ngam], w=[w.wv])
                    b.op("act", lambda e: e.activation(out=w.wv[:], in_=w.wv[:], func=AF.Exp), r=[w.wv], w=[w.wv])
                else:
                    b.op("dve", lambda e: e.tensor_tensor(out=w.em[:], in0=pG[:].rearrange("p (h l) -> p h l", l=128),
                                                          in1=bc(w.totc[:].unsqueeze(2), [128, 4, 128]), op=ALU.add),
                         r=[pG, w.totc], w=[w.em])
                    b.op("act", lambda e: e.activation(out=fl(w.Ebc), in_=fl(w.em), func=AF.Exp), r=[w.em], w=[w.Ebc])
                    b.op("act", lambda e: e.activation(out=w.wv[:], in_=w.ngam[:], func=AF.Exp), r=[w.ngam], w=[w.wv])
                b.op("act", lambda e: e.activation(out=w.etot[:], in_=w.totc[:], func=AF.Exp), r=[w.totc], w=[w.etot])

            for d in range(2):
                if d == 1:
                    b.barrier()
                order = (ctxc + lat) if d == 0 else (ctxc[::-1] + lat[::-1])
                b.op("pool", lambda e: e.memset(Sf[:], 0.0), w=[Sf])
                b.op("pool", lambda e: e.memset(Sb[:], 0.0), w=[Sb])
                if not ssd:
                    wc = W[0]
                    b.op("dve", lambda e: e.tensor_copy(out=wc.la[:], in_=prm[:, 0, d * 4:(d + 1) * 4]), r=[prm], w=[wc.la])
                    decay_quants(d, wc)
                def stage_l(ci, tc):
                    is_ctx = tc >= S
                    want_y = (not is_ctx) or need_ctx
                    q_, k_, km_, v_ = qT[ci % 3], kT[ci % 3], ktm[ci % 3], vtm[ci % 3]
                    x_, dr, yl, z_ = xsT[ci % 3], dtr[ci % 3], ysb[ci % 3], zt[ci % 3]
                    if ssd:
                        b.dma("sp", q_, q_[:], QTd, QTd.t[:, :, tc:tc + 128].rearrange("g p t -> p g t"))
                        b.dma("sp", k_, k_[:], KTd, KTd.t[:, :, tc:tc + 128].rearrange("g p t -> p g t"))
                        b.dma("sp", x_, x_[:], self.XST, self.XST.t[:, :, tc:tc + 128].rearrange("g p t -> p g t"))
                        b.dma("sp", dr, dr[:], self.DT, self.DT.t[tc:tc + 128, :])
                    else:
                        b.dma("sp", q_, q_[0:64, :, :], QTd, QTd.t[:, :, tc:tc + 128].rearrange("g p t -> p g t"))
                        b.dma("sp", k_, k_[0:64, :, :], KTd, KTd.t[:, :, tc:tc + 128].rearrange("g p t -> p g t"))
                        b.dma("sp", km_, km_[:], self.RKK, self.RKK.t[tc:tc + 128, :])
                        b.dma("sp", v_, v_[:].rearrange("p h e -> p (h e)"), self.RV, self.RV.t[tc:tc + 128, :])
                    if d == 1 and want_y:
                        b.dma("sp", yl, yl[:], YP, YP.t[tc:tc + 128, :])
                        zsrc = self.Z if ssd else self.RG
                        b.dma("sp", z_, z_[:], zsrc, zsrc.t[tc:tc + 128, :])

                def stage_a(ci, tc):
                    is_ctx = tc >= S
                    want_y = (not is_ctx) or need_ctx
                    w = W[ci % 2]
                    wq = w if ssd else W[0]
                    q_, k_, km_, v_ = qT[ci % 3], kT[ci % 3], ktm[ci % 3], vtm[ci % 3]
                    x_, dr, yl, z_ = xsT[ci % 3], dtr[ci % 3], ysb[ci % 3], zt[ci % 3]
                    if ssd:
                        b.op("dve", lambda e: e.tensor_tensor(out=w.dtv[:], in0=dr[:], in1=prm[:, 1, d * 4:(d + 1) * 4], op=ALU.add),
                             r=[dr, prm], w=[w.dtv])
                        b.op("act", lambda e: e.activation(out=w.dtv[:], in_=w.dtv[:], func=AF.Exp), r=[w.dtv], w=[w.dtv])
                        b.op("act", lambda e: e.activation(out=w.dtv[:], in_=w.dtv[:], func=AF.Ln, bias=self.one_t[:]),
                             r=[w.dtv, self.one_t], w=[w.dtv])
                        b.op("dve", lambda e: e.tensor_tensor(out=w.la[:], in0=w.dtv[:], in1=negA[:, d * 4:(d + 1) * 4], op=ALU.mult),
                             r=[w.dtv, negA], w=[w.la])
                        decay_quants(d, w)
                        for g in range(2):
                            b.op("pe", lambda e, g=g: e.transpose(pX[:, g * 128:(g + 1) * 128], x_[:, g, :], self.ident[:]),
                                 r=[x_, self.ident], w=[pX])
                        b.op("act", lambda e: e.copy(out=w.xs[:].rearrange("p h e -> p (h e)"), in_=pX[:]), r=[pX], w=[w.xs])
                        for g in range(2):
                            b.op("pe", lambda e, g=g: e.transpose(pK[:, g * 128:(g + 1) * 128], k_[:, g, :], self.identb[:]),
                                 r=[k_, self.identb], w=[pK])
                        b.op("act", lambda e: e.copy(out=km_[:], in_=pK[:]), r=[pK], w=[km_])
                        b.op("dve", lambda e: e.tensor_tensor(out=w.coef[:], in0=w.dtv[:], in1=w.wv[:], op=ALU.mult),
                             r=[w.dtv, w.wv], w=[w.coef])
                        b.op("dve", lambda e: e.tensor_tensor(out=w.vd[:], in0=w.xs[:], in1=bc(w.dtv[:].unsqueeze(2), [128, 4, 64]),
                                                              op=ALU.mult), r=[w.xs, w.dtv], w=[w.vd])
                        b.op("dve", lambda e: e.tensor_tensor(out=w.vw[:], in0=w.xs[:], in1=bc(w.coef[:].unsqueeze(2), [128, 4, 64]),
                                                              op=ALU.mult), r=[w.xs, w.coef], w=[w.vw])
                        vdd = w.vd
                    else:
                        b.op("dve", lambda e: e.tensor_tensor(out=w.vw[:], in0=v_[:], in1=bc(wq.wv[:].unsqueeze(2), [128, 4, 64]),
                                                              op=ALU.mult), r=[v_, wq.wv], w=[w.vw])
                        vdd = v_
                    if want_y:
                        for g in range(NK):
                            b.op("pe", lambda e, g=g: e.matmul(pGT[:, g * 128:(g + 1) * 128], lhsT=k_[0:n, g, :],
                                                               rhs=q_[0:n, g, :], start=True, stop=True), r=[k_, q_], w=[pGT])
                        if ssd:
                            for g in range(2):
                                b.op("dve", lambda e, g=g: e.tensor_tensor(
                                    out=w.MT[:, 2 * g:2 * g + 2, :],
                                    in0=bc(pGT[:, g * 128:(g + 1) * 128].unsqueeze(1), [128, 2, 128]),
                                    in1=wq.dec[:, 2 * g:2 * g + 2, :], op=ALU.mult), r=[pGT, wq.dec], w=[w.MT])
                                b.op("pool", lambda e, g=g: e.tensor_tensor(
                                    out=w.QpT[:, 2 * g:2 * g + 2, :], in0=bc(q_[:, g:g + 1, :], [128, 2, 128]),
                                    in1=wq.Ebc[:, 2 * g:2 * g + 2, :], op=ALU.mult), r=[q_, wq.Ebc], w=[w.QpT])
                        else:
                            b.op("dve", lambda e: e.tensor_tensor(out=fl(w.MT), in0=pGT[:], in1=fl(wq.dec), op=ALU.mult),
                                 r=[pGT, wq.dec], w=[w.MT])
                            b.op("dve", lambda e: e.tensor_tensor(out=w.QpT[0:64, :, :], in0=q_[0:64, :, :], in1=wq.Ebc[0:64, :, :],
                                                                   op=ALU.mult), r=[q_, wq.Ebc], w=[w.QpT])
                    return dict(w=w, wq=wq, q_=q_, k_=k_, km_=km_, v_=v_, vdd=vdd, want_y=want_y,
                                yl=(yl if (d == 1 and want_y) else None), z_=(z_ if (d == 1 and want_y) else None))

                def stage_b(ci, tc, cx):
                    w, wq, q_, k_, km_, v_, vdd, want_y, yl, z_ = (cx[k] for k in ('w', 'wq', 'q_', 'k_', 'km_', 'v_', 'vdd', 'want_y', 'yl', 'z_'))
                    if want_y:
                        for h in range(4):
                            b.op("pe", lambda e, h=h: e.matmul(py[:, h * 64:(h + 1) * 64], lhsT=w.MT[:, h, :], rhs=vdd[:, h, :],
                                                               start=True, stop=False), r=[w.MT, vdd], w=[py])
                            b.op("pe", lambda e, h=h: e.matmul(py[:, h * 64:(h + 1) * 64], lhsT=w.QpT[0:n, h, :], rhs=Sb[0:n, h, :],
                                                               start=False, stop=True), r=[w.QpT, Sb], w=[py])
                    last = ci == len(order) - 1
                    if not last:
                        for h in range(4):
                            g = kq(h)
                            b.op("pe", lambda e, h=h, g=g: e.matmul(pS[0:n, h * 64:(h + 1) * 64], lhsT=km_[:, g * n:(g + 1) * n],
                                                                   rhs=w.vw[:, h, :], start=True, stop=True), r=[km_, w.vw], w=[pS])
                        b.op("dve", lambda e: e.tensor_tensor(out=Sf[0:n, :, :], in0=Sf[0:n, :, :],
                                                              in1=bc(wq.etot[0:n, :].unsqueeze(2), [n, 4, 64]), op=ALU.mult),
                             r=[Sf, wq.etot], w=[Sf])
                        b.op("dve", lambda e: e.tensor_tensor(out=Sf[0:n, :, :].rearrange("p h e -> p (h e)"),
                                                              in0=Sf[0:n, :, :].rearrange("p h e -> p (h e)"), in1=pS[0:n, :], op=ALU.add),
                             r=[Sf, pS], w=[Sf])
                        b.op("act", lambda e: e.copy(out=Sb[0:n, :, :], in_=Sf[0:n, :, :]), r=[Sf], w=[Sb])
                    if not want_y:
                        return False
                    if d == 0:
                        yo = ysb[ci % 3]
                        b.op("act", lambda e: e.copy(out=yo[:], in_=py[:]), r=[py], w=[yo])
                        b.dma("sp", YP, YP.t[tc:tc + 128, :], yo, yo[:])
                        return False
                    y2, y3, ss, mvh, sth = w.y2, w.y3, w.ss, w.mvh, w.sth
                    b.op("dve", lambda e: e.tensor_tensor(out=y2[:], in0=py[:], in1=yl[:], op=ALU.add), r=[py, yl], w=[y2])
                    if ssd:
                        b.op("pool", lambda e: e.tensor_tensor(out=y3[:], in0=w.xs[:].rearrange("p h e -> p (h e)"),
                                                               in1=dsum_bc[:].rearrange("p h e -> p (h e)"), op=ALU.mult),
                             r=[w.xs, dsum_bc], w=[y3])
                        b.op("pool", lambda e: e.tensor_tensor(out=y2[:], in0=y2[:], in1=y3[:], op=ALU.add), r=[y2, y3], w=[y2])
                        b.op("dve", lambda e: e.tensor_tensor(out=y2[:], in0=y2[:], in1=z_[:], op=ALU.mult), r=[y2, z_], w=[y2])
                        b.op("act", lambda e: e.activation(out=y3[:], in_=y2[:], func=AF.Square, accum_out=ss[:, 0:1]),
                             r=[y2], w=[y3, ss])
                        b.op("act", lambda e: e.activation(out=ss[:, 1:2], in_=ss[:, 0:1], func=AF.Ln, scale=1.0 / 256,
                                                           bias=self.eps6[:]), r=[ss, self.eps6], w=[ss])
                        b.op("act", lambda e: e.activation(out=ss[:, 2:3], in_=ss[:, 1:2], func=AF.Exp, scale=-0.5), r=[ss], w=[ss])
                        b.op("dve", lambda e: e.scalar_tensor_tensor(out=y3[:], in0=y2[:], scalar=ss[:, 2:3], in1=gn[:],
                                                                     op0=ALU.mult, op1=ALU.mult), r=[y2, ss, gn], w=[y3])
                    else:
                        for h in range(4):
                            b.op("dve", lambda e, h=h: e.bn_stats(out=sth[:, h, :], in_=y2[:, h * 64:(h + 1) * 64]), r=[y2], w=[sth])
                            b.op("dve", lambda e, h=h: e.bn_aggr(out=mvh[:, h, :], in_=sth[:, h, :]), r=[sth], w=[mvh])
                        b.op("act", lambda e: e.activation(out=ss[:], in_=mvh[:, :, 1], func=AF.Ln, bias=self.eps_t[:]),
                             r=[mvh, self.eps_t], w=[ss])
                        b.op("act", lambda e: e.activation(out=ss[:], in_=ss[:], func=AF.Exp, scale=-0.5), r=[ss], w=[ss])
                        y2v = y2[:].rearrange("p (h e) -> p h e", e=64)
                        y3v = y3[:].rearrange("p (h e) -> p h e", e=64)
                        b.op("dve", lambda e: e.tensor_tensor(out=y3v, in0=y2v, in1=bc(mvh[:, :, 0:1], [128, 4, 64]), op=ALU.subtract),
                             r=[y2, mvh], w=[y3])
                        b.op("dve", lambda e: e.tensor_tensor(out=y3v, in0=y3v, in1=bc(ss[:].unsqueeze(2), [128, 4, 64]), op=ALU.mult),
                             r=[y3, ss], w=[y3])
                        b.op("dve", lambda e: e.tensor_tensor(out=y3[:], in0=y3[:], in1=z_[:], op=ALU.mult), r=[y3, z_], w=[y3])
                    return True

                def stage_c(ci, tc, cx):
                    y3 = cx['w'].y3
                    g_ = mgo[ci % 2]
                    for j in range(2):
                        b.op("pe", lambda e, j=j: e.transpose(pM[:, j * 128:(j + 1) * 128], y3[:, j * 128:(j + 1) * 128],
                                                             self.ident[:]), r=[y3, self.ident], w=[pM])
                    b.op("act", lambda e: e.copy(out=g_[:].rearrange("p j t -> p (j t)"), in_=pM[:]), r=[pM], w=[g_])
                    b.dma("sp", self.MGT, self.MGT.t[col_base:col_base + 256, tc:tc + 128].rearrange("(j p) t -> p j t", p=128),
                          g_, g_[:])
                stage_l(0, order[0])
                if len(order) > 1:
                    stage_l(1, order[1])
                cxs = {0: stage_a(0, order[0])}
                prev_c = None
                for ci, tc in enumerate(order):
                    if ci + 2 < len(order):
                        stage_l(ci + 2, order[ci + 2])
                    if ci + 1 < len(order):
                        cxs[ci + 1] = stage_a(ci + 1, order[ci + 1])
                    cx_ = cxs.pop(ci)
                    pend_ = stage_b(ci, tc, cx_)
                    if prev_c is not None:
                        stage_c(*prev_c)
                        prev_c = None
                    if pend_:
                        prev_c = (ci, tc, cx_)
                if prev_c is not None:
                    stage_c(*prev_c)
            b.barrier()
            rel = [prm] + dtr + qT + kT + ktm + xsT + vtm + ysb + zt + mgo
            if ssd:
                rel.append(gn)
            b.release(rel)

    def mixer_outproj(self, l):
        b, S, T = self.b, self.S, self.T
        need_ctx = l < self.depth - 1
        with ExitStack() as st:
            gate, g_bc, b_bc = self.load_bcast(st, l, 1, 1.0)
            wo = b.sb(st, "wo", [128, KC, D], BF16)
            wstg = [b.sb(st, "wostg%d" % k, [128, D], F32) for k in range(2)]
            engs = ["act", "pool", "dve"]
            for kc in range(KC):
                s_ = wstg[kc % 2]
                b.dma("sp", s_, s_[:], self.w_out, self.w_out.t[l, kc * 128:(kc + 1) * 128, :])
                self.cast_to(engs[kc % 3], wo[:, kc, :], s_[:], [s_], [wo])
            mg = [b.sb(st, "mg%d" % k, [128, KC, 512], BF16) for k in range(2)]
            xin = [b.sb(st, "xin%d" % k, [128, 4, D], F32) for k in range(2)]
            ybuf = [b.sb(st, "ybuf%d" % k, [128, D], F32) for k in range(2)]
            stt = b.sb(st, "stt", [128, 12], F32)
            mv = b.sb(st, "mv", [128, 2], F32)
            rstd = b.sb(st, "rstd", [128, 1], F32)
            nmr = b.sb(st, "nmr", [128, 1], F32)
            pd = [b.ps(st, "pd%d" % k, [128, D]) for k in range(2)]
            it = 0
            blks = [(bi, t0, ntok, v) for bi, (t0, ntok, v) in enumerate(self.blocks) if not (v == 1 and not need_ctx)]

            def load_blk(k):
                bi_, t0_, ntok_, v_ = blks[k]
                b.dma("sp", xin[bi_ % 2], xin[bi_ % 2][:, 0:ntok_ // 128, :], self.Xb[bi_],
                      self.X.t[t0_:t0_ + ntok_, :].rearrange("(s p) d -> p s d", p=128))
                b.dma("sp", mg[bi_ % 2], mg[bi_ % 2][:, :, 0:ntok_], self.MGT,
                      self.MGT.t[:, t0_:t0_ + ntok_].rearrange("(k p) t -> p k t", p=128))

            load_blk(0)
            for k_, (bi, t0, ntok, v) in enumerate(blks):
                if k_ + 1 < len(blks):
                    load_blk(k_ + 1)
                nsub = ntok // 128
                xi, m_ = xin[bi % 2], mg[bi % 2]
                for s in range(nsub):
                    p_ = pd[it % 2]
                    y = ybuf[it % 2]
                    it += 1
                    for h in range(2):
                        for kc in range(KC):
                            b.op("pe", lambda e, kc=kc, h=h: e.matmul(p_[:, h * 512:(h + 1) * 512], lhsT=m_[:, kc, s * 128:(s + 1) * 128],
                                                                     rhs=wo[:, kc, h * 512:(h + 1) * 512], start=(kc == 0),
                                                                     stop=(kc == KC - 1)), r=[m_, wo], w=[p_])
                    b.op("dve", lambda e: e.tensor_tensor(out=y[:], in0=p_[:], in1=gate[v][:], op=ALU.mult), r=[p_, gate[v]], w=[y])
                    b.op("dve", lambda e, s=s: e.scalar_tensor_tensor(out=y[:], in0=xi[:, s, :], scalar=ALPHA, in1=y[:],
                                                                     op0=ALU.mult, op1=ALU.add), r=[xi, y], w=[y])
                    self._xo_tl = xi
                    self.layer_norm_store(y, xi[:, s, :], g_bc, b_bc, stt, mv, rstd, nmr)
                b.dma("sp", self.Xb[bi], self.X.t[t0:t0 + ntok, :].rearrange("(s p) d -> p s d", p=128), xi, xi[:, 0:nsub, :])
            b.barrier()
            b.release([gate[0], gate[1], g_bc, b_bc, wo] + wstg + mg + xin)

    def build(self):
        b = self.b
        with b.es:
            self.declare_io()
            self.declare_mixer_io()
            with ExitStack() as st:
                self.eps_t = b.sb(st, "eps_t", [128, 1], F32)
                b.op("pool", lambda e: e.memset(self.eps_t[:], LN_EPS), w=[self.eps_t])
                self.prologue(st)
                self.load_consts(st)
                b.barrier()
                first = True
                stop = self.stop_after
                for l in range(self.depth):
                    need_ctx = l < self.depth - 1
                    self.ffn(l, 0, first=first)
                    first = False
                    if stop == ("ffn0", l):
                        break
                    self.mixer_inproj(l)
                    self.mixer_conv(l)
                    if stop == ("inproj", l):
                        break
                    self.mixer_attention(l)
                    if stop == ("att", l):
                        break
                    self.mixer_scan(l, "ssd")
                    if stop == ("ssd", l):
                        break
                    self.mixer_scan(l, "ret")
                    if stop == ("ret", l):
                        break
                    self.mixer_outproj(l)
                    if stop == ("mix", l):
                        break
                    self.ffn(l, 1, skip_ctx=not need_ctx)
                for bi, (t0, ntok, v) in enumerate(self.blocks):
                    if t0 < self.out_rows:
                        b.dma("sp", self.out, self.out.t[t0:t0 + ntok, :], self.Xb[bi], self.X.t[t0:t0 + ntok, :],
                              sem_tl=self.out)
                for name in self.dbg.get("dump", []):
                    src = getattr(self, name)
                    dst = dram_tl(b, "dbg_" + name, list(src.t.shape), src.t.dtype, "ExternalOutput")
                    b.dma("sp", dst, dst.t, src, src.t, sem_tl=self.out)
                E = b.engs["sp"]
                for ds in b.dsems:
                    if ds.cum > 0:
                        E.h.wait_ge(ds.sem, ds.cum)
                b.barrier()
        return self.nc


def _rot_tables(S):
    f32 = np.float32
    nb = S // 512
    t = np.arange(S, dtype=f32)
    row_pos = np.floor(t / f32(64)).astype(f32)
    col_pos = (t - row_pos * f32(64)).astype(f32)
    axis_freq = (f32(1.0) / (f32(10000.0) ** (np.arange(0, 32, 2, dtype=f32) / f32(32)))).astype(f32)
    ret_freq = (f32(1.0) / (f32(10000.0) ** np.linspace(0.0, 1.0, 32, dtype=f32))).astype(f32)
    r = np.arange(128)
    d = r % 64
    fa = axis_freq[d % 16]
    pos_a = np.where((d < 32)[:, None], row_pos[None, :], col_pos[None, :]).astype(f32)
    ang_a = (pos_a * fa[:, None]).astype(f32)
    sgn_a = np.where((d % 32) < 16, -1.0, 1.0).astype(f32)
    fr = ret_freq[d % 32]
    ang_r = (t[None, :] * fr[:, None]).astype(f32)
    sgn_r = np.where(d < 32, -1.0, 1.0).astype(f32)
    cosA, sinA = np.cos(ang_a).astype(f32), (np.sin(ang_a).astype(f32) * sgn_a[:, None])
    cosR, sinR = np.cos(ang_r).astype(f32), (np.sin(ang_r).astype(f32) * sgn_r[:, None])
    tabs = np.stack([cosA, sinA, cosR, sinR, cosR * f32(0.125), sinR * f32(0.125)], axis=1)
    return np.ascontiguousarray(tabs.reshape(128, 6, nb, 512).transpose(2, 0, 1, 3)).astype(f32)


def _cmats():
    cm = np.zeros((128, 8, 128), np.float32)
    r = np.arange(128)
    permA = (r // 32) * 32 + ((r % 32) + 16) % 32
    permR = (r // 64) * 64 + ((r % 64) + 32) % 64
    cm[permA, 0, r] = 1.0
    cm[permR, 1, r] = 1.0
    s, l = np.meshgrid(r, r, indexing="ij")
    cm[:, 2, :] = (s <= l)
    cm[:, 3, :] = -1.0 * (s < l)
    cm[:, 4, :] = 1.0
    cm[:, 5, :] = np.where(s <= l, 0.0, -30000.0)
    cm[:, 6, :] = np.where(s >= l, 0.0, -30000.0)
    return cm


def make_in_maps(inputs, S, SC, depth, n_cores):
    f = lambda a: np.ascontiguousarray(np.asarray(a, dtype=np.float32))
    L = depth
    shared = {
        "ident": np.eye(128, dtype=np.float32),
        "rot_tab": _rot_tables(S),
        "cmats": _cmats(),
    }
    for k in ("ada_w", "ada_b", "norm_g", "norm_b", "ffn_w_gate", "ffn_w_up", "ffn_w_down", "w_in", "w_out", "conv_w",
              "conv_b", "att_lambda", "att_subln_g", "ssm_norm_g"):
        shared[k] = f(inputs[k][:L])
    for k in ("ssm_a_log", "ssm_dt_bias", "ssm_d", "ret_log_gamma"):
        shared[k] = f(np.asarray(inputs[k][:L]).reshape(L, 8))
    maps = []
    for c in range(n_cores):
        m = dict(shared)
        m["x"] = f(inputs["x"][c])
        m["ctx"] = f(inputs["ctx"][c])
        m["c2"] = f(np.stack([np.asarray(inputs["c"][c]), np.asarray(inputs["c_ctx"])], axis=1))
        maps.append(m)
    return maps


def kernel(**inputs):
    S, SC = 4096, 256
    n = 8
    prog = Prog(S, SC, DEPTH)
    nc = prog.build()
    maps = make_in_maps(inputs, S, SC, DEPTH, n)
    res = run_bass_kernel_spmd(nc, maps, core_ids=list(range(n)))
    return np.stack([r["out"] for r in res.results], axis=0).astype(np.float32)
```
